# Optimizing a Trainium2 kernel written in Bass

```python
import jax, jax.numpy as jnp
from jax import lax
import numpy as np

D_MODEL = 1024
BATCH = 32
SEQ = 256
DEPTH = 1
DEC_BATCH = 8
DEC_SEQ = 4096
PAST_LEN = 256

GRID_W = 64
N_DN_HEADS = 4
DN_HEAD_K = 128
DN_HEAD_V = 128
QK_WIDTH = N_DN_HEADS * DN_HEAD_K
DN_WIDTH = N_DN_HEADS * DN_HEAD_V
N_FOURIER_GROUPS = 4
FOURIER_GROUP = 128
FOURIER_WIDTH = N_FOURIER_GROUPS * FOURIER_GROUP
MIX_WIDTH = DN_WIDTH + FOURIER_WIDTH
N_DIR = 2
CONV_K = 3
CHUNK = 64
D_FF = -(-8 * D_MODEL // (3 * 256)) * 256
QKV_WIDTH = 2 * QK_WIDTH + DN_WIDTH
IN_SPLITS = [QKV_WIDTH, QKV_WIDTH + DN_WIDTH, QKV_WIDTH + DN_WIDTH + N_DIR * N_DN_HEADS,
             QKV_WIDTH + DN_WIDTH + 2 * N_DIR * N_DN_HEADS]
IN_COLS = QKV_WIDTH + DN_WIDTH + 2 * N_DIR * N_DN_HEADS + FOURIER_WIDTH
RMS_EPS = 1e-6

kernel_name = "hybrid_deltanet_fourier_diffusion_step"


def rmsnorm(x, g):
    xf = x.astype(jnp.float32)
    y = xf * lax.rsqrt(jnp.mean(xf * xf, axis=-1, keepdims=True) + RMS_EPS)
    return (y * g.astype(jnp.float32)).astype(x.dtype)


def l2norm(x):
    return x * lax.rsqrt(jnp.sum(x * x, axis=-1, keepdims=True) + 1e-6)


def short_conv(x, w):
    C = x.shape[-1]
    return lax.conv_general_dilated(x, w[:, None, :].astype(x.dtype), window_strides=(1,),
                                    padding=[(CONV_K // 2, CONV_K // 2)],
                                    dimension_numbers=('NWC', 'WIO', 'NWC'),
                                    feature_group_count=C)


def to_chunks(t):
    B, N = t.shape[:2]
    t = t.reshape(B, N // CHUNK, CHUNK, *t.shape[2:])
    return jnp.moveaxis(t, 2, 3)


def gated_delta_chunked(q, k, v, beta, log_a, s0):
    B, N, H, _ = q.shape
    DV = v.shape[-1]
    q, k, v, beta, log_a = map(to_chunks, (q, k, v, beta, log_a))
    g = jnp.cumsum(log_a, axis=-1)
    idx = jnp.arange(CHUNK)
    incl = idx[:, None] >= idx[None, :]
    strict = idx[:, None] > idx[None, :]
    decay = jnp.exp(jnp.where(incl, g[..., :, None] - g[..., None, :], -jnp.inf))
    kk = jnp.einsum('bnhik,bnhjk->bnhij', k, k)
    a_mat = jnp.where(strict, beta[..., :, None] * decay * kk, 0.0)
    eye = jnp.eye(CHUNK, dtype=q.dtype)
    rhs = jnp.concatenate([beta[..., None] * v, (beta * jnp.exp(g))[..., None] * k], axis=-1)
    sol = lax.linalg.triangular_solve(eye + a_mat, rhs, left_side=True, lower=True,
                                      unit_diagonal=True)
    u_v, w = sol[..., :DV], sol[..., DV:]
    aqk = jnp.einsum('bnhik,bnhjk->bnhij', q, k) * decay
    qg = q * jnp.exp(g)[..., None]
    g_last = g[..., -1:]
    kd = k * jnp.exp(g_last - g)[..., None]
    gl = jnp.exp(g_last[..., 0])
    xs = tuple(jnp.moveaxis(t, 1, 0) for t in (u_v, w, qg, aqk, kd, gl))

    def step(s, inp):
        u_v_c, w_c, qg_c, aqk_c, kd_c, gl_c = inp
        u = u_v_c - jnp.einsum('bhck,bhkv->bhcv', w_c, s)
        o = jnp.einsum('bhck,bhkv->bhcv', qg_c, s) + jnp.einsum('bhij,bhjv->bhiv', aqk_c, u)
        s = gl_c[..., None, None] * s + jnp.einsum('bhck,bhcv->bhkv', kd_c, u)
        return s, o

    s_end, o = lax.scan(step, s0, xs)
    o = jnp.moveaxis(jnp.moveaxis(o, 0, 1), 3, 2).reshape(B, N, H, DV)
    return o, s_end


def bidir_delta(q, k, v, beta, log_a, s0_f, s0_b):
    o_f, s_f = gated_delta_chunked(q, k, v, beta[:, :, 0], log_a[:, :, 0], s0_f)
    flip = lambda t: jnp.flip(t, axis=1)
    o_b, s_b = gated_delta_chunked(flip(q), flip(k), flip(v), flip(beta[:, :, 1]),
                                   flip(log_a[:, :, 1]), s0_b)
    return o_f + flip(o_b), s_f, s_b


def fourier_mix(f, w_fno, on_grid):
    B, N, _ = f.shape
    ff = f.astype(jnp.float32).reshape(B, N, N_FOURIER_GROUPS, FOURIER_GROUP)
    if on_grid:
        rows = N // GRID_W
        ff = ff.reshape(B, rows, GRID_W, N_FOURIER_GROUPS, FOURIER_GROUP)
        spec = jnp.fft.fftn(ff, axes=(1, 2, 4)).real
    else:
        spec = jnp.fft.fftn(ff, axes=(1, 3)).real
    spec = spec.reshape(B, N, N_FOURIER_GROUPS, FOURIER_GROUP) * (N * FOURIER_GROUP) ** -0.5
    out = jnp.einsum('bngc,gcd->bngd', spec, w_fno.astype(jnp.float32))
    return out.reshape(B, N, FOURIER_WIDTH).astype(f.dtype)


def trunk_layer(x, cond, s0_f, s0_b, on_grid, w_ada, b_ada, g_mix, w_in, w_conv, a_log,
                dt_bias, g_o, w_fno, w_out, g_ffn, w_gu, w_down):
    B, N, _ = x.shape
    mod = (jax.nn.silu(cond) @ w_ada + b_ada)[..., None, :]
    shift1, scale1, gate1, shift2, scale2, gate2 = jnp.split(mod, 6, axis=-1)
    h = rmsnorm(x, g_mix) * (1 + scale1) + shift1
    p = h @ w_in
    qkv, z, a_raw, b_raw, f = jnp.split(p, IN_SPLITS, axis=-1)
    qkv = jax.nn.silu(short_conv(qkv, w_conv)).astype(jnp.float32)
    q, k, v = jnp.split(qkv, [QK_WIDTH, 2 * QK_WIDTH], axis=-1)
    q = l2norm(q.reshape(B, N, N_DN_HEADS, DN_HEAD_K)) * DN_HEAD_K ** -0.5
    k = l2norm(k.reshape(B, N, N_DN_HEADS, DN_HEAD_K))
    v = v.reshape(B, N, N_DN_HEADS, DN_HEAD_V)
    beta = jax.nn.sigmoid(b_raw.astype(jnp.float32)).reshape(B, N, N_DIR, N_DN_HEADS)
    log_a = -jnp.exp(a_log.astype(jnp.float32)) * jax.nn.softplus(
        a_raw.astype(jnp.float32).reshape(B, N, N_DIR, N_DN_HEADS) + dt_bias.astype(jnp.float32))
    o, s_f, s_b = bidir_delta(q, k, v, beta, log_a, s0_f, s0_b)
    o = rmsnorm(o, g_o) * jax.nn.silu(z.astype(jnp.float32).reshape(B, N, N_DN_HEADS, DN_HEAD_V))
    mix = jnp.concatenate([o.reshape(B, N, DN_WIDTH).astype(x.dtype),
                           fourier_mix(f, w_fno, on_grid)], axis=-1)
    x = x + gate1 * (mix @ w_out)
    h2 = rmsnorm(x, g_ffn) * (1 + scale2) + shift2
    gt, up = jnp.split(h2 @ w_gu, 2, axis=-1)
    x = x + gate2 * ((jax.nn.silu(gt) * up) @ w_down)
    return x, s_f, s_b


def setup_inputs(seed: int = 0) -> dict:
    key = jax.random.key(seed)
    ks = jax.random.split(key, 24)
    nrm = lambda k, s, sc: jax.random.normal(k, s, jnp.float32) * sc
    L, D, H = DEPTH, D_MODEL, N_DN_HEADS
    state_shape = (DEC_BATCH, L, H, DN_HEAD_K, DN_HEAD_V)
    dt = jnp.exp(jax.random.uniform(ks[13], (L, N_DIR, H), jnp.float32,
                                    np.log(1e-3), np.log(1e-1)))
    return {
        "x_prompt": nrm(ks[0], (BATCH, SEQ, D), 1.0),
        "x_sample": nrm(ks[1], (DEC_BATCH, DEC_SEQ, D), 1.0),
        "c": nrm(ks[2], (DEC_BATCH, D), 1.0),
        "state_dn_fwd": nrm(ks[3], state_shape, DN_HEAD_K ** -0.5),
        "state_dn_bwd": nrm(ks[4], state_shape, DN_HEAD_K ** -0.5),
        "c_ctx": nrm(ks[5], (D,), 1.0),
        "w_ada": nrm(ks[6], (L, D, 6 * D), D ** -0.5),
        "b_ada": nrm(ks[7], (L, 6 * D), 0.02),
        "g_mix": 1.0 + nrm(ks[8], (L, D), 0.02),
        "w_in": nrm(ks[9], (L, D, IN_COLS), D ** -0.5),
        "w_conv": nrm(ks[10], (L, CONV_K, QKV_WIDTH), CONV_K ** -0.5),
        "a_log": jnp.log(jax.random.uniform(ks[11], (L, N_DIR, H), jnp.float32, 1.0, 16.0)),
        "dt_bias": dt + jnp.log(-jnp.expm1(-dt)),
        "g_o": 1.0 + nrm(ks[12], (L, DN_HEAD_V), 0.02),
        "w_fno": nrm(ks[14], (L, N_FOURIER_GROUPS, FOURIER_GROUP, FOURIER_GROUP), FOURIER_GROUP ** -0.5),
        "w_out": nrm(ks[15], (L, MIX_WIDTH, D), MIX_WIDTH ** -0.5),
        "g_ffn": 1.0 + nrm(ks[16], (L, D), 0.02),
        "w_gu": nrm(ks[17], (L, D, 2 * D_FF), D ** -0.5),
        "w_down": nrm(ks[18], (L, D_FF, D), D_FF ** -0.5),
        "g_final": 1.0 + nrm(ks[19], (D,), 0.02),
    }


def reference(x_prompt, x_sample, c, state_dn_fwd, state_dn_bwd, c_ctx, w_ada, b_ada, g_mix,
              w_in, w_conv, a_log, dt_bias, g_o, w_fno, w_out, g_ffn, w_gu, w_down, g_final):
    bp = x_prompt.shape[0]
    zero = jnp.zeros((bp, N_DN_HEADS, DN_HEAD_K, DN_HEAD_V), jnp.float32)
    hp, hs = x_prompt, x_sample
    new_f, new_b = [], []
    for l in range(DEPTH):
        w = (w_ada[l], b_ada[l], g_mix[l], w_in[l], w_conv[l], a_log[l], dt_bias[l], g_o[l],
             w_fno[l], w_out[l], g_ffn[l], w_gu[l], w_down[l])
        hp, s_f, s_b = trunk_layer(hp, c_ctx, zero, zero, False, *w)
        new_f.append(s_f)
        new_b.append(s_b)
        hs, _, _ = trunk_layer(hs, c, state_dn_fwd[:, l].astype(jnp.float32),
                               state_dn_bwd[:, l].astype(jnp.float32), True, *w)
    y_prompt = rmsnorm(hp, g_final)
    y_sample = rmsnorm(hs, g_final)
    new_state_fwd = jnp.stack(new_f, axis=1)
    new_state_bwd = jnp.stack(new_b, axis=1)
    return (y_prompt, y_sample, new_state_fwd, new_state_bwd)
```

```python
import numpy as np
import ml_dtypes
import concourse.bass as bass
import concourse.mybir as mybir
from concourse.bass_utils import run_bass_kernel_spmd

F32 = mybir.dt.float32
BF16 = mybir.dt.bfloat16
AF = mybir.ActivationFunctionType
ALU = mybir.AluOpType
AX = mybir.AxisListType

ENGS = ("pe", "act", "dve", "pool", "sp")
D = 1024
NCOL = 2576
DFF = 2816
EPS = 1e-6
DBG_TILES = None
SKEW = 8
DBG_STAGE = None
DBG_DIRS = (0, 1)


class Buf:
    def __init__(self, t, key):
        self.t = t
        self.key = key

    def __getitem__(self, idx):
        return self.t[idx]


class Ring:
    def __init__(self, bufs):
        self.bufs = bufs
        self.i = 0

    def get(self):
        b = self.bufs[self.i]
        self.i = (self.i + 1) % len(self.bufs)
        return b


class Stream:
    def __init__(self, P, sid):
        self.P = P
        self.sid = sid
        self.free = {}

    def alloc(self, name, shape, dtype=F32):
        P = self.P
        n = int(np.prod(shape[1:])) * (2 if dtype == F32 else 1)
        n = (n + 15) // 16 * 16
        if n >= 200:
            n = (n + 255) // 256 * 256
        lst = self.free.setdefault(n, [])
        if lst:
            off, key = lst.pop()
        else:
            off = (P.aoff + 15) // 16 * 16
            assert off + n <= P.alim, (name, off, n, P.alim)
            P.aoff = off + n
            P.nbuf += 1
            key = f"slot{self.sid}_{P.nbuf}"
        m = int(np.prod(shape[1:])) * (2 if dtype == F32 else 1)
        v = P.arena[:, off:off + m]
        if dtype == F32:
            v = v.bitcast(F32)
        if len(shape) == 3:
            v = v.rearrange("p (a b) -> p a b", a=shape[1])
        elif len(shape) == 4:
            v = v.rearrange("p (a b c) -> p a b c", a=shape[1], b=shape[2])
        b = Buf(v, key)
        b.slot = (n, off, key)
        return b

    def release(self, b):
        n, off, key = b.slot
        self.free[n].append((off, key))


class Prog:
    def __init__(self, nc, n_dma_sems=8):
        self.nc = nc
        self.ops = {e: [] for e in ENGS}
        self.cnt = {e: 0 for e in ENGS}
        self.sem = {e: nc.alloc_semaphore("sem_" + e) for e in ENGS}
        self.dq = {}
        for q in ("sp", "act", "pool"):
            self.dq[q] = dict(sems=[nc.alloc_semaphore(f"dsem_{q}{i}") for i in range(n_dma_sems)],
                              val=[0] * n_dma_sems, nxt=0)
        self.waited = {e: {} for e in ENGS}
        self.lastw = {}
        self.readers = {}
        self.out_events = []
        self.rings = {}
        self.arena_rings = []
        self.nbuf = 0

    def buf(self, name, shape, dtype=F32):
        self.nbuf += 1
        return Buf(self.nc.alloc_sbuf_tensor(f"{name}_{self.nbuf}", list(shape), dtype), f"{name}_{self.nbuf}")

    def ring(self, name, shape, dtype=F32, bufs=2):
        if name not in self.rings:
            self.rings[name] = Ring([self.buf(name, shape, dtype) for _ in range(bufs)])
        return self.rings[name].get()

    def init_arena(self, nelem):
        self.arena = self.nc.alloc_sbuf_tensor("arena", [128, nelem], BF16)
        self.aoff = 0
        self.alim = nelem

    def abuf(self, name, shape, dtype=F32):
        n = int(np.prod(shape[1:])) * (2 if dtype == F32 else 1)
        off = (self.aoff + 15) // 16 * 16
        assert off + n <= self.alim, (name, off, n, self.alim)
        self.aoff = off + n
        v = self.arena[:, off:off + n]
        if dtype == F32:
            v = v.bitcast(F32)
        if len(shape) == 3:
            v = v.rearrange("p (a b) -> p a b", a=shape[1])
        elif len(shape) == 4:
            v = v.rearrange("p (a b c) -> p a b c", a=shape[1], b=shape[2])
        self.nbuf += 1
        return Buf(v, f"{name}_{self.nbuf}")

    def aring(self, name, shape, dtype=F32, bufs=1):
        if name not in self.rings:
            self.rings[name] = Ring([self.abuf(name, shape, dtype) for _ in range(bufs)])
            self.arena_rings.append(name)
        return self.rings[name].get()

    def arena_reset(self, off):
        self.barrier()
        for nme in self.arena_rings:
            self.rings.pop(nme, None)
        self.arena_rings = []
        self.aoff = off

    def init_psum(self):
        self.psr = Ring([Buf(self.nc.alloc_psum_tensor(f"psb{i}", [128, 512], F32), f"psb{i}") for i in range(8)])
        self.psub = [Ring(self.psr.bufs[0:4]), Ring(self.psr.bufs[4:8])]

    def ps(self, sid=None):
        if sid is None:
            return self.psr.get()
        return self.psub[sid].get()

    def _need(self, eng, ev, waits):
        if ev is None:
            return
        sem, val, owner = ev
        if owner == "pe" and eng == "pe":
            return
        sid = id(sem)
        if self.waited[eng].get(sid, 0) >= val:
            return
        cur = waits.get(sid)
        if cur is None or cur[1] < val:
            waits[sid] = (sem, val)

    def _deps(self, eng, reads, writes):
        waits = {}
        for b in reads:
            self._need(eng, self.lastw.get(b.key), waits)
            if b.key.startswith("psb"):
                for ev in self.readers.get(b.key, ()):
                    if ev[2] != eng:
                        self._need(eng, ev, waits)
        for b in writes:
            self._need(eng, self.lastw.get(b.key), waits)
            for ev in self.readers.get(b.key, ()):
                self._need(eng, ev, waits)
        for sid, (sem, val) in waits.items():
            self.waited[eng][sid] = val
        return list(waits.values())

    def _commit(self, ev, reads, writes):
        for b in reads:
            self.readers.setdefault(b.key, []).append(ev)
        for b in writes:
            self.lastw[b.key] = ev
            self.readers[b.key] = []

    def op(self, eng, fn, reads=(), writes=()):
        waits = self._deps(eng, reads, writes)
        self.cnt[eng] += 1
        ev = (self.sem[eng], self.cnt[eng], eng)
        self.ops[eng].append((fn, waits, (self.sem[eng], 1)))
        self._commit(ev, reads, writes)
        return ev

    def dma(self, q, out, in_, reads=(), writes=(), is_output=False, slow=False):
        d = self.dq[q]
        i = d["nxt"]
        d["nxt"] = (i + 1) % len(d["sems"])
        sem = d["sems"][i]
        waits = self._deps(q, reads, writes)
        prev = d["val"][i]
        if prev > 0 and self.waited[q].get(id(sem), 0) < prev:
            waits.append((sem, prev))
            self.waited[q][id(sem)] = prev
        d["val"][i] = prev + 16
        ev = (sem, prev + 16, None)
        if slow:
            fn = lambda e: e.dma_start(out=out, in_=in_, allow_slow_non_contiguous=True)
        else:
            fn = lambda e: e.dma_start(out=out, in_=in_)
        self.ops[q].append((fn, waits, (sem, 16)))
        self._commit(ev, reads, writes)
        if is_output:
            self.out_events.append(ev)
        return ev

    def barrier(self):
        evs = [(self.sem[e], self.cnt[e]) for e in ENGS if self.cnt[e] > 0]
        for q in self.dq.values():
            for s, v in zip(q["sems"], q["val"]):
                if v > 0:
                    evs.append((s, v))
        for e in ENGS:
            waits = []
            for s, v in evs:
                if self.waited[e].get(id(s), 0) < v:
                    waits.append((s, v))
                    self.waited[e][id(s)] = v
            if waits:
                self.cnt[e] += 1
                self.ops[e].append((lambda eng: eng.nop(), waits, (self.sem[e], 1)))

    def mm(self, ps, out, lhsT, rhs, reads, start=True, stop=True):
        return self.op("pe", lambda e: e.matmul(out, lhsT=lhsT, rhs=rhs, start=start, stop=stop),
                       reads=reads, writes=[ps])

    def tr(self, ps, out, in_, ident, reads):
        return self.op("pe", lambda e: e.transpose(out, in_, ident), reads=reads, writes=[ps])

    def act(self, out, in_, func, reads, writes, scale=1.0, bias=0.0, accum_out=None):
        return self.op("act", lambda e: e.activation(out=out, in_=in_, func=func, bias=bias, scale=scale,
                                                     accum_out=accum_out), reads=reads, writes=writes)

    def ts(self, eng, out, in0, s1, op0, reads, writes, s2=None, op1=None):
        if op1 is None:
            return self.op(eng, lambda e: e.tensor_scalar(out=out, in0=in0, scalar1=s1, scalar2=None, op0=op0),
                           reads=reads, writes=writes)
        return self.op(eng, lambda e: e.tensor_scalar(out=out, in0=in0, scalar1=s1, scalar2=s2, op0=op0, op1=op1),
                       reads=reads, writes=writes)

    def tt(self, eng, out, in0, in1, op, reads, writes):
        return self.op(eng, lambda e: e.tensor_tensor(out=out, in0=in0, in1=in1, op=op), reads=reads, writes=writes)

    def stt(self, eng, out, in0, scalar, in1, op0, op1, reads, writes):
        return self.op(eng, lambda e: e.scalar_tensor_tensor(out=out, in0=in0, scalar=scalar, in1=in1, op0=op0, op1=op1),
                       reads=reads, writes=writes)

    def cp(self, eng, out, in_, reads, writes):
        if eng == "act":
            return self.act(out, in_, AF.Copy, reads, writes)
        return self.op(eng, lambda e: e.tensor_copy(out=out, in_=in_), reads=reads, writes=writes)

    def memset(self, eng, b, ap, val):
        return self.op(eng, lambda e: e.memset(ap, val), writes=[b])

    def emit(self):
        nc = self.nc
        fin = {}
        for sem, val, _ in self.out_events:
            if id(sem) not in fin or fin[id(sem)][1] < val:
                fin[id(sem)] = (sem, val)
        final_waits = list(fin.values())
        prog = self

        def run(engname, eng):
            for fn, waits, inc in prog.ops[engname]:
                for sem, val in waits:
                    eng.wait_ge(sem, val)
                fn(eng).then_inc(inc[0], inc[1])
            if engname == "sp":
                for sem, val in final_waits:
                    eng.wait_ge(sem, val)

        with nc.Block() as block:
            @block.tensor
            def _(e):
                run("pe", e)

            @block.scalar
            def _(e):
                run("act", e)

            @block.vector
            def _(e):
                run("dve", e)

            @block.gpsimd
            def _(e):
                run("pool", e)

            @block.sync
            def _(e):
                run("sp", e)


def _const_tables():
    idx = np.arange(128)
    same64 = (idx[:, None] // 64) == (idx[None, :] // 64)
    same16 = (idx[:, None] // 16) == (idx[None, :] // 16)
    cols = {}
    cols["ident"] = np.eye(128)
    cols["ones"] = np.ones((128, 128))
    for d in (0, 1):
        if d == 0:
            le = idx[:, None] <= idx[None, :]
            strict = idx[:, None] > idx[None, :]
        else:
            le = idx[:, None] >= idx[None, :]
            strict = idx[:, None] < idx[None, :]
        cols[f"U{d}"] = (le & same64) * 1.0
        cols[f"NMD{d}"] = -1.0 * (strict & same16)
        cols[f"MO{d}"] = 1.0 * (strict & same64 & ~same16)
        incl_ij = (strict | np.eye(128, dtype=bool)) & same64
        cols[f"MIT{d}"] = incl_ij.T * 1.0
    mab = np.zeros((128, 128))
    mab[:64, 0] = 1.0
    mab[64:, 1] = 1.0
    cols["mab"] = mab
    ang = 2 * np.pi * np.outer(idx, idx) / 128.0
    cols["CC"] = np.cos(ang)
    cols["nSC"] = -np.sin(ang)
    cols["SC"] = np.sin(ang)
    n = np.arange(256)
    angp = 2 * np.pi * np.outer(n, n) / 256.0
    sp = (256 * 128) ** -0.5
    cn = np.cos(angp) * sp
    sn = np.sin(angp) * sp
    for c in range(2):
        cols[f"CN{c}a"] = cn[c * 128:(c + 1) * 128, 0:128]
        cols[f"CN{c}b"] = cn[c * 128:(c + 1) * 128, 128:256]
        cols[f"SN{c}a"] = sn[c * 128:(c + 1) * 128, 0:128]
        cols[f"SN{c}b"] = sn[c * 128:(c + 1) * 128, 128:256]
    r = np.arange(64)
    a64 = 2 * np.pi * np.outer(r, r) / 64.0
    z = np.zeros((64, 64))
    bd = lambda m: np.block([[m, z], [z, m]])
    ss = (4096 * 128) ** -0.5
    cols["BDC"] = bd(np.cos(a64))
    cols["BDS"] = bd(np.sin(a64))
    cols["BDCs"] = bd(np.cos(a64)) * ss
    cols["nBDSs"] = -bd(np.sin(a64)) * ss
    names = list(cols.keys())
    arr = np.concatenate([cols[k] for k in names], axis=1).astype(np.float32)
    off = {k: i * 128 for i, k in enumerate(names)}
    return arr, off


_CST, _COFF = _const_tables()
NCST = _CST.shape[1]


def build_nc(stop_after=None, dbg=False):
    nc = bass.Bass("TRN2", target_bir_lowering=False)
    dt = nc.dram_tensor
    xin = {0: dt("xp", [1024, D], F32, kind="ExternalInput").ap(),
           1: dt("xs", [4096, D], F32, kind="ExternalInput").ap()}
    cvec = dt("cvec", [2, D], F32, kind="ExternalInput").ap()
    s0f = dt("s0f", [4, 128, 128], F32, kind="ExternalInput").ap()
    s0b = dt("s0b", [4, 128, 128], F32, kind="ExternalInput").ap()
    w_ada = dt("w_ada", [D, 6 * D], F32, kind="ExternalInput").ap()
    b_ada = dt("b_ada", [6 * D], F32, kind="ExternalInput").ap()
    g_mix = dt("g_mix", [D], F32, kind="ExternalInput").ap()
    w_in = dt("w_in", [D, NCOL], F32, kind="ExternalInput").ap()
    w_conv = dt("w_conv", [3, 1536], F32, kind="ExternalInput").ap()
    a_log = dt("a_log", [8], F32, kind="ExternalInput").ap()
    dt_bias = dt("dt_bias", [8], F32, kind="ExternalInput").ap()
    g_o = dt("g_o", [128], F32, kind="ExternalInput").ap()
    w_fno = dt("w_fno", [4, 128, 128], F32, kind="ExternalInput").ap()
    w_out = dt("w_out", [D, D], F32, kind="ExternalInput").ap()
    g_ffn = dt("g_ffn", [D], F32, kind="ExternalInput").ap()
    w_gu = dt("w_gu", [D, 2 * DFF], F32, kind="ExternalInput").ap()
    w_down = dt("w_down", [DFF, D], F32, kind="ExternalInput").ap()
    g_final = dt("g_final", [D], F32, kind="ExternalInput").ap()
    cst_d = dt("cst", [128, NCST], F32, kind="ExternalInput").ap()
    yout = {0: dt("yp", [1024, D], F32, kind="ExternalOutput").ap(),
            1: dt("ys", [4096, D], F32, kind="ExternalOutput").ap()}
    nsf = dt("nsf", [4, 4, 128, 128], F32, kind="ExternalOutput").ap()
    nsb = dt("nsb", [4, 4, 128, 128], F32, kind="ExternalOutput").ap()
    OF = dt("scr_of", [5120, 512], F32).ap()
    X1 = dt("scr_x1", [5120, D], F32).ap()
    SQ = dt("scr_sq", [40, 128, 1536], F32).ap()
    SK = dt("scr_sk", [40, 128, 1024], F32).ap()
    SL = dt("scr_sl", [40, 128, 16], F32).ap()
    dbg_out = {}

    P = Prog(nc)
    P.init_psum()
    P.init_arena(95000)

    cst = P.abuf("cst", [128, NCST])
    P.dma("sp", cst[:, :], cst_d, writes=[cst])
    C = lambda name, w=128: cst[:, _COFF[name]:_COFF[name] + w]
    ident = C("ident")
    ones = C("ones")
    cbf = P.buf("cbf", [128, 6 * 128], BF16)
    BFOFF = {}
    for i, nm in enumerate(["ident", "BDC", "BDS", "BDCs", "nBDSs"]):
        P.cp("dve", cbf[:, i * 128:(i + 1) * 128], C(nm), reads=[cst], writes=[cbf])
        BFOFF[nm] = i * 128
    CB = lambda nm: cbf[:, BFOFF[nm]:BFOFF[nm] + 128]
    cnbf = P.abuf("cnbf", [128, 1024], BF16)
    o = _COFF["CN0a"]
    P.cp("dve", cnbf[:, :], cst[:, o:o + 1024], reads=[cst], writes=[cnbf])
    AB0 = P.aoff
    CNT = lambda c: cnbf[:, c * 512:c * 512 + 256]
    SNT = lambda c: cnbf[:, c * 512 + 256:c * 512 + 512]

    gmixT = P.buf("gmixT", [128, 8])
    gffnT = P.buf("gffnT", [128, 8])
    P.dma("sp", gmixT[:, :], g_mix.rearrange("(c p) -> p c", p=128), writes=[gmixT], slow=True)
    P.dma("sp", gffnT[:, :], g_ffn.rearrange("(c p) -> p c", p=128), writes=[gffnT], slow=True)
    wconvT = P.buf("wconvT", [128, 3, 12])
    for k in range(3):
        P.dma("sp", wconvT[:, k, :], w_conv[k].rearrange("(c p) -> p c", p=128), writes=[wconvT], slow=True)
    badaT = P.buf("badaT", [128, 48])
    P.dma("sp", badaT[:, :], b_ada.rearrange("(c p) -> p c", p=128), writes=[badaT], slow=True)
    cT = P.buf("cT", [128, 2, 8])
    for t in range(2):
        P.dma("sp", cT[:, t, :], cvec[t].rearrange("(c p) -> p c", p=128), writes=[cT], slow=True)
    alog_bc = P.buf("alog", [128, 8])
    dtb_bc = P.buf("dtb", [128, 8])
    P.dma("sp", alog_bc[:, :], a_log.partition_broadcast(128), writes=[alog_bc])
    P.dma("sp", dtb_bc[:, :], dt_bias.partition_broadcast(128), writes=[dtb_bc])
    go_bc = P.buf("go", [128, 128])
    P.dma("sp", go_bc[:, :], g_o.partition_broadcast(128), writes=[go_bc])
    bgate = P.abuf("bgate", [128, 2, D])
    P.dma("sp", bgate[:, 0, :], b_ada[2 * D:3 * D].partition_broadcast(128), writes=[bgate])
    P.dma("sp", bgate[:, 1, :], b_ada[5 * D:6 * D].partition_broadcast(128), writes=[bgate])
    eps_c = P.buf("epsc", [128, 1])
    P.memset("dve", eps_c, eps_c[:, :], EPS)
    nexpA = P.buf("nexpA", [128, 8])
    P.act(nexpA[:, :], alog_bc[:, :], AF.Exp, reads=[alog_bc], writes=[nexpA])
    P.ts("dve", nexpA[:, :], nexpA[:, :], -1.0, ALU.mult, reads=[nexpA], writes=[nexpA])

    scT = P.buf("scT", [128, 8, 2])
    for t in range(2):
        P.act(scT[:, :, t], cT[:, t, :], AF.Silu, reads=[cT], writes=[scT])
    scbc = P.abuf("scbc", [128, 2, 8, 128])
    for t in range(2):
        P.cp("dve", scbc[:, t, :, :], scT[:, :, t].unsqueeze(2).to_broadcast([128, 8, 128]), reads=[scT], writes=[scbc])
    modT = P.buf("modT", [128, 48, 2])
    gates = P.buf("gates", [128, 2, 2, D])
    for blk in range(12):
        wst = P.aring("wada_st", [128, 8, 512], F32, bufs=2)
        P.dma("sp" if blk % 2 == 0 else "act", wst[:, :, :],
              w_ada[:, blk * 512:(blk + 1) * 512].rearrange("(c p) n -> p c n", p=128), writes=[wst])
        ps = P.ps()
        for fc in range(4):
            for kc in range(8):
                P.mm(ps, ps[:, fc * 2:fc * 2 + 2], wst[:, kc, fc * 128:(fc + 1) * 128], scT[:, kc, :],
                     reads=[wst, scT], start=(kc == 0), stop=(kc == 7))
        P.tt("dve", modT[:, blk * 4:blk * 4 + 4, :], ps[:, 0:8].rearrange("p (f t) -> p f t", t=2),
             badaT[:, blk * 4:blk * 4 + 4].unsqueeze(2).to_broadcast([128, 4, 2]), ALU.add,
             reads=[ps, badaT], writes=[modT])
        gsel = {4: (0, 0), 5: (0, 1), 10: (1, 0), 11: (1, 1)}.get(blk)
        if gsel is not None:
            gi, half = gsel
            for t in range(2):
                ps2 = P.ps()
                for kc in range(8):
                    P.mm(ps2, ps2[:, :], scbc[:, t, kc, :], wst[:, kc, :], reads=[scbc, wst],
                         start=(kc == 0), stop=(kc == 7))
                P.tt("dve", gates[:, t, gi, half * 512:(half + 1) * 512], ps2[:, :],
                     bgate[:, gi, half * 512:(half + 1) * 512], ALU.add, reads=[ps2, bgate], writes=[gates])
    s1T = P.buf("s1T", [128, 8, 2]); sh1T = P.buf("sh1T", [128, 8, 2])
    s2T = P.buf("s2T", [128, 8, 2]); sh2T = P.buf("sh2T", [128, 8, 2])
    for (sT, shT, gT, base) in ((s1T, sh1T, gmixT, 0), (s2T, sh2T, gffnT, 24)):
        P.cp("dve", shT[:, :, :], modT[:, base:base + 8, :], reads=[modT], writes=[shT])
        P.ts("dve", sT[:, :, :], modT[:, base + 8:base + 16, :], 1.0, ALU.add, reads=[modT], writes=[sT])
        P.tt("dve", sT[:, :, :], sT[:, :, :], gT[:, :].unsqueeze(2).to_broadcast([128, 8, 2]), ALU.mult,
             reads=[sT, gT], writes=[sT])
    if dbg:
        dbg_out["modT"] = dt("dbg_modT", [128, 96], F32, kind="ExternalOutput").ap()
        P.dma("sp", dbg_out["modT"], modT[:, :, :].rearrange("p f t -> p (f t)"), reads=[modT], is_output=True)
        dbg_out["gates"] = dt("dbg_gates", [128, 4 * D], F32, kind="ExternalOutput").ap()
        P.dma("sp", dbg_out["gates"], gates[:, :, :, :].rearrange("p a b d -> p (a b d)"), reads=[gates], is_output=True)
    if stop_after == "mod":
        P.emit()
        return nc

    class Key:
        def __init__(s, k):
            s.key = k

    P.arena_reset(AB0)
    winb = P.abuf("winb", [128, 8, NCOL], BF16)
    woutb = P.abuf("woutb", [128, 8, D], BF16)
    wfs = P.abuf("wfs", [128, 4, 128])
    W1 = P.abuf("W1", [128, 4, 256], BF16)
    W2 = P.abuf("W2", [128, 4, 256], BF16)
    eps128 = P.abuf("eps128", [128, 1])
    zt = P.abuf("zt", [128, 8, 1], BF16)
    Sst = [P.abuf("Sf", [128, 4, 128]), P.abuf("Sb", [128, 4, 128])]
    SCR0 = P.aoff
    for kc in range(8):
        st = P.aring("wst", [128, NCOL], F32, bufs=2)
        P.dma("sp" if kc % 2 == 0 else "act", st[:, :], w_in[kc * 128:(kc + 1) * 128, :], writes=[st])
        P.cp("dve" if kc % 2 == 0 else "pool", winb[:, kc, :], st[:, :], reads=[st], writes=[winb])
    for kc in range(8):
        st = P.aring("wst2", [128, D], F32, bufs=2)
        P.dma("sp" if kc % 2 == 0 else "act", st[:, :], w_out[kc * 128:(kc + 1) * 128, :], writes=[st])
        P.cp("dve" if kc % 2 == 0 else "pool", woutb[:, kc, :], st[:, :], reads=[st], writes=[woutb])
    P.dma("sp", wfs[:, :, :], w_fno.rearrange("g l d -> l g d"), writes=[wfs])
    for (tab, dsts) in (("CC", ((W1, 0), (W2, 128))), ("SC", ((W1, 128),)), ("nSC", ((W2, 0),))):
        ps = P.ps()
        for g in range(4):
            P.mm(ps, ps[:, g * 128:(g + 1) * 128], C(tab), wfs[:, g, :], reads=[cst, wfs])
        for (Wt, o_) in dsts:
            P.cp("dve", Wt[:, :, o_:o_ + 128], ps[:, :].rearrange("p (g d) -> p g d", g=4), reads=[ps], writes=[Wt])
    P.memset("dve", eps128, eps128[:, :], 128.0 * EPS)
    P.memset("dve", zt, zt[:, :, :], 0.0)
    mk = lambda nm: C(nm).unsqueeze(1).to_broadcast([128, 4, 128])
    v3 = lambda ps_: ps_[:, :].rearrange("p (h j) -> p h j", h=4)
    bc4 = lambda ap: ap.unsqueeze(2).to_broadcast([128, 4, 128])
    dq = ["sp", "act"]

    seqs = [(0, s, 256, 0) for s in range(4)] + [(1, 0, 4096, 1)]
    if stop_after == "seq0":
        seqs = seqs[:1]
    if stop_after in ("seq4", "seq4_a0", "seq4_f"):
        seqs = seqs[4:]
    if stop_after == "ffnonly":
        seqs = []
    tokbase = 0
    for (grp, si, N, cond) in seqs:
        xr = xin[grp][si * N:(si + 1) * N, :]
        NTL = N // 128
        HT = dt(f"ht_{grp}_{si}", [128, 8, N + 2], BF16).ap()
        YTD = dt(f"ytd_{grp}_{si}", [128, 4, N], BF16).ap()
        htk = [Key(f"HT{grp}{si}_{b}") for b in range(NTL)]
        hk0 = Key(f"HTz{grp}{si}")
        ytk = Key(f"YT{grp}{si}")
        P.arena_reset(SCR0)
        P.dma("sp", HT[:, :, 0:1], zt[:, :, :], reads=[zt], writes=[hk0], slow=True)
        P.dma("sp", HT[:, :, N + 1:N + 2], zt[:, :, :], reads=[zt], writes=[hk0], slow=True)
        for b in range(NTL):
            xt = P.aring("xt", [128, D], F32, bufs=2)
            P.dma("sp", xt[:, :], xr[b * 128:(b + 1) * 128, :], writes=[xt])
            junk = P.aring("junk", [128, D], F32)
            ss = P.ring("ss", [128, 1], F32, bufs=2)
            P.act(junk[:, :], xt[:, :], AF.Square, reads=[xt], writes=[junk, ss], accum_out=ss[:, :])
            P.act(ss[:, :], ss[:, :], AF.Ln, reads=[ss, eps_c], writes=[ss], scale=1.0 / D, bias=eps_c[:, :])
            P.act(ss[:, :], ss[:, :], AF.Exp, reads=[ss], writes=[ss], scale=-0.5)
            xnb = P.aring("xnb", [128, D], BF16)
            P.act(xnb[:, :], xt[:, :], AF.Copy, reads=[xt, ss], writes=[xnb], scale=ss[:, :])
            ps = P.ps()
            psb = ps.t[:, :].bitcast(BF16)
            for kc in range(8):
                P.tr(ps, psb[:, kc * 128:(kc + 1) * 128], xnb[:, kc * 128:(kc + 1) * 128], CB("ident"), reads=[xnb, cbf])
            tmp = P.aring("httmp", [128, 8, 128], F32)
            P.tt("dve", tmp[:, :, :], psb.rearrange("p (k t) -> p k t", k=8),
                 s1T[:, :, cond].unsqueeze(2).to_broadcast([128, 8, 128]), ALU.mult, reads=[ps, s1T], writes=[tmp])
            hb = P.aring("hb", [128, 8, 128], BF16, bufs=2)
            P.tt("pool", hb[:, :, :], tmp[:, :, :], sh1T[:, :, cond].unsqueeze(2).to_broadcast([128, 8, 128]), ALU.add,
                 reads=[tmp, sh1T], writes=[hb])
            P.dma("pool", HT[:, :, 1 + b * 128:1 + (b + 1) * 128], hb[:, :, :], reads=[hb], writes=[htk[b]])

        def load_ht(t):
            ht = P.aring("ht", [128, 8, 130], BF16, bufs=2)
            rk = [hk0] + [htk[j] for j in (t - 1, t, t + 1) if 0 <= j < NTL]
            P.dma("sp", ht[:, :, :], HT[:, :, t * 128:t * 128 + 130], reads=rk, writes=[ht])
            return ht

        if stop_after == "seq4_a0":
            dbg_out["o"] = dt("dbg_o", [128, 8], F32, kind="ExternalOutput").ap()
            P.dma("sp", dbg_out["o"], gmixT[:, :], reads=[gmixT] + htk, is_output=True)
            P.emit()
            return nc
        P.arena_reset(SCR0)
        if N == 256:
            fbs = []
            for blk in range(2):
                ht = load_ht(blk)
                ps = P.ps()
                for kc in range(8):
                    P.mm(ps, ps[:, :], ht[:, kc, 1:129], winb[:, kc, 2064:2576], reads=[ht, winb], start=(kc == 0), stop=(kc == 7))
                fb = P.abuf("fb", [128, 512], BF16)
                P.cp("act", fb[:, :], ps[:, :], reads=[ps], writes=[fb])
                fbs.append(fb)
            ytg = P.abuf("ytg", [128, 4, 256], BF16)
            for g in range(4):
                psA = P.ps()
                for blk in range(2):
                    P.mm(psA, psA[:, 0:256], fbs[blk][:, g * 128:(g + 1) * 128], CNT(blk), reads=[fbs[blk], cnbf], start=(blk == 0), stop=(blk == 1))
                for blk in range(2):
                    P.mm(psA, psA[:, 256:512], fbs[blk][:, g * 128:(g + 1) * 128], SNT(blk), reads=[fbs[blk], cnbf], start=(blk == 0), stop=(blk == 1))
                Ab = P.aring("Ab", [128, 512], BF16, bufs=2)
                P.cp("dve", Ab[:, :], psA[:, :], reads=[psA], writes=[Ab])
                psY = P.ps()
                P.mm(psY, psY[:, 0:256], W1[:, g, 0:128], Ab[:, 0:256], reads=[W1, Ab], start=True, stop=False)
                P.mm(psY, psY[:, 0:256], W2[:, g, 0:128], Ab[:, 256:512], reads=[W2, Ab], start=False, stop=True)
                P.cp("act", ytg[:, g, :], psY[:, 0:256], reads=[psY], writes=[ytg])
            P.dma("act", YTD[:, :, :], ytg[:, :, :], reads=[ytg], writes=[ytk])
        else:
            A_all = P.abuf("A_all", [128, 2, 4, 4096], BF16)
            for rp in range(32):
                ht = load_ht(rp)
                ps = P.ps()
                for kc in range(8):
                    P.mm(ps, ps[:, :], ht[:, kc, 1:129], winb[:, kc, 2064:2576], reads=[ht, winb], start=(kc == 0), stop=(kc == 7))
                fb = P.aring("fb", [128, 512], BF16, bufs=2)
                P.cp("act", fb[:, :], ps[:, :], reads=[ps], writes=[fb])
                for gp in range(2):
                    ps2 = P.ps()
                    for gg in range(2):
                        g = gp * 2 + gg
                        P.mm(ps2, ps2[:, gg * 256:gg * 256 + 128], fb[:, g * 128:(g + 1) * 128], CB("BDC"), reads=[fb, cbf])
                        P.mm(ps2, ps2[:, gg * 256 + 128:gg * 256 + 256], fb[:, g * 128:(g + 1) * 128], CB("BDS"), reads=[fb, cbf])
                    for ri in range(2):
                        dst = A_all[:, ri, gp * 2:gp * 2 + 2, :].rearrange("p g (kc r) -> p g kc r", r=64)[:, :, :, 2 * rp:2 * rp + 2]
                        dst = dst.rearrange("p g kc b -> p g b kc")
                        src = ps2[:, :].rearrange("p (g ri b kc) -> p ri g b kc", g=2, ri=2, b=2)[:, ri]
                        P.cp("dve" if ri == 0 else "act", dst, src, reads=[ps2], writes=[A_all])
            for g in range(4):
                ytg = P.aring("ytg", [128, 4096], BF16, bufs=1)
                def s3(kcp_, Bb_):
                    psY = P.ps()
                    P.mm(psY, psY[:, 0:128], Bb_[:, 0:128], CB("BDCs"), reads=[Bb_, cbf], start=True, stop=False)
                    P.mm(psY, psY[:, 0:128], Bb_[:, 128:256], CB("nBDSs"), reads=[Bb_, cbf], start=False, stop=True)
                    P.cp("dve", ytg[:, :].rearrange("p (kr kc) -> p kc kr", kc=64)[:, 2 * kcp_:2 * kcp_ + 2, :],
                         psY[:, 0:128].rearrange("p (a k) -> p a k", a=2), reads=[psY], writes=[ytg])
                pendf = None
                for kcp in range(32):
                    sub = lambda ri: A_all[:, ri, g, kcp * 128:(kcp + 1) * 128]
                    psB = P.ps()
                    P.mm(psB, psB[:, 0:256], sub(0), W1[:, g, :], reads=[A_all, W1], start=True, stop=False)
                    P.mm(psB, psB[:, 0:256], sub(1), W2[:, g, :], reads=[A_all, W2], start=False, stop=True)
                    Bb = P.aring("Bb", [128, 256], BF16, bufs=3)
                    P.cp("act", Bb[:, :], psB[:, 0:256], reads=[psB], writes=[Bb])
                    if pendf is not None:
                        s3(*pendf)
                    pendf = (kcp, Bb)
                s3(*pendf)
                P.dma("act", YTD[:, g, :], ytg[:, :], reads=[ytg], writes=[ytk])

        if stop_after == "seq4_f":
            dbg_out["o"] = dt("dbg_o", [128, 8], F32, kind="ExternalOutput").ap()
            P.dma("sp", dbg_out["o"], gmixT[:, :], reads=[gmixT, ytk], is_output=True)
            P.emit()
            return nc
        P.arena_reset(SCR0)
        for T2 in range(N // 256):
            ht2 = P.aring("ht2", [128, 8, 258], BF16, bufs=2)
            rk = [hk0] + [htk[j] for j in range(2 * T2 - 1, 2 * T2 + 3) if 0 <= j < NTL]
            P.dma("sp", ht2[:, :, :], HT[:, :, T2 * 256:T2 * 256 + 258], reads=rk, writes=[ht2])
            q2 = P.aring("q2", [128, 12, 256], F32, bufs=2)
            pend = None
            for cc in range(12):
                ps = P.ps()
                for kc in range(8):
                    P.mm(ps, ps[:, 0:258], winb[:, kc, cc * 128:(cc + 1) * 128], ht2[:, kc, :], reads=[winb, ht2], start=(kc == 0), stop=(kc == 7))
                c1 = P.aring("fc1", [128, 256], bufs=3)
                c2 = P.aring("fc2", [128, 256], bufs=3)
                P.act(c1[:, :], ps[:, 0:256], AF.Copy, reads=[ps, wconvT], writes=[c1], scale=wconvT[:, 0, cc:cc + 1])
                P.stt("dve", c2[:, :], ps[:, 1:257], wconvT[:, 1, cc:cc + 1], c1[:, :], ALU.mult, ALU.add, reads=[ps, c1, wconvT], writes=[c2])
                P.stt("dve", c1[:, :], ps[:, 2:258], wconvT[:, 2, cc:cc + 1], c2[:, :], ALU.mult, ALU.add, reads=[ps, c2, wconvT], writes=[c1])
                if pend is not None:
                    P.act(q2[:, pend[1], :], pend[0][:, :], AF.Silu, reads=[pend[0]], writes=[q2])
                pend = (c1, cc)
            P.act(q2[:, pend[1], :], pend[0][:, :], AF.Silu, reads=[pend[0]], writes=[q2])
            sq = P.aring("fsq", [128, 8, 256], F32, bufs=2)
            P.act(sq[:, :, :], q2[:, 0:8, :], AF.Square, reads=[q2], writes=[sq])
            rn = P.aring("frn", [128, 8, 256], F32, bufs=2)
            for pair in range(4):
                ps = P.ps()
                for j in range(2):
                    P.mm(ps, ps[:, j * 256:(j + 1) * 256], ones, sq[:, pair * 2 + j, :], reads=[cst, sq])
                P.act(rn[:, pair * 2:pair * 2 + 2, :], ps[:, :].rearrange("p (a b) -> p a b", a=2), AF.Ln, reads=[ps, eps128, eps_c], writes=[rn],
                      scale=(128.0 if pair < 2 else 1.0), bias=(eps128 if pair < 2 else eps_c)[:, :])
            P.act(rn[:, :, :], rn[:, :, :], AF.Exp, reads=[rn], writes=[rn], scale=-0.5)
            P.tt("dve", q2[:, 0:8, :], q2[:, 0:8, :], rn[:, :, :], ALU.mult, reads=[q2, rn], writes=[q2])
            for blk in range(2):
                gtp = tokbase // 128 + 2 * T2 + blk
                bs = slice(blk * 128, (blk + 1) * 128)
                knv = P.aring("fknv", [128, 2, 4, 128], F32, bufs=2)
                for which, base in ((0, 4), (1, 8)):
                    ps = P.ps()
                    for j in range(4):
                        P.tr(ps, ps[:, j * 128:(j + 1) * 128], q2[:, base + j, bs], ident, reads=[q2, cst])
                    P.cp("act", knv[:, which, :, :], v3(ps), reads=[ps], writes=[knv])
                P.dma("act", SK[gtp].rearrange("p (a h j) -> p a h j", a=2, h=4), knv[:, :, :, :], reads=[knv], writes=[Key(f"SHk{gtp}")])
                ps = P.ps()
                for kc in range(8):
                    P.mm(ps, ps[:, 0:16], ht2[:, kc, 1 + blk * 128:129 + blk * 128], winb[:, kc, 2048:2064], reads=[ht2, winb], start=(kc == 0), stop=(kc == 7))
                lab = P.aring("flab", [128, 16], F32, bufs=2)
                P.tt("dve", lab[:, 0:8], ps[:, 0:8], dtb_bc[:, :], ALU.add, reads=[ps, dtb_bc], writes=[lab])
                P.act(lab[:, 8:16], ps[:, 8:16], AF.Exp, reads=[ps], writes=[lab], scale=-1.0)
                P.act(lab[:, 0:8], lab[:, 0:8], AF.Exp, reads=[lab], writes=[lab])
                P.act(lab[:, 0:16], lab[:, 0:16], AF.Ln, reads=[lab], writes=[lab], bias=1.0)
                P.act(lab[:, 8:16], lab[:, 8:16], AF.Exp, reads=[lab], writes=[lab], scale=-1.0)
                P.tt("dve", lab[:, 0:8], lab[:, 0:8], nexpA[:, :], ALU.mult, reads=[lab, nexpA], writes=[lab])
                P.dma("act", SL[gtp], lab[:, :], reads=[lab], writes=[Key(f"SHl{gtp}")])
                P.dma("pool", SQ[gtp].rearrange("p (c t) -> p c t", c=12), q2[:, :, bs], reads=[q2], writes=[Key(f"SHq{gtp}")])
        P.arena_reset(SCR0)
        streams = [Stream(P, 0), Stream(P, 1)]
        for d in (0, 1):
            S = Sst[d]
            if grp == 0:
                P.memset("pool", S, S[:, :, :], 0.0)
            else:
                P.dma("sp", S[:, :, :], (s0f if d == 0 else s0b).rearrange("h k v -> k h v"), writes=[S])
            streams[d].u = streams[d].alloc("u", [128, 4, 128])
            P.memset("pool", streams[d].u, streams[d].u[:, :, :], 0.0)
            streams[d].oS = streams[d].alloc("oS", [128, 4, 128])

        def tile_gen(st, t, d, second):
            A = st.alloc
            R = st.release
            PS = lambda: P.ps(d)
            S = Sst[d]
            u, oS = st.u, st.oS
            gt_ = tokbase // 128 + t
            ht = None
            if second:
                ht = A("ht", [128, 8, 130], BF16)
            qkvT = A("qkvT", [128, 12, 128])
            knv = A("knv", [128, 2, 4, 128])
            lab = A("lab", [128, 16])
            P.dma("sp", lab[:, :], SL[gt_], reads=[Key(f"SHl{gt_}")], writes=[lab])
            P.dma("sp", qkvT[:, :, :], SQ[gt_].rearrange("p (c t) -> p c t", c=12), reads=[Key(f"SHq{gt_}")], writes=[qkvT])
            P.dma("sp", knv[:, :, :, :], SK[gt_].rearrange("p (a h j) -> p a h j", a=2, h=4), reads=[Key(f"SHk{gt_}")], writes=[knv])
            if second:
                rk = [hk0] + [htk[j] for j in (t - 1, t, t + 1) if 0 <= j < NTL]
                P.dma("sp", ht[:, :, :], HT[:, :, t * 128:t * 128 + 130], reads=rk, writes=[ht])
            yield
            la = lab[:, d * 4:(d + 1) * 4]
            be = lab[:, 8 + d * 4:8 + (d + 1) * 4]
            yield
            labc = A("labc", [128, 4, 128])
            P.cp("pool", labc[:, :, :], bc4(la), reads=[lab], writes=[labc])
            psg = PS()
            for h in range(4):
                P.mm(psg, psg[:, h * 128:(h + 1) * 128], labc[:, h, :], C(f"U{d}"), reads=[labc, cst])
            R(labc)
            psc = PS()
            P.mm(psc, psc[:, 0:4], C(f"U{d}"), la, reads=[cst, lab])
            sm = A("sm", [128, 9, 4])
            P.cp("dve", sm[:, 0, :], psc[:, 0:4], reads=[psc], writes=[sm])
            P.ts("dve", sm[:, 8, :], sm[:, 0, :], -1.0, ALU.mult, reads=[sm], writes=[sm])
            li = (63, 127) if d == 0 else (0, 64)
            P.cp("dve", sm[:, 6, :], v3(psg)[:, :, li[0]], reads=[psg], writes=[sm])
            P.cp("dve", sm[:, 7, :], v3(psg)[:, :, li[1]], reads=[psg], writes=[sm])
            Dm = A("Dm", [128, 4, 128])
            DT = A("DT", [128, 4, 128])
            for h in range(4):
                P.act(Dm[:, h, :], psg[:, h * 128:(h + 1) * 128], AF.Relu, reads=[psg, sm], writes=[Dm], scale=1.0, bias=sm[:, 8, h:h + 1])
                P.act(DT[:, h, :], psg[:, h * 128:(h + 1) * 128], AF.Relu, reads=[psg, sm], writes=[DT], scale=-1.0, bias=sm[:, 0, h:h + 1])
            P.act(Dm[:, :, :], Dm[:, :, :], AF.Exp, reads=[Dm], writes=[Dm], scale=-1.0)
            P.act(DT[:, :, :], DT[:, :, :], AF.Exp, reads=[DT], writes=[DT], scale=-1.0)
            m0 = C("mab")[:, 0:1]
            m1 = C("mab")[:, 1:2]
            P.ts("dve", sm[:, 2, :], sm[:, 6, :], m0, ALU.mult, reads=[sm, cst], writes=[sm])
            P.stt("dve", sm[:, 2, :], sm[:, 7, :], m1, sm[:, 2, :], ALU.mult, ALU.add, reads=[sm, cst], writes=[sm])
            P.tt("dve", sm[:, 2, :], sm[:, 2, :], sm[:, 0, :], ALU.subtract, reads=[sm], writes=[sm])
            P.act(sm[:, 2, :], sm[:, 2, :], AF.Exp, reads=[sm], writes=[sm])
            P.ts("dve", sm[:, 3, :], sm[:, 2, :], m0, ALU.mult, reads=[sm, cst], writes=[sm])
            P.ts("dve", sm[:, 4, :], sm[:, 2, :], m1, ALU.mult, reads=[sm, cst], writes=[sm])
            P.act(sm[:, 1, :], sm[:, 0, :], AF.Exp, reads=[sm], writes=[sm])
            P.tt("dve", sm[:, 5, :], sm[:, 1, :], be, ALU.mult, reads=[sm, lab], writes=[sm])
            P.act(sm[:, 6:8, :], sm[:, 6:8, :], AF.Exp, reads=[sm], writes=[sm])
            yield
            psk = PS()
            for h in range(4):
                P.mm(psk, psk[:, h * 128:(h + 1) * 128], qkvT[:, 4 + h, :], qkvT[:, 4 + h, :], reads=[qkvT])
            t2 = A("t2", [128, 4, 128])
            P.tt("dve", t2[:, :, :], Dm[:, :, :], v3(psk), ALU.mult, reads=[Dm, psk], writes=[t2])
            R(Dm)
            P.tt("pool", t2[:, :, :], t2[:, :, :], bc4(be), ALU.mult, reads=[t2, lab], writes=[t2])
            Nn = A("Nn", [128, 4, 128])
            Aoff = A("Aoff", [128, 4, 128])
            P.tt("pool", Nn[:, :, :], t2[:, :, :], mk(f"NMD{d}"), ALU.mult, reads=[t2, cst], writes=[Nn])
            P.tt("dve", Aoff[:, :, :], t2[:, :, :], mk(f"MO{d}"), ALU.mult, reads=[t2, cst], writes=[Aoff])
            R(t2)
            psq = PS()
            for h in range(4):
                P.mm(psq, psq[:, h * 128:(h + 1) * 128], qkvT[:, 4 + h, :], qkvT[:, h, :], reads=[qkvT])
            aqkT = A("aqkT", [128, 4, 128])
            P.tt("dve", aqkT[:, :, :], DT[:, :, :], v3(psq), ALU.mult, reads=[DT, psq], writes=[aqkT])
            R(DT)
            P.tt("pool", aqkT[:, :, :], aqkT[:, :, :], mk(f"MIT{d}"), ALU.mult, reads=[aqkT, cst], writes=[aqkT])
            yield
            RHS = A("RHS", [128, 4, 256])
            P.tt("pool", RHS[:, :, 0:128], knv[:, 1, :, :], bc4(be), ALU.mult, reads=[knv, lab], writes=[RHS])
            P.tt("pool", RHS[:, :, 128:256], knv[:, 0, :, :], bc4(sm[:, 5, :]), ALU.mult, reads=[knv, sm], writes=[RHS])
            kdA = A("kdA", [128, 4, 128])
            kdB = A("kdB", [128, 4, 128])
            P.tt("pool", kdA[:, :, :], knv[:, 0, :, :], bc4(sm[:, 3, :]), ALU.mult, reads=[knv, sm], writes=[kdA])
            P.tt("pool", kdB[:, :, :], knv[:, 0, :, :], bc4(sm[:, 4, :]), ALU.mult, reads=[knv, sm], writes=[kdB])
            R(knv)
            def mm4(lh, rh):
                ps_ = PS()
                for h in range(4):
                    P.mm(ps_, ps_[:, h * 128:(h + 1) * 128], lh[:, h, :], rh[:, h, :], reads=[lh, rh])
                return ps_

            def step2(lh, rh, evac, tr=False):
                for hp in range(2):
                    ps_ = PS()
                    for j in range(2):
                        h = hp * 2 + j
                        if tr:
                            P.tr(ps_, ps_[:, j * 128:(j + 1) * 128], lh(h), ident, reads=[rh, cst])
                        else:
                            P.mm(ps_, ps_[:, j * 128:(j + 1) * 128], lh[:, h, :], rh[:, h, :], reads=[lh, rh])
                    evac(hp, slice(hp * 2, hp * 2 + 2), ps_[:, 0:256].rearrange("p (j n) -> p j n", j=2), ps_)

            def ev_copy(dst, eng_pair=("act", "dve")):
                return lambda hp, hs, pv, ps_: P.cp(eng_pair[hp], dst[:, hs, :], pv, reads=[ps_], writes=[dst])

            def ev_add(dst, other):
                return lambda hp, hs, pv, ps_: P.tt("dve", dst[:, hs, :], pv, other[:, hs, :], ALU.add, reads=[ps_, other], writes=[dst])

            NTt = A("NTt", [128, 4, 128])
            TTa = A("TTa", [128, 4, 128])
            step2(lambda h: Nn[:, h, :], Nn, ev_copy(NTt, ("act", "act")), tr=True)
            P.tt("pool", TTa[:, :, :], NTt[:, :, :], mk("ident"), ALU.add, reads=[NTt, cst], writes=[TTa])
            yield
            Pa = A("Pa", [128, 4, 128])
            PaT = A("PaT", [128, 4, 128])
            step2(NTt, Nn, ev_copy(Pa))
            step2(Nn, NTt, ev_copy(PaT, ("dve", "act")))
            R(Nn); R(NTt)
            yield
            TTb = A("TTb", [128, 4, 128])
            step2(Pa, TTa, ev_add(TTb, TTa))
            Pb = A("Pb", [128, 4, 128])
            PbT = A("PbT", [128, 4, 128])
            step2(PaT, Pa, ev_copy(Pb))
            yield
            step2(Pa, PaT, ev_copy(PbT, ("dve", "act")))
            R(PaT)
            step2(Pb, TTb, ev_add(TTa, TTb))
            yield
            step2(PbT, Pb, ev_copy(Pa))
            R(PbT)
            step2(Pa, TTa, ev_add(TTb, TTa))
            R(TTa); R(Pa)
            yield
            step2(Aoff, TTb, lambda hp, hs, pv, ps_: P.act(Pb[:, hs, :], pv, AF.Copy, reads=[ps_], writes=[Pb], scale=-1.0))
            R(Aoff)
            nGT = Pb
            Xs = [A("Xa", [128, 4, 256]), A("Xb", [128, 4, 256])]

            XS1 = A("XS1", [128, 4, 256])

            def xsweep(dst, src):
                for hp in range(2):
                    ps_ = PS()
                    for j in range(2):
                        h = hp * 2 + j
                        if src is None:
                            P.mm(ps_, ps_[:, j * 256:(j + 1) * 256], TTb[:, h, :], RHS[:, h, :], reads=[TTb, RHS])
                        else:
                            P.mm(ps_, ps_[:, j * 256:(j + 1) * 256], nGT[:, h, :], src[:, h, :], reads=[nGT, src])
                    pv = ps_[:, :].rearrange("p (j n) -> p j n", j=2)
                    hsl = slice(hp * 2, hp * 2 + 2)
                    if src is None:
                        P.cp("act" if hp == 0 else "dve", dst[:, hsl, :], pv, reads=[ps_], writes=[dst])
                    else:
                        P.tt("dve", dst[:, hsl, :], pv, XS1[:, hsl, :], ALU.add, reads=[ps_, XS1], writes=[dst])
            xsweep(XS1, None)
            yield
            xsweep(Xs[0], XS1)
            yield
            xsweep(Xs[1], Xs[0])
            yield
            xsweep(Xs[0], Xs[1])
            Xs = [Xs[1], Xs[0]]
            Xc = Xs[1]
            R(TTb); R(RHS); R(Xs[0]); R(nGT); R(XS1)
            wT = A("wT", [128, 4, 128])
            step2(lambda h: Xc[:, h, 128:256], Xc, ev_copy(wT, ("act", "act")), tr=True)
            yield
            for c in ((0, 1) if d == 0 else (1, 0)):
                r0, r1 = c * 64, (c + 1) * 64
                ps = mm4(wT, S)
                P.tt("dve", u[r0:r1, :, :], Xc[r0:r1, :, 0:128], v3(ps)[r0:r1], ALU.subtract, reads=[Xc, ps], writes=[u])
                ps2 = mm4(qkvT, S)
                P.tt("dve", oS[r0:r1, :, :], v3(ps2)[r0:r1], sm[r0:r1, 1, :].unsqueeze(2).to_broadcast([64, 4, 128]), ALU.mult,
                     reads=[ps2, sm], writes=[oS])
                yield
                ps3 = mm4(kdA if c == 0 else kdB, u)
                for h in range(4):
                    P.stt("dve", S[:, h, :], S[:, h, :], sm[:, 6 + c, h:h + 1], ps3[:, h * 128:(h + 1) * 128], ALU.mult, ALU.add,
                          reads=[S, sm, ps3], writes=[S])
                yield
            ps = mm4(aqkT, u)
            ot = A("ot", [128, 4, 128])
            P.tt("dve", ot[:, :, :], oS[:, :, :], v3(ps), ALU.add, reads=[oS, ps], writes=[ot])
            for b_ in (qkvT, aqkT, kdA, kdB, Xc, wT, sm):
                R(b_)
            ofk = Key(f"OF{gt_}")
            if not second:
                P.dma("act", OF[gt_ * 128:(gt_ + 1) * 128, :], ot[:, :, :].rearrange("p h v -> p (h v)"), reads=[ot], writes=[ofk])
                R(ot); R(lab)
                return
            yield
            of = A("of", [128, 4, 128])
            P.dma("sp", of[:, :, :], OF[gt_ * 128:(gt_ + 1) * 128, :].rearrange("p (h v) -> p h v", h=4), reads=[ofk], writes=[of])
            P.tt("pool", ot[:, :, :], ot[:, :, :], of[:, :, :], ALU.add, reads=[ot, of], writes=[ot])
            R(of)
            osq = A("osq", [128, 4, 128])
            P.act(osq[:, :, :], ot[:, :, :], AF.Square, reads=[ot], writes=[osq])
            rs = A("rs", [128, 4])
            P.op("dve", (lambda o_, i_: lambda e: e.tensor_reduce(out=o_, in_=i_, axis=AX.X, op=ALU.add))(rs[:, :], osq[:, :, :]),
                 reads=[osq], writes=[rs])
            R(osq)
            P.act(rs[:, :], rs[:, :], AF.Ln, reads=[rs, eps_c], writes=[rs], scale=1.0 / 128, bias=eps_c[:, :])
            P.act(rs[:, :], rs[:, :], AF.Exp, reads=[rs], writes=[rs], scale=-0.5)
            P.tt("dve", ot[:, :, :], ot[:, :, :], bc4(rs[:, :]), ALU.mult, reads=[ot, rs], writes=[ot])
            P.tt("pool", ot[:, :, :], ot[:, :, :], go_bc[:, :].unsqueeze(1).to_broadcast([128, 4, 128]), ALU.mult, reads=[ot, go_bc], writes=[ot])
            R(rs)
            psz = PS()
            for kc in range(8):
                P.mm(psz, psz[:, :], ht[:, kc, 1:129], winb[:, kc, 1536:2048], reads=[ht, winb], start=(kc == 0), stop=(kc == 7))
            sz = A("sz", [128, 512])
            P.act(sz[:, :], psz[:, :], AF.Silu, reads=[psz], writes=[sz])
            mixb = A("mixb", [128, 512], BF16)
            P.tt("dve", mixb[:, :], ot[:, :, :].rearrange("p h v -> p (h v)"), sz[:, :], ALU.mult, reads=[ot, sz], writes=[mixb])
            R(sz); R(ot); R(ht); R(lab)
            yield
            pst = PS()
            pstb = pst.t[:, :].bitcast(BF16)
            for j in range(4):
                P.tr(pst, pstb[:, j * 128:(j + 1) * 128], mixb[:, j * 128:(j + 1) * 128], CB("ident"), reads=[mixb, cbf])
            mixT = A("mixT", [128, 8, 128], BF16)
            P.cp("act", mixT[:, 0:4, :], pstb[:, 0:512].rearrange("p (k t) -> p k t", k=4), reads=[pst], writes=[mixT])
            R(mixb)
            P.dma("sp", mixT[:, 4:8, :], YTD[:, :, t * 128:(t + 1) * 128], reads=[ytk], writes=[mixT])
            xt = A("xt", [128, D])
            P.dma("sp", xt[:, :], xr[t * 128:(t + 1) * 128, :], writes=[xt])
            x1 = A("x1", [128, D])
            yield
            for half in range(2):
                pso = PS()
                for kc in range(8):
                    P.mm(pso, pso[:, :], mixT[:, kc, :], woutb[:, kc, half * 512:(half + 1) * 512], reads=[mixT, woutb], start=(kc == 0), stop=(kc == 7))
                P.tt("dve", x1[:, half * 512:(half + 1) * 512], pso[:, :], gates[:, cond, 0, half * 512:(half + 1) * 512], ALU.mult,
                     reads=[pso, gates], writes=[x1])
            P.tt("pool", x1[:, :], x1[:, :], xt[:, :], ALU.add, reads=[x1, xt], writes=[x1])
            P.dma("pool", X1[gt_ * 128:(gt_ + 1) * 128, :], x1[:, :], reads=[x1], writes=[Key(f"X1{gt_}")])
            R(mixT); R(xt); R(x1)

        order = {0: list(range(NTL)), 1: list(range(NTL - 1, -1, -1))}
        pos = {0: 0, 1: 0}
        done = {0: 0, 1: 0}
        cur = {0: None, 1: None}
        ycnt = {0: 0, 1: 0}

        def try_start(d):
            i = pos[d]
            if cur[d] is not None or i >= NTL:
                return
            t = order[d][i]
            second = (t >= NTL // 2) if d == 0 else (t < NTL // 2)
            if second and done[1 - d] < NTL - i:
                return
            cur[d] = tile_gen(streams[d], t, d, second)
            pos[d] += 1

        def step(d):
            try_start(d)
            if cur[d] is None:
                return False
            try:
                next(cur[d])
                ycnt[d] += 1
            except StopIteration:
                cur[d] = None
                done[d] += 1
            return True

        while done[0] < NTL or done[1] < NTL:
            p0 = step(0)
            lead = ycnt[0] - ycnt[1]
            if (not p0) or lead >= SKEW or done[0] >= NTL:
                p1 = step(1)
            else:
                p1 = False
            assert p0 or p1 or lead < SKEW, "scheduler stuck"
            if not p0 and not p1:
                assert step(1), "scheduler deadlock"
        if grp == 0:
            for d in (0, 1):
                P.dma("sp", (nsf if d == 0 else nsb)[si].rearrange("h k v -> k h v"), Sst[d][:, :, :], reads=[Sst[d]], is_output=True)
        tokbase += N

    if dbg:
        dbg_out["x1"] = dt("dbg_x1", [256, D], F32, kind="ExternalOutput").ap()
        P.arena_reset(SCR0)
        dtile = P.aring("xtd", [128, D], F32, bufs=1)
        for b in range(2):
            P.dma("sp", dtile[:, :], X1[b * 128:(b + 1) * 128, :], reads=[Key(f"X1{b}")], writes=[dtile])
            P.dma("sp", dbg_out["x1"][b * 128:(b + 1) * 128, :], dtile[:, :], reads=[dtile], is_output=True)
    if stop_after in ("seq0", "seq4", "ab"):
        P.emit()
        return nc

    P.arena_reset(0)
    wgub = P.abuf("wgub", [128, 8, 2 * DFF], BF16)
    wdnb = P.abuf("wdnb", [128, 22, D], BF16)
    gfin_bc = P.abuf("gfin", [128, D])
    P.dma("sp", gfin_bc[:, :], g_final.partition_broadcast(128), writes=[gfin_bc])
    stg = [P.abuf(f"stg{i}", [128, 1408]) for i in range(2)]
    n = 0
    for kc in range(8):
        for q in range(4):
            st = stg[n % 2]
            P.dma(dq[n % 2], st[:, :], w_gu[kc * 128:(kc + 1) * 128, q * 1408:(q + 1) * 1408], writes=[st])
            P.cp(("dve", "pool", "act")[n % 3], wgub[:, kc, q * 1408:(q + 1) * 1408], st[:, :], reads=[st], writes=[wgub])
            n += 1
    for fc in range(22):
        st = stg[n % 2]
        P.dma(dq[n % 2], st[:, 0:D], w_down[fc * 128:(fc + 1) * 128, :], writes=[st])
        P.cp(("dve", "pool", "act")[n % 3], wdnb[:, fc, :], st[:, 0:D], reads=[st], writes=[wdnb])
        n += 1
    PC0 = P.aoff
    P.arena_reset(PC0 - 2 * 2816)
    for T in range(20):
        cond = 0 if T < 4 else 1
        x1 = P.aring("x1c", [128, 2, D], F32, bufs=1)
        h2 = P.aring("h2", [128, 8, 256], BF16, bufs=1)
        for blk in range(2):
            gt_ = 2 * T + blk
            P.dma("sp", x1[:, blk, :], X1[gt_ * 128:(gt_ + 1) * 128, :], reads=[Key(f"X1{gt_}")], writes=[x1])
            junk = P.aring("junkc", [128, D], F32, bufs=1)
            ss = P.ring("ss", [128, 1], F32, bufs=2)
            P.act(junk[:, :], x1[:, blk, :], AF.Square, reads=[x1], writes=[junk, ss], accum_out=ss[:, :])
            P.act(ss[:, :], ss[:, :], AF.Ln, reads=[ss, eps_c], writes=[ss], scale=1.0 / D, bias=eps_c[:, :])
            P.act(ss[:, :], ss[:, :], AF.Exp, reads=[ss], writes=[ss], scale=-0.5)
            xnb = P.aring("xnbc", [128, D], BF16, bufs=2)
            P.act(xnb[:, :], x1[:, blk, :], AF.Copy, reads=[x1, ss], writes=[xnb], scale=ss[:, :])
            ps = P.ps()
            psb = ps.t[:, :].bitcast(BF16)
            for kc in range(8):
                P.tr(ps, psb[:, kc * 128:(kc + 1) * 128], xnb[:, kc * 128:(kc + 1) * 128], CB("ident"), reads=[xnb, cbf])
            tmp = P.aring("httmpc", [128, 8, 128], F32, bufs=1)
            P.tt("dve", tmp[:, :, :], psb.rearrange("p (k t) -> p k t", k=8),
                 s2T[:, :, cond].unsqueeze(2).to_broadcast([128, 8, 128]), ALU.mult, reads=[ps, s2T], writes=[tmp])
            P.tt("pool", h2[:, :, blk * 128:(blk + 1) * 128], tmp[:, :, :], sh2T[:, :, cond].unsqueeze(2).to_broadcast([128, 8, 128]), ALU.add,
                 reads=[tmp, sh2T], writes=[h2])
        actT = P.aring("actT", [128, 22, 256], BF16, bufs=1)
        for fc in range(22):
            ps = P.ps()
            for kc in range(8):
                P.mm(ps, ps[:, 0:256], wgub[:, kc, fc * 128:(fc + 1) * 128], h2[:, kc, :], reads=[wgub, h2], start=(kc == 0), stop=(kc == 7))
            for kc in range(8):
                P.mm(ps, ps[:, 256:512], wgub[:, kc, DFF + fc * 128:DFF + (fc + 1) * 128], h2[:, kc, :], reads=[wgub, h2], start=(kc == 0), stop=(kc == 7))
            sg = P.aring("sg", [128, 256], F32, bufs=2)
            P.act(sg[:, :], ps[:, 0:256], AF.Silu, reads=[ps], writes=[sg])
            P.tt("dve", actT[:, fc, :], sg[:, :], ps[:, 256:512], ALU.mult, reads=[sg, ps], writes=[actT])
        for blk in range(2):
            gt_ = 2 * T + blk
            yt = P.aring("yt", [128, D], F32, bufs=2)
            for half in range(2):
                ps = P.ps()
                for fc in range(22):
                    P.mm(ps, ps[:, :], actT[:, fc, blk * 128:(blk + 1) * 128], wdnb[:, fc, half * 512:(half + 1) * 512], reads=[actT, wdnb], start=(fc == 0), stop=(fc == 21))
                P.tt("dve", yt[:, half * 512:(half + 1) * 512], ps[:, :], gates[:, cond, 1, half * 512:(half + 1) * 512], ALU.mult,
                     reads=[ps, gates], writes=[yt])
            P.tt("pool", yt[:, :], yt[:, :], x1[:, blk, :], ALU.add, reads=[yt, x1], writes=[yt])
            junk = P.aring("junkc", [128, D], F32, bufs=1)
            ss2 = P.ring("ss", [128, 1], F32, bufs=2)
            P.act(junk[:, :], yt[:, :], AF.Square, reads=[yt], writes=[junk, ss2], accum_out=ss2[:, :])
            P.act(ss2[:, :], ss2[:, :], AF.Ln, reads=[ss2, eps_c], writes=[ss2], scale=1.0 / D, bias=eps_c[:, :])
            P.act(ss2[:, :], ss2[:, :], AF.Exp, reads=[ss2], writes=[ss2], scale=-0.5)
            P.stt("dve", yt[:, :], yt[:, :], ss2[:, :], gfin_bc[:, :], ALU.mult, ALU.mult, reads=[yt, ss2, gfin_bc], writes=[yt])
            dst = yout[0][gt_ * 128:(gt_ + 1) * 128, :] if gt_ < 8 else yout[1][(gt_ - 8) * 128:(gt_ - 7) * 128, :]
            P.dma("pool", dst, yt[:, :], reads=[yt], is_output=True)
    P.emit()
    return nc


def _in_maps(inputs):
    g = lambda k: np.ascontiguousarray(np.asarray(inputs[k], dtype=np.float32))
    maps = []
    for c in range(8):
        m = {
            "xp": g("x_prompt")[4 * c:4 * c + 4].reshape(1024, D),
            "xs": g("x_sample")[c],
            "cvec": np.stack([g("c_ctx"), g("c")[c]], 0),
            "s0f": g("state_dn_fwd")[c, 0], "s0b": g("state_dn_bwd")[c, 0],
            "w_ada": g("w_ada")[0], "b_ada": g("b_ada")[0], "g_mix": g("g_mix")[0], "w_in": g("w_in")[0],
            "w_conv": g("w_conv")[0], "a_log": g("a_log")[0].reshape(8), "dt_bias": g("dt_bias")[0].reshape(8),
            "g_o": g("g_o")[0], "w_fno": g("w_fno")[0], "w_out": g("w_out")[0], "g_ffn": g("g_ffn")[0],
            "w_gu": g("w_gu")[0], "w_down": g("w_down")[0], "g_final": g("g_final"), "cst": _CST,
        }
        maps.append({k: np.ascontiguousarray(v) for k, v in m.items()})
    return maps


def kernel(**inputs):
    nc = build_nc()
    res = run_bass_kernel_spmd(nc, _in_maps(inputs), core_ids=list(range(8)))
    r = res.results
    yp = np.concatenate([r[c]["yp"].reshape(4, 256, D) for c in range(8)], 0)
    ys = np.stack([r[c]["ys"] for c in range(8)], 0)
    f = np.concatenate([r[c]["nsf"] for c in range(8)], 0)[:, None]
    b = np.concatenate([r[c]["nsb"] for c in range(8)], 0)[:, None]
    return (yp.astype(np.float32), ys.astype(np.float32), f.astype(np.float32), b.astype(np.float32))
```

```python
import numpy as np
import ml_dtypes
import concourse.bass as bass
import concourse.mybir as mybir
from concourse.bass_utils import run_bass_kernel_spmd

F32 = mybir.dt.float32
BF16 = mybir.dt.bfloat16
AF = mybir.ActivationFunctionType
ALU = mybir.AluOpType
AX = mybir.AxisListType

ENGS = ("pe", "act", "dve", "pool", "sp")
D = 1024
NCOL = 2576
DFF = 2816
EPS = 1e-6
DBG_TILES = None
SKEW = 8
DBG_STAGE = None
DBG_DIRS = (0, 1)


class Buf:
    def __init__(self, t, key):
        self.t = t
        self.key = key

    def __getitem__(self, idx):
        return self.t[idx]


class Ring:
    def __init__(self, bufs):
        self.bufs = bufs
        self.i = 0

    def get(self):
        b = self.bufs[self.i]
        self.i = (self.i + 1) % len(self.bufs)
        return b


class Stream:
    def __init__(self, P, sid):
        self.P = P
        self.sid = sid
        self.free = {}

    def alloc(self, name, shape, dtype=F32):
        P = self.P
        n = int(np.prod(shape[1:])) * (2 if dtype == F32 else 1)
        n = (n + 15) // 16 * 16
        if n >= 200:
            n = (n + 255) // 256 * 256
        lst = self.free.setdefault(n, [])
        if lst:
            off, key = lst.pop()
        else:
            off = (P.aoff + 15) // 16 * 16
            assert off + n <= P.alim, (name, off, n, P.alim)
            P.aoff = off + n
            P.nbuf += 1
            key = f"slot{self.sid}_{P.nbuf}"
        m = int(np.prod(shape[1:])) * (2 if dtype == F32 else 1)
        v = P.arena[:, off:off + m]
        if dtype == F32:
            v = v.bitcast(F32)
        if len(shape) == 3:
            v = v.rearrange("p (a b) -> p a b", a=shape[1])
        elif len(shape) == 4:
            v = v.rearrange("p (a b c) -> p a b c", a=shape[1], b=shape[2])
        b = Buf(v, key)
        b.slot = (n, off, key)
        return b

    def release(self, b):
        n, off, key = b.slot
        self.free[n].append((off, key))


class Prog:
    def __init__(self, nc, n_dma_sems=8):
        self.nc = nc
        self.ops = {e: [] for e in ENGS}
        self.cnt = {e: 0 for e in ENGS}
        self.sem = {e: nc.alloc_semaphore("sem_" + e) for e in ENGS}
        self.dq = {}
        for q in ("sp", "act", "pool"):
            self.dq[q] = dict(sems=[nc.alloc_semaphore(f"dsem_{q}{i}") for i in range(n_dma_sems)],
                              val=[0] * n_dma_sems, nxt=0)
        self.waited = {e: {} for e in ENGS}
        self.lastw = {}
        self.readers = {}
        self.out_events = []
        self.rings = {}
        self.arena_rings = []
        self.nbuf = 0

    def buf(self, name, shape, dtype=F32):
        self.nbuf += 1
        return Buf(self.nc.alloc_sbuf_tensor(f"{name}_{self.nbuf}", list(shape), dtype), f"{name}_{self.nbuf}")

    def ring(self, name, shape, dtype=F32, bufs=2):
        if name not in self.rings:
            self.rings[name] = Ring([self.buf(name, shape, dtype) for _ in range(bufs)])
        return self.rings[name].get()

    def init_arena(self, nelem):
        self.arena = self.nc.alloc_sbuf_tensor("arena", [128, nelem], BF16)
        self.aoff = 0
        self.alim = nelem

    def abuf(self, name, shape, dtype=F32):
        n = int(np.prod(shape[1:])) * (2 if dtype == F32 else 1)
        off = (self.aoff + 15) // 16 * 16
        assert off + n <= self.alim, (name, off, n, self.alim)
        self.aoff = off + n
        v = self.arena[:, off:off + n]
        if dtype == F32:
            v = v.bitcast(F32)
        if len(shape) == 3:
            v = v.rearrange("p (a b) -> p a b", a=shape[1])
        elif len(shape) == 4:
            v = v.rearrange("p (a b c) -> p a b c", a=shape[1], b=shape[2])
        self.nbuf += 1
        return Buf(v, f"{name}_{self.nbuf}")

    def aring(self, name, shape, dtype=F32, bufs=1):
        if name not in self.rings:
            self.rings[name] = Ring([self.abuf(name, shape, dtype) for _ in range(bufs)])
            self.arena_rings.append(name)
        return self.rings[name].get()

    def arena_reset(self, off):
        self.barrier()
        for nme in self.arena_rings:
            self.rings.pop(nme, None)
        self.arena_rings = []
        self.aoff = off

    def init_psum(self):
        self.psr = Ring([Buf(self.nc.alloc_psum_tensor(f"psb{i}", [128, 512], F32), f"psb{i}") for i in range(8)])
        self.psub = [Ring(self.psr.bufs[0:4]), Ring(self.psr.bufs[4:8])]

    def ps(self, sid=None):
        if sid is None:
            return self.psr.get()
        return self.psub[sid].get()

    def _need(self, eng, ev, waits):
        if ev is None:
            return
        sem, val, owner = ev
        if owner == "pe" and eng == "pe":
            return
        sid = id(sem)
        if self.waited[eng].get(sid, 0) >= val:
            return
        cur = waits.get(sid)
        if cur is None or cur[1] < val:
            waits[sid] = (sem, val)

    def _deps(self, eng, reads, writes):
        waits = {}
        for b in reads:
            self._need(eng, self.lastw.get(b.key), waits)
            if b.key.startswith("psb"):
                for ev in self.readers.get(b.key, ()):
                    if ev[2] != eng:
                        self._need(eng, ev, waits)
        for b in writes:
            self._need(eng, self.lastw.get(b.key), waits)
            for ev in self.readers.get(b.key, ()):
                self._need(eng, ev, waits)
        for sid, (sem, val) in waits.items():
            self.waited[eng][sid] = val
        return list(waits.values())

    def _commit(self, ev, reads, writes):
        for b in reads:
            self.readers.setdefault(b.key, []).append(ev)
        for b in writes:
            self.lastw[b.key] = ev
            self.readers[b.key] = []

    def op(self, eng, fn, reads=(), writes=()):
        waits = self._deps(eng, reads, writes)
        self.cnt[eng] += 1
        ev = (self.sem[eng], self.cnt[eng], eng)
        self.ops[eng].append((fn, waits, (self.sem[eng], 1)))
        self._commit(ev, reads, writes)
        return ev

    def dma(self, q, out, in_, reads=(), writes=(), is_output=False, slow=False):
        d = self.dq[q]
        i = d["nxt"]
        d["nxt"] = (i + 1) % len(d["sems"])
        sem = d["sems"][i]
        waits = self._deps(q, reads, writes)
        prev = d["val"][i]
        if prev > 0 and self.waited[q].get(id(sem), 0) < prev:
            waits.append((sem, prev))
            self.waited[q][id(sem)] = prev
        d["val"][i] = prev + 16
        ev = (sem, prev + 16, None)
        if slow:
            fn = lambda e: e.dma_start(out=out, in_=in_, allow_slow_non_contiguous=True)
        else:
            fn = lambda e: e.dma_start(out=out, in_=in_)
        self.ops[q].append((fn, waits, (sem, 16)))
        self._commit(ev, reads, writes)
        if is_output:
            self.out_events.append(ev)
        return ev

    def barrier(self):
        evs = [(self.sem[e], self.cnt[e]) for e in ENGS if self.cnt[e] > 0]
        for q in self.dq.values():
            for s, v in zip(q["sems"], q["val"]):
                if v > 0:
                    evs.append((s, v))
        for e in ENGS:
            waits = []
            for s, v in evs:
                if self.waited[e].get(id(s), 0) < v:
                    waits.append((s, v))
                    self.waited[e][id(s)] = v
            if waits:
                self.cnt[e] += 1
                self.ops[e].append((lambda eng: eng.nop(), waits, (self.sem[e], 1)))

    def mm(self, ps, out, lhsT, rhs, reads, start=True, stop=True):
        return self.op("pe", lambda e: e.matmul(out, lhsT=lhsT, rhs=rhs, start=start, stop=stop),
                       reads=reads, writes=[ps])

    def tr(self, ps, out, in_, ident, reads):
        return self.op("pe", lambda e: e.transpose(out, in_, ident), reads=reads, writes=[ps])

    def act(self, out, in_, func, reads, writes, scale=1.0, bias=0.0, accum_out=None):
        return self.op("act", lambda e: e.activation(out=out, in_=in_, func=func, bias=bias, scale=scale,
                                                     accum_out=accum_out), reads=reads, writes=writes)

    def ts(self, eng, out, in0, s1, op0, reads, writes, s2=None, op1=None):
        if op1 is None:
            return self.op(eng, lambda e: e.tensor_scalar(out=out, in0=in0, scalar1=s1, scalar2=None, op0=op0),
                           reads=reads, writes=writes)
        return self.op(eng, lambda e: e.tensor_scalar(out=out, in0=in0, scalar1=s1, scalar2=s2, op0=op0, op1=op1),
                       reads=reads, writes=writes)

    def tt(self, eng, out, in0, in1, op, reads, writes):
        return self.op(eng, lambda e: e.tensor_tensor(out=out, in0=in0, in1=in1, op=op), reads=reads, writes=writes)

    def stt(self, eng, out, in0, scalar, in1, op0, op1, reads, writes):
        return self.op(eng, lambda e: e.scalar_tensor_tensor(out=out, in0=in0, scalar=scalar, in1=in1, op0=op0, op1=op1),
                       reads=reads, writes=writes)

    def cp(self, eng, out, in_, reads, writes):
        if eng == "act":
            return self.act(out, in_, AF.Copy, reads, writes)
        return self.op(eng, lambda e: e.tensor_copy(out=out, in_=in_), reads=reads, writes=writes)

    def memset(self, eng, b, ap, val):
        return self.op(eng, lambda e: e.memset(ap, val), writes=[b])

    def emit(self):
        nc = self.nc
        fin = {}
        for sem, val, _ in self.out_events:
            if id(sem) not in fin or fin[id(sem)][1] < val:
                fin[id(sem)] = (sem, val)
        final_waits = list(fin.values())
        prog = self

        def run(engname, eng):
            for fn, waits, inc in prog.ops[engname]:
                for sem, val in waits:
                    eng.wait_ge(sem, val)
                fn(eng).then_inc(inc[0], inc[1])
            if engname == "sp":
                for sem, val in final_waits:
                    eng.wait_ge(sem, val)

        with nc.Block() as block:
            @block.tensor
            def _(e):
                run("pe", e)

            @block.scalar
            def _(e):
                run("act", e)

            @block.vector
            def _(e):
                run("dve", e)

            @block.gpsimd
            def _(e):
                run("pool", e)

            @block.sync
            def _(e):
                run("sp", e)


def _const_tables():
    idx = np.arange(128)
    same64 = (idx[:, None] // 64) == (idx[None, :] // 64)
    same16 = (idx[:, None] // 16) == (idx[None, :] // 16)
    cols = {}
    cols["ident"] = np.eye(128)
    cols["ones"] = np.ones((128, 128))
    for d in (0, 1):
        if d == 0:
            le = idx[:, None] <= idx[None, :]
            strict = idx[:, None] > idx[None, :]
        else:
            le = idx[:, None] >= idx[None, :]
            strict = idx[:, None] < idx[None, :]
        cols[f"U{d}"] = (le & same64) * 1.0
        cols[f"NMD{d}"] = -1.0 * (strict & same16)
        cols[f"MO{d}"] = 1.0 * (strict & same64 & ~same16)
        incl_ij = (strict | np.eye(128, dtype=bool)) & same64
        cols[f"MIT{d}"] = incl_ij.T * 1.0
    mab = np.zeros((128, 128))
    mab[:64, 0] = 1.0
    mab[64:, 1] = 1.0
    cols["mab"] = mab
    ang = 2 * np.pi * np.outer(idx, idx) / 128.0
    cols["CC"] = np.cos(ang)
    cols["nSC"] = -np.sin(ang)
    cols["SC"] = np.sin(ang)
    n = np.arange(256)
    angp = 2 * np.pi * np.outer(n, n) / 256.0
    sp = (256 * 128) ** -0.5
    cn = np.cos(angp) * sp
    sn = np.sin(angp) * sp
    for c in range(2):
        cols[f"CN{c}a"] = cn[c * 128:(c + 1) * 128, 0:128]
        cols[f"CN{c}b"] = cn[c * 128:(c + 1) * 128, 128:256]
        cols[f"SN{c}a"] = sn[c * 128:(c + 1) * 128, 0:128]
        cols[f"SN{c}b"] = sn[c * 128:(c + 1) * 128, 128:256]
    r = np.arange(64)
    a64 = 2 * np.pi * np.outer(r, r) / 64.0
    z = np.zeros((64, 64))
    bd = lambda m: np.block([[m, z], [z, m]])
    ss = (4096 * 128) ** -0.5
    cols["BDC"] = bd(np.cos(a64))
    cols["BDS"] = bd(np.sin(a64))
    cols["BDCs"] = bd(np.cos(a64)) * ss
    cols["nBDSs"] = -bd(np.sin(a64)) * ss
    names = list(cols.keys())
    arr = np.concatenate([cols[k] for k in names], axis=1).astype(np.float32)
    off = {k: i * 128 for i, k in enumerate(names)}
    return arr, off


_CST, _COFF = _const_tables()
NCST = _CST.shape[1]


def build_nc(stop_after=None, dbg=False):
    nc = bass.Bass("TRN2", target_bir_lowering=False)
    dt = nc.dram_tensor
    xin = {0: dt("xp", [1024, D], F32, kind="ExternalInput").ap(),
           1: dt("xs", [4096, D], F32, kind="ExternalInput").ap()}
    cvec = dt("cvec", [2, D], F32, kind="ExternalInput").ap()
    s0f = dt("s0f", [4, 128, 128], F32, kind="ExternalInput").ap()
    s0b = dt("s0b", [4, 128, 128], F32, kind="ExternalInput").ap()
    w_ada = dt("w_ada", [D, 6 * D], F32, kind="ExternalInput").ap()
    b_ada = dt("b_ada", [6 * D], F32, kind="ExternalInput").ap()
    g_mix = dt("g_mix", [D], F32, kind="ExternalInput").ap()
    w_in = dt("w_in", [D, NCOL], F32, kind="ExternalInput").ap()
    w_conv = dt("w_conv", [3, 1536], F32, kind="ExternalInput").ap()
    a_log = dt("a_log", [8], F32, kind="ExternalInput").ap()
    dt_bias = dt("dt_bias", [8], F32, kind="ExternalInput").ap()
    g_o = dt("g_o", [128], F32, kind="ExternalInput").ap()
    w_fno = dt("w_fno", [4, 128, 128], F32, kind="ExternalInput").ap()
    w_out = dt("w_out", [D, D], F32, kind="ExternalInput").ap()
    g_ffn = dt("g_ffn", [D], F32, kind="ExternalInput").ap()
    w_gu = dt("w_gu", [D, 2 * DFF], F32, kind="ExternalInput").ap()
    w_down = dt("w_down", [DFF, D], F32, kind="ExternalInput").ap()
    g_final = dt("g_final", [D], F32, kind="ExternalInput").ap()
    cst_d = dt("cst", [128, NCST], F32, kind="ExternalInput").ap()
    yout = {0: dt("yp", [1024, D], F32, kind="ExternalOutput").ap(),
            1: dt("ys", [4096, D], F32, kind="ExternalOutput").ap()}
    nsf = dt("nsf", [4, 4, 128, 128], F32, kind="ExternalOutput").ap()
    nsb = dt("nsb", [4, 4, 128, 128], F32, kind="ExternalOutput").ap()
    OF = dt("scr_of", [5120, 512], F32).ap()
    X1 = dt("scr_x1", [5120, D], F32).ap()
    SQ = dt("scr_sq", [40, 128, 1536], F32).ap()
    SK = dt("scr_sk", [40, 128, 1024], F32).ap()
    SL = dt("scr_sl", [40, 128, 16], F32).ap()
    dbg_out = {}

    P = Prog(nc)
    P.init_psum()
    P.init_arena(95000)

    cst = P.abuf("cst", [128, NCST])
    P.dma("sp", cst[:, :], cst_d, writes=[cst])
    C = lambda name, w=128: cst[:, _COFF[name]:_COFF[name] + w]
    ident = C("ident")
    ones = C("ones")
    cbf = P.buf("cbf", [128, 6 * 128], BF16)
    BFOFF = {}
    for i, nm in enumerate(["ident", "BDC", "BDS", "BDCs", "nBDSs"]):
        P.cp("dve", cbf[:, i * 128:(i + 1) * 128], C(nm), reads=[cst], writes=[cbf])
        BFOFF[nm] = i * 128
    CB = lambda nm: cbf[:, BFOFF[nm]:BFOFF[nm] + 128]
    cnbf = P.abuf("cnbf", [128, 1024], BF16)
    o = _COFF["CN0a"]
    P.cp("dve", cnbf[:, :], cst[:, o:o + 1024], reads=[cst], writes=[cnbf])
    AB0 = P.aoff
    CNT = lambda c: cnbf[:, c * 512:c * 512 + 256]
    SNT = lambda c: cnbf[:, c * 512 + 256:c * 512 + 512]

    gmixT = P.buf("gmixT", [128, 8])
    gffnT = P.buf("gffnT", [128, 8])
    P.dma("sp", gmixT[:, :], g_mix.rearrange("(c p) -> p c", p=128), writes=[gmixT], slow=True)
    P.dma("sp", gffnT[:, :], g_ffn.rearrange("(c p) -> p c", p=128), writes=[gffnT], slow=True)
    wconvT = P.buf("wconvT", [128, 3, 12])
    for k in range(3):
        P.dma("sp", wconvT[:, k, :], w_conv[k].rearrange("(c p) -> p c", p=128), writes=[wconvT], slow=True)
    badaT = P.buf("badaT", [128, 48])
    P.dma("sp", badaT[:, :], b_ada.rearrange("(c p) -> p c", p=128), writes=[badaT], slow=True)
    cT = P.buf("cT", [128, 2, 8])
    for t in range(2):
        P.dma("sp", cT[:, t, :], cvec[t].rearrange("(c p) -> p c", p=128), writes=[cT], slow=True)
    alog_bc = P.buf("alog", [128, 8])
    dtb_bc = P.buf("dtb", [128, 8])
    P.dma("sp", alog_bc[:, :], a_log.partition_broadcast(128), writes=[alog_bc])
    P.dma("sp", dtb_bc[:, :], dt_bias.partition_broadcast(128), writes=[dtb_bc])
    go_bc = P.buf("go", [128, 128])
    P.dma("sp", go_bc[:, :], g_o.partition_broadcast(128), writes=[go_bc])
    bgate = P.abuf("bgate", [128, 2, D])
    P.dma("sp", bgate[:, 0, :], b_ada[2 * D:3 * D].partition_broadcast(128), writes=[bgate])
    P.dma("sp", bgate[:, 1, :], b_ada[5 * D:6 * D].partition_broadcast(128), writes=[bgate])
    eps_c = P.buf("epsc", [128, 1])
    P.memset("dve", eps_c, eps_c[:, :], EPS)
    nexpA = P.buf("nexpA", [128, 8])
    P.act(nexpA[:, :], alog_bc[:, :], AF.Exp, reads=[alog_bc], writes=[nexpA])
    P.ts("dve", nexpA[:, :], nexpA[:, :], -1.0, ALU.mult, reads=[nexpA], writes=[nexpA])

    scT = P.buf("scT", [128, 8, 2])
    for t in range(2):
        P.act(scT[:, :, t], cT[:, t, :], AF.Silu, reads=[cT], writes=[scT])
    scbc = P.abuf("scbc", [128, 2, 8, 128])
    for t in range(2):
        P.cp("dve", scbc[:, t, :, :], scT[:, :, t].unsqueeze(2).to_broadcast([128, 8, 128]), reads=[scT], writes=[scbc])
    modT = P.buf("modT", [128, 48, 2])
    gates = P.buf("gates", [128, 2, 2, D])
    for blk in range(12):
        wst = P.aring("wada_st", [128, 8, 512], F32, bufs=2)
        P.dma("sp" if blk % 2 == 0 else "act", wst[:, :, :],
              w_ada[:, blk * 512:(blk + 1) * 512].rearrange("(c p) n -> p c n", p=128), writes=[wst])
        ps = P.ps()
        for fc in range(4):
            for kc in range(8):
                P.mm(ps, ps[:, fc * 2:fc * 2 + 2], wst[:, kc, fc * 128:(fc + 1) * 128], scT[:, kc, :],
                     reads=[wst, scT], start=(kc == 0), stop=(kc == 7))
        P.tt("dve", modT[:, blk * 4:blk * 4 + 4, :], ps[:, 0:8].rearrange("p (f t) -> p f t", t=2),
             badaT[:, blk * 4:blk * 4 + 4].unsqueeze(2).to_broadcast([128, 4, 2]), ALU.add,
             reads=[ps, badaT], writes=[modT])
        gsel = {4: (0, 0), 5: (0, 1), 10: (1, 0), 11: (1, 1)}.get(blk)
        if gsel is not None:
            gi, half = gsel
            for t in range(2):
                ps2 = P.ps()
                for kc in range(8):
                    P.mm(ps2, ps2[:, :], scbc[:, t, kc, :], wst[:, kc, :], reads=[scbc, wst],
                         start=(kc == 0), stop=(kc == 7))
                P.tt("dve", gates[:, t, gi, half * 512:(half + 1) * 512], ps2[:, :],
                     bgate[:, gi, half * 512:(half + 1) * 512], ALU.add, reads=[ps2, bgate], writes=[gates])
    s1T = P.buf("s1T", [128, 8, 2]); sh1T = P.buf("sh1T", [128, 8, 2])
    s2T = P.buf("s2T", [128, 8, 2]); sh2T = P.buf("sh2T", [128, 8, 2])
    for (sT, shT, gT, base) in ((s1T, sh1T, gmixT, 0), (s2T, sh2T, gffnT, 24)):
        P.cp("dve", shT[:, :, :], modT[:, base:base + 8, :], reads=[modT], writes=[shT])
        P.ts("dve", sT[:, :, :], modT[:, base + 8:base + 16, :], 1.0, ALU.add, reads=[modT], writes=[sT])
        P.tt("dve", sT[:, :, :], sT[:, :, :], gT[:, :].unsqueeze(2).to_broadcast([128, 8, 2]), ALU.mult,
             reads=[sT, gT], writes=[sT])
    if dbg:
        dbg_out["modT"] = dt("dbg_modT", [128, 96], F32, kind="ExternalOutput").ap()
        P.dma("sp", dbg_out["modT"], modT[:, :, :].rearrange("p f t -> p (f t)"), reads=[modT], is_output=True)
        dbg_out["gates"] = dt("dbg_gates", [128, 4 * D], F32, kind="ExternalOutput").ap()
        P.dma("sp", dbg_out["gates"], gates[:, :, :, :].rearrange("p a b d -> p (a b d)"), reads=[gates], is_output=True)
    if stop_after == "mod":
        P.emit()
        return nc

    class Key:
        def __init__(s, k):
            s.key = k

    P.arena_reset(AB0)
    winb = P.abuf("winb", [128, 8, NCOL], BF16)
    woutb = P.abuf("woutb", [128, 8, D], BF16)
    wfs = P.abuf("wfs", [128, 4, 128])
    W1 = P.abuf("W1", [128, 4, 256], BF16)
    W2 = P.abuf("W2", [128, 4, 256], BF16)
    eps128 = P.abuf("eps128", [128, 1])
    zt = P.abuf("zt", [128, 8, 1], BF16)
    Sst = [P.abuf("Sf", [128, 4, 128]), P.abuf("Sb", [128, 4, 128])]
    SCR0 = P.aoff
    for kc in range(8):
        st = P.aring("wst", [128, NCOL], F32, bufs=2)
        P.dma("sp" if kc % 2 == 0 else "act", st[:, :], w_in[kc * 128:(kc + 1) * 128, :], writes=[st])
        P.cp("dve" if kc % 2 == 0 else "pool", winb[:, kc, :], st[:, :], reads=[st], writes=[winb])
    for kc in range(8):
        st = P.aring("wst2", [128, D], F32, bufs=2)
        P.dma("sp" if kc % 2 == 0 else "act", st[:, :], w_out[kc * 128:(kc + 1) * 128, :], writes=[st])
        P.cp("dve" if kc % 2 == 0 else "pool", woutb[:, kc, :], st[:, :], reads=[st], writes=[woutb])
    P.dma("sp", wfs[:, :, :], w_fno.rearrange("g l d -> l g d"), writes=[wfs])
    for (tab, dsts) in (("CC", ((W1, 0), (W2, 128))), ("SC", ((W1, 128),)), ("nSC", ((W2, 0),))):
        ps = P.ps()
        for g in range(4):
            P.mm(ps, ps[:, g * 128:(g + 1) * 128], C(tab), wfs[:, g, :], reads=[cst, wfs])
        for (Wt, o_) in dsts:
            P.cp("dve", Wt[:, :, o_:o_ + 128], ps[:, :].rearrange("p (g d) -> p g d", g=4), reads=[ps], writes=[Wt])
    P.memset("dve", eps128, eps128[:, :], 128.0 * EPS)
    P.memset("dve", zt, zt[:, :, :], 0.0)
    mk = lambda nm: C(nm).unsqueeze(1).to_broadcast([128, 4, 128])
    v3 = lambda ps_: ps_[:, :].rearrange("p (h j) -> p h j", h=4)
    bc4 = lambda ap: ap.unsqueeze(2).to_broadcast([128, 4, 128])
    dq = ["sp", "act"]

    seqs = [(0, s, 256, 0) for s in range(4)] + [(1, 0, 4096, 1)]
    if stop_after == "seq0":
        seqs = seqs[:1]
    if stop_after in ("seq4", "seq4_a0", "seq4_f"):
        seqs = seqs[4:]
    if stop_after == "ffnonly":
        seqs = []
    tokbase = 0
    for (grp, si, N, cond) in seqs:
        xr = xin[grp][si * N:(si + 1) * N, :]
        NTL = N // 128
        HT = dt(f"ht_{grp}_{si}", [128, 8, N + 2], BF16).ap()
        YTD = dt(f"ytd_{grp}_{si}", [128, 4, N], BF16).ap()
        htk = [Key(f"HT{grp}{si}_{b}") for b in range(NTL)]
        hk0 = Key(f"HTz{grp}{si}")
        ytk = Key(f"YT{grp}{si}")
        P.arena_reset(SCR0)
        P.dma("sp", HT[:, :, 0:1], zt[:, :, :], reads=[zt], writes=[hk0], slow=True)
        P.dma("sp", HT[:, :, N + 1:N + 2], zt[:, :, :], reads=[zt], writes=[hk0], slow=True)
        for b in range(NTL):
            xt = P.aring("xt", [128, D], F32, bufs=2)
            P.dma("sp", xt[:, :], xr[b * 128:(b + 1) * 128, :], writes=[xt])
            junk = P.aring("junk", [128, D], F32)
            ss = P.ring("ss", [128, 1], F32, bufs=2)
            P.act(junk[:, :], xt[:, :], AF.Square, reads=[xt], writes=[junk, ss], accum_out=ss[:, :])
            P.act(ss[:, :], ss[:, :], AF.Ln, reads=[ss, eps_c], writes=[ss], scale=1.0 / D, bias=eps_c[:, :])
            P.act(ss[:, :], ss[:, :], AF.Exp, reads=[ss], writes=[ss], scale=-0.5)
            xnb = P.aring("xnb", [128, D], BF16)
            P.act(xnb[:, :], xt[:, :], AF.Copy, reads=[xt, ss], writes=[xnb], scale=ss[:, :])
            ps = P.ps()
            psb = ps.t[:, :].bitcast(BF16)
            for kc in range(8):
                P.tr(ps, psb[:, kc * 128:(kc + 1) * 128], xnb[:, kc * 128:(kc + 1) * 128], CB("ident"), reads=[xnb, cbf])
            tmp = P.aring("httmp", [128, 8, 128], F32)
            P.tt("dve", tmp[:, :, :], psb.rearrange("p (k t) -> p k t", k=8),
                 s1T[:, :, cond].unsqueeze(2).to_broadcast([128, 8, 128]), ALU.mult, reads=[ps, s1T], writes=[tmp])
            hb = P.aring("hb", [128, 8, 128], BF16, bufs=2)
            P.tt("pool", hb[:, :, :], tmp[:, :, :], sh1T[:, :, cond].unsqueeze(2).to_broadcast([128, 8, 128]), ALU.add,
                 reads=[tmp, sh1T], writes=[hb])
            P.dma("pool", HT[:, :, 1 + b * 128:1 + (b + 1) * 128], hb[:, :, :], reads=[hb], writes=[htk[b]])

        def load_ht(t):
            ht = P.aring("ht", [128, 8, 130], BF16, bufs=2)
            rk = [hk0] + [htk[j] for j in (t - 1, t, t + 1) if 0 <= j < NTL]
            P.dma("sp", ht[:, :, :], HT[:, :, t * 128:t * 128 + 130], reads=rk, writes=[ht])
            return ht

        if stop_after == "seq4_a0":
            dbg_out["o"] = dt("dbg_o", [128, 8], F32, kind="ExternalOutput").ap()
            P.dma("sp", dbg_out["o"], gmixT[:, :], reads=[gmixT] + htk, is_output=True)
            P.emit()
            return nc
        P.arena_reset(SCR0)
        if N == 256:
            fbs = []
            for blk in range(2):
                ht = load_ht(blk)
                ps = P.ps()
                for kc in range(8):
                    P.mm(ps, ps[:, :], ht[:, kc, 1:129], winb[:, kc, 2064:2576], reads=[ht, winb], start=(kc == 0), stop=(kc == 7))
                fb = P.abuf("fb", [128, 512], BF16)
                P.cp("act", fb[:, :], ps[:, :], reads=[ps], writes=[fb])
                fbs.append(fb)
            ytg = P.abuf("ytg", [128, 4, 256], BF16)
            for g in range(4):
                psA = P.ps()
                for blk in range(2):
                    P.mm(psA, psA[:, 0:256], fbs[blk][:, g * 128:(g + 1) * 128], CNT(blk), reads=[fbs[blk], cnbf], start=(blk == 0), stop=(blk == 1))
                for blk in range(2):
                    P.mm(psA, psA[:, 256:512], fbs[blk][:, g * 128:(g + 1) * 128], SNT(blk), reads=[fbs[blk], cnbf], start=(blk == 0), stop=(blk == 1))
                Ab = P.aring("Ab", [128, 512], BF16, bufs=2)
                P.cp("dve", Ab[:, :], psA[:, :], reads=[psA], writes=[Ab])
                psY = P.ps()
                P.mm(psY, psY[:, 0:256], W1[:, g, 0:128], Ab[:, 0:256], reads=[W1, Ab], start=True, stop=False)
                P.mm(psY, psY[:, 0:256], W2[:, g, 0:128], Ab[:, 256:512], reads=[W2, Ab], start=False, stop=True)
                P.cp("act", ytg[:, g, :], psY[:, 0:256], reads=[psY], writes=[ytg])
            P.dma("act", YTD[:, :, :], ytg[:, :, :], reads=[ytg], writes=[ytk])
        else:
            A_all = P.abuf("A_all", [128, 2, 4, 4096], BF16)
            for rp in range(32):
                ht = load_ht(rp)
                ps = P.ps()
                for kc in range(8):
                    P.mm(ps, ps[:, :], ht[:, kc, 1:129], winb[:, kc, 2064:2576], reads=[ht, winb], start=(kc == 0), stop=(kc == 7))
                fb = P.aring("fb", [128, 512], BF16, bufs=2)
                P.cp("act", fb[:, :], ps[:, :], reads=[ps], writes=[fb])
                for gp in range(2):
                    ps2 = P.ps()
                    for gg in range(2):
                        g = gp * 2 + gg
                        P.mm(ps2, ps2[:, gg * 256:gg * 256 + 128], fb[:, g * 128:(g + 1) * 128], CB("BDC"), reads=[fb, cbf])
                        P.mm(ps2, ps2[:, gg * 256 + 128:gg * 256 + 256], fb[:, g * 128:(g + 1) * 128], CB("BDS"), reads=[fb, cbf])
                    for ri in range(2):
                        dst = A_all[:, ri, gp * 2:gp * 2 + 2, :].rearrange("p g (kc r) -> p g kc r", r=64)[:, :, :, 2 * rp:2 * rp + 2]
                        dst = dst.rearrange("p g kc b -> p g b kc")
                        src = ps2[:, :].rearrange("p (g ri b kc) -> p ri g b kc", g=2, ri=2, b=2)[:, ri]
                        P.cp("dve" if ri == 0 else "act", dst, src, reads=[ps2], writes=[A_all])
            for g in range(4):
                ytg = P.aring("ytg", [128, 4096], BF16, bufs=1)
                def s3(kcp_, Bb_):
                    psY = P.ps()
                    P.mm(psY, psY[:, 0:128], Bb_[:, 0:128], CB("BDCs"), reads=[Bb_, cbf], start=True, stop=False)
                    P.mm(psY, psY[:, 0:128], Bb_[:, 128:256], CB("nBDSs"), reads=[Bb_, cbf], start=False, stop=True)
                    P.cp("dve", ytg[:, :].rearrange("p (kr kc) -> p kc kr", kc=64)[:, 2 * kcp_:2 * kcp_ + 2, :],
                         psY[:, 0:128].rearrange("p (a k) -> p a k", a=2), reads=[psY], writes=[ytg])
                pendf = None
                for kcp in range(32):
                    sub = lambda ri: A_all[:, ri, g, kcp * 128:(kcp + 1) * 128]
                    psB = P.ps()
                    P.mm(psB, psB[:, 0:256], sub(0), W1[:, g, :], reads=[A_all, W1], start=True, stop=False)
                    P.mm(psB, psB[:, 0:256], sub(1), W2[:, g, :], reads=[A_all, W2], start=False, stop=True)
                    Bb = P.aring("Bb", [128, 256], BF16, bufs=3)
                    P.cp("act", Bb[:, :], psB[:, 0:256], reads=[psB], writes=[Bb])
                    if pendf is not None:
                        s3(*pendf)
                    pendf = (kcp, Bb)
                s3(*pendf)
                P.dma("act", YTD[:, g, :], ytg[:, :], reads=[ytg], writes=[ytk])

        if stop_after == "seq4_f":
            dbg_out["o"] = dt("dbg_o", [128, 8], F32, kind="ExternalOutput").ap()
            P.dma("sp", dbg_out["o"], gmixT[:, :], reads=[gmixT, ytk], is_output=True)
            P.emit()
            return nc
        P.arena_reset(SCR0)
        for T2 in range(N // 256):
            ht2 = P.aring("ht2", [128, 8, 258], BF16, bufs=2)
            rk = [hk0] + [htk[j] for j in range(2 * T2 - 1, 2 * T2 + 3) if 0 <= j < NTL]
            P.dma("sp", ht2[:, :, :], HT[:, :, T2 * 256:T2 * 256 + 258], reads=rk, writes=[ht2])
            q2 = P.aring("q2", [128, 12, 256], F32, bufs=2)
            pend = None
            for cc in range(12):
                ps = P.ps()
                for kc in range(8):
                    P.mm(ps, ps[:, 0:258], winb[:, kc, cc * 128:(cc + 1) * 128], ht2[:, kc, :], reads=[winb, ht2], start=(kc == 0), stop=(kc == 7))
                c1 = P.aring("fc1", [128, 256], bufs=3)
                c2 = P.aring("fc2", [128, 256], bufs=3)
                P.act(c1[:, :], ps[:, 0:256], AF.Copy, reads=[ps, wconvT], writes=[c1], scale=wconvT[:, 0, cc:cc + 1])
                P.stt("dve", c2[:, :], ps[:, 1:257], wconvT[:, 1, cc:cc + 1], c1[:, :], ALU.mult, ALU.add, reads=[ps, c1, wconvT], writes=[c2])
                P.stt("dve", c1[:, :], ps[:, 2:258], wconvT[:, 2, cc:cc + 1], c2[:, :], ALU.mult, ALU.add, reads=[ps, c2, wconvT], writes=[c1])
                if pend is not None:
                    P.act(q2[:, pend[1], :], pend[0][:, :], AF.Silu, reads=[pend[0]], writes=[q2])
                pend = (c1, cc)
            P.act(q2[:, pend[1], :], pend[0][:, :], AF.Silu, reads=[pend[0]], writes=[q2])
            sq = P.aring("fsq", [128, 8, 256], F32, bufs=2)
            P.act(sq[:, :, :], q2[:, 0:8, :], AF.Square, reads=[q2], writes=[sq])
            rn = P.aring("frn", [128, 8, 256], F32, bufs=2)
            for pair in range(4):
                ps = P.ps()
                for j in range(2):
                    P.mm(ps, ps[:, j * 256:(j + 1) * 256], ones, sq[:, pair * 2 + j, :], reads=[cst, sq])
                P.act(rn[:, pair * 2:pair * 2 + 2, :], ps[:, :].rearrange("p (a b) -> p a b", a=2), AF.Ln, reads=[ps, eps128, eps_c], writes=[rn],
                      scale=(128.0 if pair < 2 else 1.0), bias=(eps128 if pair < 2 else eps_c)[:, :])
            P.act(rn[:, :, :], rn[:, :, :], AF.Exp, reads=[rn], writes=[rn], scale=-0.5)
            P.tt("dve", q2[:, 0:8, :], q2[:, 0:8, :], rn[:, :, :], ALU.mult, reads=[q2, rn], writes=[q2])
            for blk in range(2):
                gtp = tokbase // 128 + 2 * T2 + blk
                bs = slice(blk * 128, (blk + 1) * 128)
                knv = P.aring("fknv", [128, 2, 4, 128], F32, bufs=2)
                for which, base in ((0, 4), (1, 8)):
                    ps = P.ps()
                    for j in range(4):
                        P.tr(ps, ps[:, j * 128:(j + 1) * 128], q2[:, base + j, bs], ident, reads=[q2, cst])
                    P.cp("act", knv[:, which, :, :], v3(ps), reads=[ps], writes=[knv])
                P.dma("act", SK[gtp].rearrange("p (a h j) -> p a h j", a=2, h=4), knv[:, :, :, :], reads=[knv], writes=[Key(f"SHk{gtp}")])
                ps = P.ps()
                for kc in range(8):
                    P.mm(ps, ps[:, 0:16], ht2[:, kc, 1 + blk * 128:129 + blk * 128], winb[:, kc, 2048:2064], reads=[ht2, winb], start=(kc == 0), stop=(kc == 7))
                lab = P.aring("flab", [128, 16], F32, bufs=2)
                P.tt("dve", lab[:, 0:8], ps[:, 0:8], dtb_bc[:, :], ALU.add, reads=[ps, dtb_bc], writes=[lab])
                P.act(lab[:, 8:16], ps[:, 8:16], AF.Exp, reads=[ps], writes=[lab], scale=-1.0)
                P.act(lab[:, 0:8], lab[:, 0:8], AF.Exp, reads=[lab], writes=[lab])
                P.act(lab[:, 0:16], lab[:, 0:16], AF.Ln, reads=[lab], writes=[lab], bias=1.0)
                P.act(lab[:, 8:16], lab[:, 8:16], AF.Exp, reads=[lab], writes=[lab], scale=-1.0)
                P.tt("dve", lab[:, 0:8], lab[:, 0:8], nexpA[:, :], ALU.mult, reads=[lab, nexpA], writes=[lab])
                P.dma("act", SL[gtp], lab[:, :], reads=[lab], writes=[Key(f"SHl{gtp}")])
                P.dma("pool", SQ[gtp][:, 0:1024].rearrange("p (c t) -> p c t", c=8), q2[:, 0:8, bs], reads=[q2], writes=[Key(f"SHq{gtp}")])
        P.arena_reset(SCR0)
        streams = [Stream(P, 0), Stream(P, 1)]
        for d in (0, 1):
            S = Sst[d]
            if grp == 0:
                P.memset("pool", S, S[:, :, :], 0.0)
            else:
                P.dma("sp", S[:, :, :], (s0f if d == 0 else s0b).rearrange("h k v -> k h v"), writes=[S])
            streams[d].u = streams[d].alloc("u", [128, 4, 128])
            P.memset("pool", streams[d].u, streams[d].u[:, :, :], 0.0)
            streams[d].oS = streams[d].alloc("oS", [128, 4, 128])

        def tile_gen(st, t, d, second):
            A = st.alloc
            R = st.release
            PS = lambda: P.ps(d)
            S = Sst[d]
            u, oS = st.u, st.oS
            gt_ = tokbase // 128 + t
            ht = None
            if second:
                ht = A("ht", [128, 8, 130], BF16)
            qkvT = A("qkvT", [128, 8, 128])
            knv = A("knv", [128, 2, 4, 128])
            lab = A("lab", [128, 16])
            P.dma("sp", lab[:, :], SL[gt_], reads=[Key(f"SHl{gt_}")], writes=[lab])
            P.dma("sp", qkvT[:, 4:8, :], SQ[gt_][:, 512:1024].rearrange("p (c t) -> p c t", c=4), reads=[Key(f"SHq{gt_}")], writes=[qkvT])
            P.dma("sp", qkvT[:, 0:4, :], SQ[gt_][:, 0:512].rearrange("p (c t) -> p c t", c=4), reads=[Key(f"SHq{gt_}")], writes=[qkvT])
            P.dma("sp", knv[:, :, :, :], SK[gt_].rearrange("p (a h j) -> p a h j", a=2, h=4), reads=[Key(f"SHk{gt_}")], writes=[knv])
            if second:
                rk = [hk0] + [htk[j] for j in (t - 1, t, t + 1) if 0 <= j < NTL]
                P.dma("sp", ht[:, :, :], HT[:, :, t * 128:t * 128 + 130], reads=rk, writes=[ht])
            yield
            la = lab[:, d * 4:(d + 1) * 4]
            be = lab[:, 8 + d * 4:8 + (d + 1) * 4]
            yield
            labc = A("labc", [128, 4, 128])
            P.cp("pool", labc[:, :, :], bc4(la), reads=[lab], writes=[labc])
            psg = PS()
            for h in range(4):
                P.mm(psg, psg[:, h * 128:(h + 1) * 128], labc[:, h, :], C(f"U{d}"), reads=[labc, cst])
            R(labc)
            psc = PS()
            P.mm(psc, psc[:, 0:4], C(f"U{d}"), la, reads=[cst, lab])
            sm = A("sm", [128, 9, 4])
            P.cp("dve", sm[:, 0, :], psc[:, 0:4], reads=[psc], writes=[sm])
            P.ts("dve", sm[:, 8, :], sm[:, 0, :], -1.0, ALU.mult, reads=[sm], writes=[sm])
            li = (63, 127) if d == 0 else (0, 64)
            P.cp("dve", sm[:, 6, :], v3(psg)[:, :, li[0]], reads=[psg], writes=[sm])
            P.cp("dve", sm[:, 7, :], v3(psg)[:, :, li[1]], reads=[psg], writes=[sm])
            Dm = A("Dm", [128, 4, 128])
            DT = A("DT", [128, 4, 128])
            for h in range(4):
                P.act(Dm[:, h, :], psg[:, h * 128:(h + 1) * 128], AF.Relu, reads=[psg, sm], writes=[Dm], scale=1.0, bias=sm[:, 8, h:h + 1])
                P.act(DT[:, h, :], psg[:, h * 128:(h + 1) * 128], AF.Relu, reads=[psg, sm], writes=[DT], scale=-1.0, bias=sm[:, 0, h:h + 1])
            P.act(Dm[:, :, :], Dm[:, :, :], AF.Exp, reads=[Dm], writes=[Dm], scale=-1.0)
            P.act(DT[:, :, :], DT[:, :, :], AF.Exp, reads=[DT], writes=[DT], scale=-1.0)
            m0 = C("mab")[:, 0:1]
            m1 = C("mab")[:, 1:2]
            P.ts("dve", sm[:, 2, :], sm[:, 6, :], m0, ALU.mult, reads=[sm, cst], writes=[sm])
            P.stt("dve", sm[:, 2, :], sm[:, 7, :], m1, sm[:, 2, :], ALU.mult, ALU.add, reads=[sm, cst], writes=[sm])
            P.tt("dve", sm[:, 2, :], sm[:, 2, :], sm[:, 0, :], ALU.subtract, reads=[sm], writes=[sm])
            P.act(sm[:, 2, :], sm[:, 2, :], AF.Exp, reads=[sm], writes=[sm])
            P.ts("dve", sm[:, 3, :], sm[:, 2, :], m0, ALU.mult, reads=[sm, cst], writes=[sm])
            P.ts("dve", sm[:, 4, :], sm[:, 2, :], m1, ALU.mult, reads=[sm, cst], writes=[sm])
            P.act(sm[:, 1, :], sm[:, 0, :], AF.Exp, reads=[sm], writes=[sm])
            P.tt("dve", sm[:, 5, :], sm[:, 1, :], be, ALU.mult, reads=[sm, lab], writes=[sm])
            P.act(sm[:, 6:8, :], sm[:, 6:8, :], AF.Exp, reads=[sm], writes=[sm])
            yield
            psk = PS()
            for h in range(4):
                P.mm(psk, psk[:, h * 128:(h + 1) * 128], qkvT[:, 4 + h, :], qkvT[:, 4 + h, :], reads=[qkvT])
            t2 = A("t2", [128, 4, 128])
            P.tt("dve", t2[:, :, :], Dm[:, :, :], v3(psk), ALU.mult, reads=[Dm, psk], writes=[t2])
            R(Dm)
            P.tt("pool", t2[:, :, :], t2[:, :, :], bc4(be), ALU.mult, reads=[t2, lab], writes=[t2])
            Nn = A("Nn", [128, 4, 128])
            Aoff = A("Aoff", [128, 4, 128])
            P.tt("pool", Nn[:, :, :], t2[:, :, :], mk(f"NMD{d}"), ALU.mult, reads=[t2, cst], writes=[Nn])
            P.tt("dve", Aoff[:, :, :], t2[:, :, :], mk(f"MO{d}"), ALU.mult, reads=[t2, cst], writes=[Aoff])
            R(t2)
            psq = PS()
            for h in range(4):
                P.mm(psq, psq[:, h * 128:(h + 1) * 128], qkvT[:, 4 + h, :], qkvT[:, h, :], reads=[qkvT])
            aqkT = A("aqkT", [128, 4, 128])
            P.tt("dve", aqkT[:, :, :], DT[:, :, :], v3(psq), ALU.mult, reads=[DT, psq], writes=[aqkT])
            R(DT)
            P.tt("pool", aqkT[:, :, :], aqkT[:, :, :], mk(f"MIT{d}"), ALU.mult, reads=[aqkT, cst], writes=[aqkT])
            yield
            RHS = A("RHS", [128, 4, 256])
            P.tt("pool", RHS[:, :, 0:128], knv[:, 1, :, :], bc4(be), ALU.mult, reads=[knv, lab], writes=[RHS])
            P.tt("pool", RHS[:, :, 128:256], knv[:, 0, :, :], bc4(sm[:, 5, :]), ALU.mult, reads=[knv, sm], writes=[RHS])
            kdA = A("kdA", [128, 4, 128])
            kdB = A("kdB", [128, 4, 128])
            P.tt("pool", kdA[:, :, :], knv[:, 0, :, :], bc4(sm[:, 3, :]), ALU.mult, reads=[knv, sm], writes=[kdA])
            P.tt("pool", kdB[:, :, :], knv[:, 0, :, :], bc4(sm[:, 4, :]), ALU.mult, reads=[knv, sm], writes=[kdB])
            R(knv)
            def mm4(lh, rh):
                ps_ = PS()
                for h in range(4):
                    P.mm(ps_, ps_[:, h * 128:(h + 1) * 128], lh[:, h, :], rh[:, h, :], reads=[lh, rh])
                return ps_

            def step2(lh, rh, evac, tr=False):
                for hp in range(2):
                    ps_ = PS()
                    for j in range(2):
                        h = hp * 2 + j
                        if tr:
                            P.tr(ps_, ps_[:, j * 128:(j + 1) * 128], lh(h), ident, reads=[rh, cst])
                        else:
                            P.mm(ps_, ps_[:, j * 128:(j + 1) * 128], lh[:, h, :], rh[:, h, :], reads=[lh, rh])
                    evac(hp, slice(hp * 2, hp * 2 + 2), ps_[:, 0:256].rearrange("p (j n) -> p j n", j=2), ps_)

            def ev_copy(dst, eng_pair=("act", "dve")):
                return lambda hp, hs, pv, ps_: P.cp(eng_pair[hp], dst[:, hs, :], pv, reads=[ps_], writes=[dst])

            def ev_add(dst, other):
                return lambda hp, hs, pv, ps_: P.tt("dve", dst[:, hs, :], pv, other[:, hs, :], ALU.add, reads=[ps_, other], writes=[dst])

            NTt = A("NTt", [128, 4, 128])
            TTa = A("TTa", [128, 4, 128])
            step2(lambda h: Nn[:, h, :], Nn, ev_copy(NTt, ("act", "act")), tr=True)
            P.tt("pool", TTa[:, :, :], NTt[:, :, :], mk("ident"), ALU.add, reads=[NTt, cst], writes=[TTa])
            yield
            Pa = A("Pa", [128, 4, 128])
            PaT = A("PaT", [128, 4, 128])
            step2(NTt, Nn, ev_copy(Pa))
            step2(Nn, NTt, ev_copy(PaT, ("dve", "act")))
            R(Nn); R(NTt)
            yield
            TTb = A("TTb", [128, 4, 128])
            step2(Pa, TTa, ev_add(TTb, TTa))
            Pb = A("Pb", [128, 4, 128])
            PbT = A("PbT", [128, 4, 128])
            step2(PaT, Pa, ev_copy(Pb))
            yield
            step2(Pa, PaT, ev_copy(PbT, ("dve", "act")))
            R(PaT)
            step2(Pb, TTb, ev_add(TTa, TTb))
            yield
            step2(PbT, Pb, ev_copy(Pa))
            R(PbT)
            step2(Pa, TTa, ev_add(TTb, TTa))
            R(TTa); R(Pa)
            yield
            step2(Aoff, TTb, lambda hp, hs, pv, ps_: P.act(Pb[:, hs, :], pv, AF.Copy, reads=[ps_], writes=[Pb], scale=-1.0))
            R(Aoff)
            nGT = Pb
            Xs = [A("Xa", [128, 4, 256]), A("Xb", [128, 4, 256])]

            XS1 = A("XS1", [128, 4, 256])

            def xsweep(dst, src):
                for hp in range(2):
                    ps_ = PS()
                    for j in range(2):
                        h = hp * 2 + j
                        if src is None:
                            P.mm(ps_, ps_[:, j * 256:(j + 1) * 256], TTb[:, h, :], RHS[:, h, :], reads=[TTb, RHS])
                        else:
                            P.mm(ps_, ps_[:, j * 256:(j + 1) * 256], nGT[:, h, :], src[:, h, :], reads=[nGT, src])
                    pv = ps_[:, :].rearrange("p (j n) -> p j n", j=2)
                    hsl = slice(hp * 2, hp * 2 + 2)
                    if src is None:
                        P.cp("act" if hp == 0 else "dve", dst[:, hsl, :], pv, reads=[ps_], writes=[dst])
                    else:
                        P.tt("dve", dst[:, hsl, :], pv, XS1[:, hsl, :], ALU.add, reads=[ps_, XS1], writes=[dst])
            xsweep(XS1, None)
            yield
            xsweep(Xs[0], XS1)
            yield
            xsweep(Xs[1], Xs[0])
            yield
            xsweep(Xs[0], Xs[1])
            Xs = [Xs[1], Xs[0]]
            Xc = Xs[1]
            R(TTb); R(RHS); R(Xs[0]); R(nGT); R(XS1)
            wT = A("wT", [128, 4, 128])
            step2(lambda h: Xc[:, h, 128:256], Xc, ev_copy(wT, ("act", "act")), tr=True)
            yield
            for c in ((0, 1) if d == 0 else (1, 0)):
                r0, r1 = c * 64, (c + 1) * 64
                ps = mm4(wT, S)
                P.tt("dve", u[r0:r1, :, :], Xc[r0:r1, :, 0:128], v3(ps)[r0:r1], ALU.subtract, reads=[Xc, ps], writes=[u])
                ps2 = mm4(qkvT, S)
                P.tt("dve", oS[r0:r1, :, :], v3(ps2)[r0:r1], sm[r0:r1, 1, :].unsqueeze(2).to_broadcast([64, 4, 128]), ALU.mult,
                     reads=[ps2, sm], writes=[oS])
                yield
                ps3 = mm4(kdA if c == 0 else kdB, u)
                for h in range(4):
                    P.stt("dve", S[:, h, :], S[:, h, :], sm[:, 6 + c, h:h + 1], ps3[:, h * 128:(h + 1) * 128], ALU.mult, ALU.add,
                          reads=[S, sm, ps3], writes=[S])
                yield
            ps = mm4(aqkT, u)
            ot = A("ot", [128, 4, 128])
            P.tt("dve", ot[:, :, :], oS[:, :, :], v3(ps), ALU.add, reads=[oS, ps], writes=[ot])
            for b_ in (qkvT, aqkT, kdA, kdB, Xc, wT, sm):
                R(b_)
            ofk = Key(f"OF{gt_}")
            if not second:
                P.dma("act", OF[gt_ * 128:(gt_ + 1) * 128, :], ot[:, :, :].rearrange("p h v -> p (h v)"), reads=[ot], writes=[ofk])
                R(ot); R(lab)
                return
            yield
            of = A("of", [128, 4, 128])
            P.dma("sp", of[:, :, :], OF[gt_ * 128:(gt_ + 1) * 128, :].rearrange("p (h v) -> p h v", h=4), reads=[ofk], writes=[of])
            P.tt("pool", ot[:, :, :], ot[:, :, :], of[:, :, :], ALU.add, reads=[ot, of], writes=[ot])
            R(of)
            osq = A("osq", [128, 4, 128])
            P.act(osq[:, :, :], ot[:, :, :], AF.Square, reads=[ot], writes=[osq])
            rs = A("rs", [128, 4])
            P.op("dve", (lambda o_, i_: lambda e: e.tensor_reduce(out=o_, in_=i_, axis=AX.X, op=ALU.add))(rs[:, :], osq[:, :, :]),
                 reads=[osq], writes=[rs])
            R(osq)
            P.act(rs[:, :], rs[:, :], AF.Ln, reads=[rs, eps_c], writes=[rs], scale=1.0 / 128, bias=eps_c[:, :])
            P.act(rs[:, :], rs[:, :], AF.Exp, reads=[rs], writes=[rs], scale=-0.5)
            P.tt("dve", ot[:, :, :], ot[:, :, :], bc4(rs[:, :]), ALU.mult, reads=[ot, rs], writes=[ot])
            P.tt("pool", ot[:, :, :], ot[:, :, :], go_bc[:, :].unsqueeze(1).to_broadcast([128, 4, 128]), ALU.mult, reads=[ot, go_bc], writes=[ot])
            R(rs)
            psz = PS()
            for kc in range(8):
                P.mm(psz, psz[:, :], ht[:, kc, 1:129], winb[:, kc, 1536:2048], reads=[ht, winb], start=(kc == 0), stop=(kc == 7))
            sz = A("sz", [128, 512])
            P.act(sz[:, :], psz[:, :], AF.Silu, reads=[psz], writes=[sz])
            mixb = A("mixb", [128, 512], BF16)
            P.tt("dve", mixb[:, :], ot[:, :, :].rearrange("p h v -> p (h v)"), sz[:, :], ALU.mult, reads=[ot, sz], writes=[mixb])
            R(sz); R(ot); R(ht); R(lab)
            yield
            pst = PS()
            pstb = pst.t[:, :].bitcast(BF16)
            for j in range(4):
                P.tr(pst, pstb[:, j * 128:(j + 1) * 128], mixb[:, j * 128:(j + 1) * 128], CB("ident"), reads=[mixb, cbf])
            mixT = A("mixT", [128, 8, 128], BF16)
            P.cp("act", mixT[:, 0:4, :], pstb[:, 0:512].rearrange("p (k t) -> p k t", k=4), reads=[pst], writes=[mixT])
            R(mixb)
            P.dma("sp", mixT[:, 4:8, :], YTD[:, :, t * 128:(t + 1) * 128], reads=[ytk], writes=[mixT])
            xt = A("xt", [128, D])
            P.dma("sp", xt[:, :], xr[t * 128:(t + 1) * 128, :], writes=[xt])
            x1 = A("x1", [128, D])
            yield
            for half in range(2):
                pso = PS()
                for kc in range(8):
                    P.mm(pso, pso[:, :], mixT[:, kc, :], woutb[:, kc, half * 512:(half + 1) * 512], reads=[mixT, woutb], start=(kc == 0), stop=(kc == 7))
                P.tt("dve", x1[:, half * 512:(half + 1) * 512], pso[:, :], gates[:, cond, 0, half * 512:(half + 1) * 512], ALU.mult,
                     reads=[pso, gates], writes=[x1])
            P.tt("pool", x1[:, :], x1[:, :], xt[:, :], ALU.add, reads=[x1, xt], writes=[x1])
            P.dma("pool", X1[gt_ * 128:(gt_ + 1) * 128, :], x1[:, :], reads=[x1], writes=[Key(f"X1{gt_}")])
            R(mixT); R(xt); R(x1)

        order = {0: list(range(NTL)), 1: list(range(NTL - 1, -1, -1))}
        pos = {0: 0, 1: 0}
        done = {0: 0, 1: 0}
        cur = {0: None, 1: None}
        ycnt = {0: 0, 1: 0}

        def try_start(d):
            i = pos[d]
            if cur[d] is not None or i >= NTL:
                return
            t = order[d][i]
            second = (t >= NTL // 2) if d == 0 else (t < NTL // 2)
            if second and done[1 - d] < NTL - i:
                return
            cur[d] = tile_gen(streams[d], t, d, second)
            pos[d] += 1

        def step(d):
            try_start(d)
            if cur[d] is None:
                return False
            try:
                next(cur[d])
                ycnt[d] += 1
            except StopIteration:
                cur[d] = None
                done[d] += 1
            return True

        while done[0] < NTL or done[1] < NTL:
            p0 = step(0)
            lead = ycnt[0] - ycnt[1]
            if (not p0) or lead >= SKEW or done[0] >= NTL:
                p1 = step(1)
            else:
                p1 = False
            assert p0 or p1 or lead < SKEW, "scheduler stuck"
            if not p0 and not p1:
                assert step(1), "scheduler deadlock"
        if grp == 0:
            for d in (0, 1):
                P.dma("sp", (nsf if d == 0 else nsb)[si].rearrange("h k v -> k h v"), Sst[d][:, :, :], reads=[Sst[d]], is_output=True)
        tokbase += N

    if dbg:
        dbg_out["x1"] = dt("dbg_x1", [256, D], F32, kind="ExternalOutput").ap()
        P.arena_reset(SCR0)
        dtile = P.aring("xtd", [128, D], F32, bufs=1)
        for b in range(2):
            P.dma("sp", dtile[:, :], X1[b * 128:(b + 1) * 128, :], reads=[Key(f"X1{b}")], writes=[dtile])
            P.dma("sp", dbg_out["x1"][b * 128:(b + 1) * 128, :], dtile[:, :], reads=[dtile], is_output=True)
    if stop_after in ("seq0", "seq4", "ab"):
        P.emit()
        return nc

    P.arena_reset(0)
    wgub = P.abuf("wgub", [128, 8, 2 * DFF], BF16)
    wdnb = P.abuf("wdnb", [128, 22, D], BF16)
    gfin_bc = P.abuf("gfin", [128, D])
    P.dma("sp", gfin_bc[:, :], g_final.partition_broadcast(128), writes=[gfin_bc])
    stg = [P.abuf(f"stg{i}", [128, 1408]) for i in range(2)]
    n = 0
    for kc in range(8):
        for q in range(4):
            st = stg[n % 2]
            P.dma(dq[n % 2], st[:, :], w_gu[kc * 128:(kc + 1) * 128, q * 1408:(q + 1) * 1408], writes=[st])
            P.cp(("dve", "pool", "act")[n % 3], wgub[:, kc, q * 1408:(q + 1) * 1408], st[:, :], reads=[st], writes=[wgub])
            n += 1
    for fc in range(22):
        st = stg[n % 2]
        P.dma(dq[n % 2], st[:, 0:D], w_down[fc * 128:(fc + 1) * 128, :], writes=[st])
        P.cp(("dve", "pool", "act")[n % 3], wdnb[:, fc, :], st[:, 0:D], reads=[st], writes=[wdnb])
        n += 1
    PC0 = P.aoff
    P.arena_reset(PC0 - 2 * 2816)
    for T in range(20):
        cond = 0 if T < 4 else 1
        x1 = P.aring("x1c", [128, 2, D], F32, bufs=1)
        h2 = P.aring("h2", [128, 8, 256], BF16, bufs=1)
        for blk in range(2):
            gt_ = 2 * T + blk
            P.dma("sp", x1[:, blk, :], X1[gt_ * 128:(gt_ + 1) * 128, :], reads=[Key(f"X1{gt_}")], writes=[x1])
            junk = P.aring("junkc", [128, D], F32, bufs=1)
            ss = P.ring("ss", [128, 1], F32, bufs=2)
            P.act(junk[:, :], x1[:, blk, :], AF.Square, reads=[x1], writes=[junk, ss], accum_out=ss[:, :])
            P.act(ss[:, :], ss[:, :], AF.Ln, reads=[ss, eps_c], writes=[ss], scale=1.0 / D, bias=eps_c[:, :])
            P.act(ss[:, :], ss[:, :], AF.Exp, reads=[ss], writes=[ss], scale=-0.5)
            xnb = P.aring("xnbc", [128, D], BF16, bufs=2)
            P.act(xnb[:, :], x1[:, blk, :], AF.Copy, reads=[x1, ss], writes=[xnb], scale=ss[:, :])
            ps = P.ps()
            psb = ps.t[:, :].bitcast(BF16)
            for kc in range(8):
                P.tr(ps, psb[:, kc * 128:(kc + 1) * 128], xnb[:, kc * 128:(kc + 1) * 128], CB("ident"), reads=[xnb, cbf])
            tmp = P.aring("httmpc", [128, 8, 128], F32, bufs=1)
            P.tt("dve", tmp[:, :, :], psb.rearrange("p (k t) -> p k t", k=8),
                 s2T[:, :, cond].unsqueeze(2).to_broadcast([128, 8, 128]), ALU.mult, reads=[ps, s2T], writes=[tmp])
            P.tt("pool", h2[:, :, blk * 128:(blk + 1) * 128], tmp[:, :, :], sh2T[:, :, cond].unsqueeze(2).to_broadcast([128, 8, 128]), ALU.add,
                 reads=[tmp, sh2T], writes=[h2])
        actT = P.aring("actT", [128, 22, 256], BF16, bufs=1)
        for fc in range(22):
            ps = P.ps()
            for kc in range(8):
                P.mm(ps, ps[:, 0:256], wgub[:, kc, fc * 128:(fc + 1) * 128], h2[:, kc, :], reads=[wgub, h2], start=(kc == 0), stop=(kc == 7))
            for kc in range(8):
                P.mm(ps, ps[:, 256:512], wgub[:, kc, DFF + fc * 128:DFF + (fc + 1) * 128], h2[:, kc, :], reads=[wgub, h2], start=(kc == 0), stop=(kc == 7))
            sg = P.aring("sg", [128, 256], F32, bufs=2)
            P.act(sg[:, :], ps[:, 0:256], AF.Silu, reads=[ps], writes=[sg])
            P.tt("dve", actT[:, fc, :], sg[:, :], ps[:, 256:512], ALU.mult, reads=[sg, ps], writes=[actT])
        for blk in range(2):
            gt_ = 2 * T + blk
            yt = P.aring("yt", [128, D], F32, bufs=2)
            for half in range(2):
                ps = P.ps()
                for fc in range(22):
                    P.mm(ps, ps[:, :], actT[:, fc, blk * 128:(blk + 1) * 128], wdnb[:, fc, half * 512:(half + 1) * 512], reads=[actT, wdnb], start=(fc == 0), stop=(fc == 21))
                P.tt("dve", yt[:, half * 512:(half + 1) * 512], ps[:, :], gates[:, cond, 1, half * 512:(half + 1) * 512], ALU.mult,
                     reads=[ps, gates], writes=[yt])
            P.tt("pool", yt[:, :], yt[:, :], x1[:, blk, :], ALU.add, reads=[yt, x1], writes=[yt])
            junk = P.aring("junkc", [128, D], F32, bufs=1)
            ss2 = P.ring("ss", [128, 1], F32, bufs=2)
            P.act(junk[:, :], yt[:, :], AF.Square, reads=[yt], writes=[junk, ss2], accum_out=ss2[:, :])
            P.act(ss2[:, :], ss2[:, :], AF.Ln, reads=[ss2, eps_c], writes=[ss2], scale=1.0 / D, bias=eps_c[:, :])
            P.act(ss2[:, :], ss2[:, :], AF.Exp, reads=[ss2], writes=[ss2], scale=-0.5)
            P.stt("dve", yt[:, :], yt[:, :], ss2[:, :], gfin_bc[:, :], ALU.mult, ALU.mult, reads=[yt, ss2, gfin_bc], writes=[yt])
            dst = yout[0][gt_ * 128:(gt_ + 1) * 128, :] if gt_ < 8 else yout[1][(gt_ - 8) * 128:(gt_ - 7) * 128, :]
            P.dma("pool", dst, yt[:, :], reads=[yt], is_output=True)
    P.emit()
    return nc


def _in_maps(inputs):
    g = lambda k: np.ascontiguousarray(np.asarray(inputs[k], dtype=np.float32))
    maps = []
    for c in range(8):
        m = {
            "xp": g("x_prompt")[4 * c:4 * c + 4].reshape(1024, D),
            "xs": g("x_sample")[c],
            "cvec": np.stack([g("c_ctx"), g("c")[c]], 0),
            "s0f": g("state_dn_fwd")[c, 0], "s0b": g("state_dn_bwd")[c, 0],
            "w_ada": g("w_ada")[0], "b_ada": g("b_ada")[0], "g_mix": g("g_mix")[0], "w_in": g("w_in")[0],
            "w_conv": g("w_conv")[0], "a_log": g("a_log")[0].reshape(8), "dt_bias": g("dt_bias")[0].reshape(8),
            "g_o": g("g_o")[0], "w_fno": g("w_fno")[0], "w_out": g("w_out")[0], "g_ffn": g("g_ffn")[0],
            "w_gu": g("w_gu")[0], "w_down": g("w_down")[0], "g_final": g("g_final"), "cst": _CST,
        }
        maps.append({k: np.ascontiguousarray(v) for k, v in m.items()})
    return maps


def kernel(**inputs):
    nc = build_nc()
    res = run_bass_kernel_spmd(nc, _in_maps(inputs), core_ids=list(range(8)))
    r = res.results
    yp = np.concatenate([r[c]["yp"].reshape(4, 256, D) for c in range(8)], 0)
    ys = np.stack([r[c]["ys"] for c in range(8)], 0)
    f = np.concatenate([r[c]["nsf"] for c in range(8)], 0)[:, None]
    b = np.concatenate([r[c]["nsb"] for c in range(8)], 0)[:, None]
    return (yp.astype(np.float32), ys.astype(np.float32), f.astype(np.float32), b.astype(np.float32))
```

```python
import numpy as np
import ml_dtypes
import concourse.bass as bass
import concourse.mybir as mybir
from concourse.bass_utils import run_bass_kernel_spmd

F32 = mybir.dt.float32
BF16 = mybir.dt.bfloat16
AF = mybir.ActivationFunctionType
ALU = mybir.AluOpType
AX = mybir.AxisListType

ENGS = ("pe", "act", "dve", "pool", "sp")
D = 1024
NCOL = 2576
DFF = 2816
EPS = 1e-6
DBG_TILES = None
SKEW = 8
DBG_STAGE = None
DBG_DIRS = (0, 1)


class Buf:
    def __init__(self, t, key):
        self.t = t
        self.key = key

    def __getitem__(self, idx):
        return self.t[idx]


class Ring:
    def __init__(self, bufs):
        self.bufs = bufs
        self.i = 0

    def get(self):
        b = self.bufs[self.i]
        self.i = (self.i + 1) % len(self.bufs)
        return b


class Stream:
    def __init__(self, P, sid):
        self.P = P
        self.sid = sid
        self.free = {}

    def alloc(self, name, shape, dtype=F32):
        P = self.P
        n = int(np.prod(shape[1:])) * (2 if dtype == F32 else 1)
        n = (n + 15) // 16 * 16
        if n >= 200:
            n = (n + 255) // 256 * 256
        lst = self.free.setdefault(n, [])
        if lst:
            off, key = lst.pop()
        else:
            off = (P.aoff + 15) // 16 * 16
            assert off + n <= P.alim, (name, off, n, P.alim)
            P.aoff = off + n
            P.nbuf += 1
            key = f"slot{self.sid}_{P.nbuf}"
        m = int(np.prod(shape[1:])) * (2 if dtype == F32 else 1)
        v = P.arena[:, off:off + m]
        if dtype == F32:
            v = v.bitcast(F32)
        if len(shape) == 3:
            v = v.rearrange("p (a b) -> p a b", a=shape[1])
        elif len(shape) == 4:
            v = v.rearrange("p (a b c) -> p a b c", a=shape[1], b=shape[2])
        b = Buf(v, key)
        b.slot = (n, off, key)
        return b

    def release(self, b):
        n, off, key = b.slot
        self.free[n].append((off, key))


class Prog:
    def __init__(self, nc, n_dma_sems=8):
        self.nc = nc
        self.ops = {e: [] for e in ENGS}
        self.cnt = {e: 0 for e in ENGS}
        self.sem = {e: nc.alloc_semaphore("sem_" + e) for e in ENGS}
        self.dq = {}
        for q in ("sp", "act", "pool"):
            self.dq[q] = dict(sems=[nc.alloc_semaphore(f"dsem_{q}{i}") for i in range(n_dma_sems)],
                              val=[0] * n_dma_sems, nxt=0)
        self.waited = {e: {} for e in ENGS}
        self.lastw = {}
        self.readers = {}
        self.out_events = []
        self.rings = {}
        self.arena_rings = []
        self.nbuf = 0

    def buf(self, name, shape, dtype=F32):
        self.nbuf += 1
        return Buf(self.nc.alloc_sbuf_tensor(f"{name}_{self.nbuf}", list(shape), dtype), f"{name}_{self.nbuf}")

    def ring(self, name, shape, dtype=F32, bufs=2):
        if name not in self.rings:
            self.rings[name] = Ring([self.buf(name, shape, dtype) for _ in range(bufs)])
        return self.rings[name].get()

    def init_arena(self, nelem):
        self.arena = self.nc.alloc_sbuf_tensor("arena", [128, nelem], BF16)
        self.aoff = 0
        self.alim = nelem

    def abuf(self, name, shape, dtype=F32):
        n = int(np.prod(shape[1:])) * (2 if dtype == F32 else 1)
        off = (self.aoff + 15) // 16 * 16
        assert off + n <= self.alim, (name, off, n, self.alim)
        self.aoff = off + n
        v = self.arena[:, off:off + n]
        if dtype == F32:
            v = v.bitcast(F32)
        if len(shape) == 3:
            v = v.rearrange("p (a b) -> p a b", a=shape[1])
        elif len(shape) == 4:
            v = v.rearrange("p (a b c) -> p a b c", a=shape[1], b=shape[2])
        self.nbuf += 1
        return Buf(v, f"{name}_{self.nbuf}")

    def aring(self, name, shape, dtype=F32, bufs=1):
        if name not in self.rings:
            self.rings[name] = Ring([self.abuf(name, shape, dtype) for _ in range(bufs)])
            self.arena_rings.append(name)
        return self.rings[name].get()

    def arena_reset(self, off):
        self.barrier()
        for nme in self.arena_rings:
            self.rings.pop(nme, None)
        self.arena_rings = []
        self.aoff = off

    def init_psum(self):
        self.psr = Ring([Buf(self.nc.alloc_psum_tensor(f"psb{i}", [128, 512], F32), f"psb{i}") for i in range(8)])
        self.psub = [Ring(self.psr.bufs[0:4]), Ring(self.psr.bufs[4:8])]

    def ps(self, sid=None):
        if sid is None:
            return self.psr.get()
        return self.psub[sid].get()

    def _need(self, eng, ev, waits):
        if ev is None:
            return
        sem, val, owner = ev
        if owner == "pe" and eng == "pe":
            return
        sid = id(sem)
        if self.waited[eng].get(sid, 0) >= val:
            return
        cur = waits.get(sid)
        if cur is None or cur[1] < val:
            waits[sid] = (sem, val)

    def _deps(self, eng, reads, writes):
        waits = {}
        for b in reads:
            self._need(eng, self.lastw.get(b.key), waits)
            if b.key.startswith("psb"):
                for ev in self.readers.get(b.key, ()):
                    if ev[2] != eng:
                        self._need(eng, ev, waits)
        for b in writes:
            self._need(eng, self.lastw.get(b.key), waits)
            for ev in self.readers.get(b.key, ()):
                self._need(eng, ev, waits)
        for sid, (sem, val) in waits.items():
            self.waited[eng][sid] = val
        return list(waits.values())

    def _commit(self, ev, reads, writes):
        for b in reads:
            self.readers.setdefault(b.key, []).append(ev)
        for b in writes:
            self.lastw[b.key] = ev
            self.readers[b.key] = []

    def op(self, eng, fn, reads=(), writes=()):
        waits = self._deps(eng, reads, writes)
        self.cnt[eng] += 1
        ev = (self.sem[eng], self.cnt[eng], eng)
        self.ops[eng].append((fn, waits, (self.sem[eng], 1)))
        self._commit(ev, reads, writes)
        return ev

    def dma(self, q, out, in_, reads=(), writes=(), is_output=False, slow=False):
        d = self.dq[q]
        i = d["nxt"]
        d["nxt"] = (i + 1) % len(d["sems"])
        sem = d["sems"][i]
        waits = self._deps(q, reads, writes)
        prev = d["val"][i]
        if prev > 0 and self.waited[q].get(id(sem), 0) < prev:
            waits.append((sem, prev))
            self.waited[q][id(sem)] = prev
        d["val"][i] = prev + 16
        ev = (sem, prev + 16, None)
        if slow:
            fn = lambda e: e.dma_start(out=out, in_=in_, allow_slow_non_contiguous=True)
        else:
            fn = lambda e: e.dma_start(out=out, in_=in_)
        self.ops[q].append((fn, waits, (sem, 16)))
        self._commit(ev, reads, writes)
        if is_output:
            self.out_events.append(ev)
        return ev

    def barrier(self):
        evs = [(self.sem[e], self.cnt[e]) for e in ENGS if self.cnt[e] > 0]
        for q in self.dq.values():
            for s, v in zip(q["sems"], q["val"]):
                if v > 0:
                    evs.append((s, v))
        for e in ENGS:
            waits = []
            for s, v in evs:
                if self.waited[e].get(id(s), 0) < v:
                    waits.append((s, v))
                    self.waited[e][id(s)] = v
            if waits:
                self.cnt[e] += 1
                self.ops[e].append((lambda eng: eng.nop(), waits, (self.sem[e], 1)))

    def mm(self, ps, out, lhsT, rhs, reads, start=True, stop=True):
        return self.op("pe", lambda e: e.matmul(out, lhsT=lhsT, rhs=rhs, start=start, stop=stop),
                       reads=reads, writes=[ps])

    def tr(self, ps, out, in_, ident, reads):
        return self.op("pe", lambda e: e.transpose(out, in_, ident), reads=reads, writes=[ps])

    def act(self, out, in_, func, reads, writes, scale=1.0, bias=0.0, accum_out=None):
        return self.op("act", lambda e: e.activation(out=out, in_=in_, func=func, bias=bias, scale=scale,
                                                     accum_out=accum_out), reads=reads, writes=writes)

    def ts(self, eng, out, in0, s1, op0, reads, writes, s2=None, op1=None):
        if op1 is None:
            return self.op(eng, lambda e: e.tensor_scalar(out=out, in0=in0, scalar1=s1, scalar2=None, op0=op0),
                           reads=reads, writes=writes)
        return self.op(eng, lambda e: e.tensor_scalar(out=out, in0=in0, scalar1=s1, scalar2=s2, op0=op0, op1=op1),
                       reads=reads, writes=writes)

    def tt(self, eng, out, in0, in1, op, reads, writes):
        return self.op(eng, lambda e: e.tensor_tensor(out=out, in0=in0, in1=in1, op=op), reads=reads, writes=writes)

    def stt(self, eng, out, in0, scalar, in1, op0, op1, reads, writes):
        return self.op(eng, lambda e: e.scalar_tensor_tensor(out=out, in0=in0, scalar=scalar, in1=in1, op0=op0, op1=op1),
                       reads=reads, writes=writes)

    def cp(self, eng, out, in_, reads, writes):
        if eng == "act":
            return self.act(out, in_, AF.Copy, reads, writes)
        return self.op(eng, lambda e: e.tensor_copy(out=out, in_=in_), reads=reads, writes=writes)

    def memset(self, eng, b, ap, val):
        return self.op(eng, lambda e: e.memset(ap, val), writes=[b])

    def emit(self):
        nc = self.nc
        fin = {}
        for sem, val, _ in self.out_events:
            if id(sem) not in fin or fin[id(sem)][1] < val:
                fin[id(sem)] = (sem, val)
        final_waits = list(fin.values())
        prog = self

        def run(engname, eng):
            for fn, waits, inc in prog.ops[engname]:
                for sem, val in waits:
                    eng.wait_ge(sem, val)
                fn(eng).then_inc(inc[0], inc[1])
            if engname == "sp":
                for sem, val in final_waits:
                    eng.wait_ge(sem, val)

        with nc.Block() as block:
            @block.tensor
            def _(e):
                run("pe", e)

            @block.scalar
            def _(e):
                run("act", e)

            @block.vector
            def _(e):
                run("dve", e)

            @block.gpsimd
            def _(e):
                run("pool", e)

            @block.sync
            def _(e):
                run("sp", e)


def _const_tables():
    idx = np.arange(128)
    same64 = (idx[:, None] // 64) == (idx[None, :] // 64)
    same16 = (idx[:, None] // 16) == (idx[None, :] // 16)
    cols = {}
    cols["ident"] = np.eye(128)
    cols["ones"] = np.ones((128, 128))
    for d in (0, 1):
        if d == 0:
            le = idx[:, None] <= idx[None, :]
            strict = idx[:, None] > idx[None, :]
        else:
            le = idx[:, None] >= idx[None, :]
            strict = idx[:, None] < idx[None, :]
        cols[f"U{d}"] = (le & same64) * 1.0
        cols[f"NMD{d}"] = -1.0 * (strict & same16)
        cols[f"MO{d}"] = 1.0 * (strict & same64 & ~same16)
        incl_ij = (strict | np.eye(128, dtype=bool)) & same64
        cols[f"MIT{d}"] = incl_ij.T * 1.0
    mab = np.zeros((128, 128))
    mab[:64, 0] = 1.0
    mab[64:, 1] = 1.0
    cols["mab"] = mab
    ang = 2 * np.pi * np.outer(idx, idx) / 128.0
    cols["CC"] = np.cos(ang)
    cols["nSC"] = -np.sin(ang)
    cols["SC"] = np.sin(ang)
    n = np.arange(256)
    angp = 2 * np.pi * np.outer(n, n) / 256.0
    sp = (256 * 128) ** -0.5
    cn = np.cos(angp) * sp
    sn = np.sin(angp) * sp
    for c in range(2):
        cols[f"CN{c}a"] = cn[c * 128:(c + 1) * 128, 0:128]
        cols[f"CN{c}b"] = cn[c * 128:(c + 1) * 128, 128:256]
        cols[f"SN{c}a"] = sn[c * 128:(c + 1) * 128, 0:128]
        cols[f"SN{c}b"] = sn[c * 128:(c + 1) * 128, 128:256]
    r = np.arange(64)
    a64 = 2 * np.pi * np.outer(r, r) / 64.0
    z = np.zeros((64, 64))
    bd = lambda m: np.block([[m, z], [z, m]])
    ss = (4096 * 128) ** -0.5
    cols["BDC"] = bd(np.cos(a64))
    cols["BDS"] = bd(np.sin(a64))
    cols["BDCs"] = bd(np.cos(a64)) * ss
    cols["nBDSs"] = -bd(np.sin(a64)) * ss
    names = list(cols.keys())
    arr = np.concatenate([cols[k] for k in names], axis=1).astype(np.float32)
    off = {k: i * 128 for i, k in enumerate(names)}
    return arr, off


_CST, _COFF = _const_tables()
NCST = _CST.shape[1]


def build_nc(stop_after=None, dbg=False):
    nc = bass.Bass("TRN2", target_bir_lowering=False)
    dt = nc.dram_tensor
    xin = {0: dt("xp", [1024, D], F32, kind="ExternalInput").ap(),
           1: dt("xs", [4096, D], F32, kind="ExternalInput").ap()}
    cvec = dt("cvec", [2, D], F32, kind="ExternalInput").ap()
    s0f = dt("s0f", [4, 128, 128], F32, kind="ExternalInput").ap()
    s0b = dt("s0b", [4, 128, 128], F32, kind="ExternalInput").ap()
    w_ada = dt("w_ada", [D, 6 * D], F32, kind="ExternalInput").ap()
    b_ada = dt("b_ada", [6 * D], F32, kind="ExternalInput").ap()
    g_mix = dt("g_mix", [D], F32, kind="ExternalInput").ap()
    w_in = dt("w_in", [D, NCOL], F32, kind="ExternalInput").ap()
    w_conv = dt("w_conv", [3, 1536], F32, kind="ExternalInput").ap()
    a_log = dt("a_log", [8], F32, kind="ExternalInput").ap()
    dt_bias = dt("dt_bias", [8], F32, kind="ExternalInput").ap()
    g_o = dt("g_o", [128], F32, kind="ExternalInput").ap()
    w_fno = dt("w_fno", [4, 128, 128], F32, kind="ExternalInput").ap()
    w_out = dt("w_out", [D, D], F32, kind="ExternalInput").ap()
    g_ffn = dt("g_ffn", [D], F32, kind="ExternalInput").ap()
    w_gu = dt("w_gu", [D, 2 * DFF], F32, kind="ExternalInput").ap()
    w_down = dt("w_down", [DFF, D], F32, kind="ExternalInput").ap()
    g_final = dt("g_final", [D], F32, kind="ExternalInput").ap()
    cst_d = dt("cst", [128, NCST], F32, kind="ExternalInput").ap()
    yout = {0: dt("yp", [1024, D], F32, kind="ExternalOutput").ap(),
            1: dt("ys", [4096, D], F32, kind="ExternalOutput").ap()}
    nsf = dt("nsf", [4, 4, 128, 128], F32, kind="ExternalOutput").ap()
    nsb = dt("nsb", [4, 4, 128, 128], F32, kind="ExternalOutput").ap()
    OF = dt("scr_of", [5120, 512], F32).ap()
    X1 = dt("scr_x1", [5120, D], F32).ap()
    SQ = dt("scr_sq", [40, 128, 1536], F32).ap()
    SK = dt("scr_sk", [40, 128, 1024], F32).ap()
    SL = dt("scr_sl", [40, 128, 16], F32).ap()
    dbg_out = {}

    P = Prog(nc)
    P.init_psum()
    P.init_arena(95000)

    cst = P.abuf("cst", [128, NCST])
    P.dma("sp", cst[:, :], cst_d, writes=[cst])
    C = lambda name, w=128: cst[:, _COFF[name]:_COFF[name] + w]
    ident = C("ident")
    ones = C("ones")
    cbf = P.buf("cbf", [128, 6 * 128], BF16)
    BFOFF = {}
    for i, nm in enumerate(["ident", "BDC", "BDS", "BDCs", "nBDSs"]):
        P.cp("dve", cbf[:, i * 128:(i + 1) * 128], C(nm), reads=[cst], writes=[cbf])
        BFOFF[nm] = i * 128
    CB = lambda nm: cbf[:, BFOFF[nm]:BFOFF[nm] + 128]
    cnbf = P.abuf("cnbf", [128, 1024], BF16)
    o = _COFF["CN0a"]
    P.cp("dve", cnbf[:, :], cst[:, o:o + 1024], reads=[cst], writes=[cnbf])
    AB0 = P.aoff
    CNT = lambda c: cnbf[:, c * 512:c * 512 + 256]
    SNT = lambda c: cnbf[:, c * 512 + 256:c * 512 + 512]

    gmixT = P.buf("gmixT", [128, 8])
    gffnT = P.buf("gffnT", [128, 8])
    P.dma("sp", gmixT[:, :], g_mix.rearrange("(c p) -> p c", p=128), writes=[gmixT], slow=True)
    P.dma("sp", gffnT[:, :], g_ffn.rearrange("(c p) -> p c", p=128), writes=[gffnT], slow=True)
    wconvT = P.buf("wconvT", [128, 3, 12])
    for k in range(3):
        P.dma("sp", wconvT[:, k, :], w_conv[k].rearrange("(c p) -> p c", p=128), writes=[wconvT], slow=True)
    badaT = P.buf("badaT", [128, 48])
    P.dma("sp", badaT[:, :], b_ada.rearrange("(c p) -> p c", p=128), writes=[badaT], slow=True)
    cT = P.buf("cT", [128, 2, 8])
    for t in range(2):
        P.dma("sp", cT[:, t, :], cvec[t].rearrange("(c p) -> p c", p=128), writes=[cT], slow=True)
    alog_bc = P.buf("alog", [128, 8])
    dtb_bc = P.buf("dtb", [128, 8])
    P.dma("sp", alog_bc[:, :], a_log.partition_broadcast(128), writes=[alog_bc])
    P.dma("sp", dtb_bc[:, :], dt_bias.partition_broadcast(128), writes=[dtb_bc])
    go_bc = P.buf("go", [128, 128])
    P.dma("sp", go_bc[:, :], g_o.partition_broadcast(128), writes=[go_bc])
    bgate = P.abuf("bgate", [128, 2, D])
    P.dma("sp", bgate[:, 0, :], b_ada[2 * D:3 * D].partition_broadcast(128), writes=[bgate])
    P.dma("sp", bgate[:, 1, :], b_ada[5 * D:6 * D].partition_broadcast(128), writes=[bgate])
    eps_c = P.buf("epsc", [128, 1])
    P.memset("dve", eps_c, eps_c[:, :], EPS)
    nexpA = P.buf("nexpA", [128, 8])
    P.act(nexpA[:, :], alog_bc[:, :], AF.Exp, reads=[alog_bc], writes=[nexpA])
    P.ts("dve", nexpA[:, :], nexpA[:, :], -1.0, ALU.mult, reads=[nexpA], writes=[nexpA])

    scT = P.buf("scT", [128, 8, 2])
    for t in range(2):
        P.act(scT[:, :, t], cT[:, t, :], AF.Silu, reads=[cT], writes=[scT])
    scbc = P.abuf("scbc", [128, 2, 8, 128])
    for t in range(2):
        P.cp("dve", scbc[:, t, :, :], scT[:, :, t].unsqueeze(2).to_broadcast([128, 8, 128]), reads=[scT], writes=[scbc])
    modT = P.buf("modT", [128, 48, 2])
    gates = P.buf("gates", [128, 2, 2, D])
    for blk in range(12):
        wst = P.aring("wada_st", [128, 8, 512], F32, bufs=2)
        P.dma("sp" if blk % 2 == 0 else "act", wst[:, :, :],
              w_ada[:, blk * 512:(blk + 1) * 512].rearrange("(c p) n -> p c n", p=128), writes=[wst])
        ps = P.ps()
        for fc in range(4):
            for kc in range(8):
                P.mm(ps, ps[:, fc * 2:fc * 2 + 2], wst[:, kc, fc * 128:(fc + 1) * 128], scT[:, kc, :],
                     reads=[wst, scT], start=(kc == 0), stop=(kc == 7))
        P.tt("dve", modT[:, blk * 4:blk * 4 + 4, :], ps[:, 0:8].rearrange("p (f t) -> p f t", t=2),
             badaT[:, blk * 4:blk * 4 + 4].unsqueeze(2).to_broadcast([128, 4, 2]), ALU.add,
             reads=[ps, badaT], writes=[modT])
        gsel = {4: (0, 0), 5: (0, 1), 10: (1, 0), 11: (1, 1)}.get(blk)
        if gsel is not None:
            gi, half = gsel
            for t in range(2):
                ps2 = P.ps()
                for kc in range(8):
                    P.mm(ps2, ps2[:, :], scbc[:, t, kc, :], wst[:, kc, :], reads=[scbc, wst],
                         start=(kc == 0), stop=(kc == 7))
                P.tt("dve", gates[:, t, gi, half * 512:(half + 1) * 512], ps2[:, :],
                     bgate[:, gi, half * 512:(half + 1) * 512], ALU.add, reads=[ps2, bgate], writes=[gates])
    s1T = P.buf("s1T", [128, 8, 2]); sh1T = P.buf("sh1T", [128, 8, 2])
    s2T = P.buf("s2T", [128, 8, 2]); sh2T = P.buf("sh2T", [128, 8, 2])
    for (sT, shT, gT, base) in ((s1T, sh1T, gmixT, 0), (s2T, sh2T, gffnT, 24)):
        P.cp("dve", shT[:, :, :], modT[:, base:base + 8, :], reads=[modT], writes=[shT])
        P.ts("dve", sT[:, :, :], modT[:, base + 8:base + 16, :], 1.0, ALU.add, reads=[modT], writes=[sT])
        P.tt("dve", sT[:, :, :], sT[:, :, :], gT[:, :].unsqueeze(2).to_broadcast([128, 8, 2]), ALU.mult,
             reads=[sT, gT], writes=[sT])
    if dbg:
        dbg_out["modT"] = dt("dbg_modT", [128, 96], F32, kind="ExternalOutput").ap()
        P.dma("sp", dbg_out["modT"], modT[:, :, :].rearrange("p f t -> p (f t)"), reads=[modT], is_output=True)
        dbg_out["gates"] = dt("dbg_gates", [128, 4 * D], F32, kind="ExternalOutput").ap()
        P.dma("sp", dbg_out["gates"], gates[:, :, :, :].rearrange("p a b d -> p (a b d)"), reads=[gates], is_output=True)
    if stop_after == "mod":
        P.emit()
        return nc

    class Key:
        def __init__(s, k):
            s.key = k

    P.arena_reset(AB0)
    winb = P.abuf("winb", [128, 8, NCOL], BF16)
    woutb = P.abuf("woutb", [128, 8, D], BF16)
    wfs = P.abuf("wfs", [128, 4, 128])
    W1 = P.abuf("W1", [128, 4, 256], BF16)
    W2 = P.abuf("W2", [128, 4, 256], BF16)
    eps128 = P.abuf("eps128", [128, 1])
    zt = P.abuf("zt", [128, 8, 1], BF16)
    Sst = [P.abuf("Sf", [128, 4, 128]), P.abuf("Sb", [128, 4, 128])]
    SCR0 = P.aoff
    for kc in range(8):
        st = P.aring("wst", [128, NCOL], F32, bufs=2)
        P.dma("sp" if kc % 2 == 0 else "act", st[:, :], w_in[kc * 128:(kc + 1) * 128, :], writes=[st])
        P.cp("dve" if kc % 2 == 0 else "pool", winb[:, kc, :], st[:, :], reads=[st], writes=[winb])
    for kc in range(8):
        st = P.aring("wst2", [128, D], F32, bufs=2)
        P.dma("sp" if kc % 2 == 0 else "act", st[:, :], w_out[kc * 128:(kc + 1) * 128, :], writes=[st])
        P.cp("dve" if kc % 2 == 0 else "pool", woutb[:, kc, :], st[:, :], reads=[st], writes=[woutb])
    P.dma("sp", wfs[:, :, :], w_fno.rearrange("g l d -> l g d"), writes=[wfs])
    for (tab, dsts) in (("CC", ((W1, 0), (W2, 128))), ("SC", ((W1, 128),)), ("nSC", ((W2, 0),))):
        ps = P.ps()
        for g in range(4):
            P.mm(ps, ps[:, g * 128:(g + 1) * 128], C(tab), wfs[:, g, :], reads=[cst, wfs])
        for (Wt, o_) in dsts:
            P.cp("dve", Wt[:, :, o_:o_ + 128], ps[:, :].rearrange("p (g d) -> p g d", g=4), reads=[ps], writes=[Wt])
    P.memset("dve", eps128, eps128[:, :], 128.0 * EPS)
    P.memset("dve", zt, zt[:, :, :], 0.0)
    mk = lambda nm: C(nm).unsqueeze(1).to_broadcast([128, 4, 128])
    v3 = lambda ps_: ps_[:, :].rearrange("p (h j) -> p h j", h=4)
    bc4 = lambda ap: ap.unsqueeze(2).to_broadcast([128, 4, 128])
    dq = ["sp", "act"]

    seqs = [(0, s, 256, 0) for s in range(4)] + [(1, 0, 4096, 1)]
    if stop_after == "seq0":
        seqs = seqs[:1]
    if stop_after in ("seq4", "seq4_a0", "seq4_f"):
        seqs = seqs[4:]
    if stop_after == "ffnonly":
        seqs = []
    tokbase = 0
    for (grp, si, N, cond) in seqs:
        xr = xin[grp][si * N:(si + 1) * N, :]
        NTL = N // 128
        HT = dt(f"ht_{grp}_{si}", [128, 8, N + 2], BF16).ap()
        YTD = dt(f"ytd_{grp}_{si}", [128, 4, N], BF16).ap()
        htk = [Key(f"HT{grp}{si}_{b}") for b in range(NTL)]
        hk0 = Key(f"HTz{grp}{si}")
        ytk = Key(f"YT{grp}{si}")
        P.arena_reset(SCR0)
        P.dma("sp", HT[:, :, 0:1], zt[:, :, :], reads=[zt], writes=[hk0], slow=True)
        P.dma("sp", HT[:, :, N + 1:N + 2], zt[:, :, :], reads=[zt], writes=[hk0], slow=True)
        for b in range(NTL):
            xt = P.aring("xt", [128, D], F32, bufs=2)
            P.dma("sp", xt[:, :], xr[b * 128:(b + 1) * 128, :], writes=[xt])
            junk = P.aring("junk", [128, D], F32)
            ss = P.ring("ss", [128, 1], F32, bufs=2)
            P.act(junk[:, :], xt[:, :], AF.Square, reads=[xt], writes=[junk, ss], accum_out=ss[:, :])
            P.act(ss[:, :], ss[:, :], AF.Ln, reads=[ss, eps_c], writes=[ss], scale=1.0 / D, bias=eps_c[:, :])
            P.act(ss[:, :], ss[:, :], AF.Exp, reads=[ss], writes=[ss], scale=-0.5)
            xnb = P.aring("xnb", [128, D], BF16)
            P.act(xnb[:, :], xt[:, :], AF.Copy, reads=[xt, ss], writes=[xnb], scale=ss[:, :])
            ps = P.ps()
            psb = ps.t[:, :].bitcast(BF16)
            for kc in range(8):
                P.tr(ps, psb[:, kc * 128:(kc + 1) * 128], xnb[:, kc * 128:(kc + 1) * 128], CB("ident"), reads=[xnb, cbf])
            tmp = P.aring("httmp", [128, 8, 128], F32)
            P.tt("dve", tmp[:, :, :], psb.rearrange("p (k t) -> p k t", k=8),
                 s1T[:, :, cond].unsqueeze(2).to_broadcast([128, 8, 128]), ALU.mult, reads=[ps, s1T], writes=[tmp])
            hb = P.aring("hb", [128, 8, 128], BF16, bufs=2)
            P.tt("pool", hb[:, :, :], tmp[:, :, :], sh1T[:, :, cond].unsqueeze(2).to_broadcast([128, 8, 128]), ALU.add,
                 reads=[tmp, sh1T], writes=[hb])
            P.dma("pool", HT[:, :, 1 + b * 128:1 + (b + 1) * 128], hb[:, :, :], reads=[hb], writes=[htk[b]])

        def load_ht(t):
            ht = P.aring("ht", [128, 8, 130], BF16, bufs=2)
            rk = [hk0] + [htk[j] for j in (t - 1, t, t + 1) if 0 <= j < NTL]
            P.dma("sp", ht[:, :, :], HT[:, :, t * 128:t * 128 + 130], reads=rk, writes=[ht])
            return ht

        if stop_after == "seq4_a0":
            dbg_out["o"] = dt("dbg_o", [128, 8], F32, kind="ExternalOutput").ap()
            P.dma("sp", dbg_out["o"], gmixT[:, :], reads=[gmixT] + htk, is_output=True)
            P.emit()
            return nc
        P.arena_reset(SCR0)
        if N == 256:
            fbs = []
            for blk in range(2):
                ht = load_ht(blk)
                ps = P.ps()
                for kc in range(8):
                    P.mm(ps, ps[:, :], ht[:, kc, 1:129], winb[:, kc, 2064:2576], reads=[ht, winb], start=(kc == 0), stop=(kc == 7))
                fb = P.abuf("fb", [128, 512], BF16)
                P.cp("act", fb[:, :], ps[:, :], reads=[ps], writes=[fb])
                fbs.append(fb)
            ytg = P.abuf("ytg", [128, 4, 256], BF16)
            for g in range(4):
                psA = P.ps()
                for blk in range(2):
                    P.mm(psA, psA[:, 0:256], fbs[blk][:, g * 128:(g + 1) * 128], CNT(blk), reads=[fbs[blk], cnbf], start=(blk == 0), stop=(blk == 1))
                for blk in range(2):
                    P.mm(psA, psA[:, 256:512], fbs[blk][:, g * 128:(g + 1) * 128], SNT(blk), reads=[fbs[blk], cnbf], start=(blk == 0), stop=(blk == 1))
                Ab = P.aring("Ab", [128, 512], BF16, bufs=2)
                P.cp("dve", Ab[:, :], psA[:, :], reads=[psA], writes=[Ab])
                psY = P.ps()
                P.mm(psY, psY[:, 0:256], W1[:, g, 0:128], Ab[:, 0:256], reads=[W1, Ab], start=True, stop=False)
                P.mm(psY, psY[:, 0:256], W2[:, g, 0:128], Ab[:, 256:512], reads=[W2, Ab], start=False, stop=True)
                P.cp("act", ytg[:, g, :], psY[:, 0:256], reads=[psY], writes=[ytg])
            P.dma("act", YTD[:, :, :], ytg[:, :, :], reads=[ytg], writes=[ytk])
        else:
            A_all = P.abuf("A_all", [128, 2, 4, 4096], BF16)
            for rp in range(32):
                ht = load_ht(rp)
                ps = P.ps()
                for kc in range(8):
                    P.mm(ps, ps[:, :], ht[:, kc, 1:129], winb[:, kc, 2064:2576], reads=[ht, winb], start=(kc == 0), stop=(kc == 7))
                fb = P.aring("fb", [128, 512], BF16, bufs=2)
                P.cp("act", fb[:, :], ps[:, :], reads=[ps], writes=[fb])
                for gp in range(2):
                    ps2 = P.ps()
                    for gg in range(2):
                        g = gp * 2 + gg
                        P.mm(ps2, ps2[:, gg * 256:gg * 256 + 128], fb[:, g * 128:(g + 1) * 128], CB("BDC"), reads=[fb, cbf])
                        P.mm(ps2, ps2[:, gg * 256 + 128:gg * 256 + 256], fb[:, g * 128:(g + 1) * 128], CB("BDS"), reads=[fb, cbf])
                    for ri in range(2):
                        dst = A_all[:, ri, gp * 2:gp * 2 + 2, :].rearrange("p g (kc r) -> p g kc r", r=64)[:, :, :, 2 * rp:2 * rp + 2]
                        dst = dst.rearrange("p g kc b -> p g b kc")
                        src = ps2[:, :].rearrange("p (g ri b kc) -> p ri g b kc", g=2, ri=2, b=2)[:, ri]
                        P.cp("dve" if ri == 0 else "act", dst, src, reads=[ps2], writes=[A_all])
            for g in range(4):
                ytg = P.aring("ytg", [128, 4096], BF16, bufs=1)
                def s3(kcp_, Bb_):
                    psY = P.ps()
                    P.mm(psY, psY[:, 0:128], Bb_[:, 0:128], CB("BDCs"), reads=[Bb_, cbf], start=True, stop=False)
                    P.mm(psY, psY[:, 0:128], Bb_[:, 128:256], CB("nBDSs"), reads=[Bb_, cbf], start=False, stop=True)
                    P.cp("dve", ytg[:, :].rearrange("p (kr kc) -> p kc kr", kc=64)[:, 2 * kcp_:2 * kcp_ + 2, :],
                         psY[:, 0:128].rearrange("p (a k) -> p a k", a=2), reads=[psY], writes=[ytg])
                pendf = None
                for kcp in range(32):
                    sub = lambda ri: A_all[:, ri, g, kcp * 128:(kcp + 1) * 128]
                    psB = P.ps()
                    P.mm(psB, psB[:, 0:256], sub(0), W1[:, g, :], reads=[A_all, W1], start=True, stop=False)
                    P.mm(psB, psB[:, 0:256], sub(1), W2[:, g, :], reads=[A_all, W2], start=False, stop=True)
                    Bb = P.aring("Bb", [128, 256], BF16, bufs=3)
                    P.cp("act", Bb[:, :], psB[:, 0:256], reads=[psB], writes=[Bb])
                    if pendf is not None:
                        s3(*pendf)
                    pendf = (kcp, Bb)
                s3(*pendf)
                P.dma("act", YTD[:, g, :], ytg[:, :], reads=[ytg], writes=[ytk])

        if stop_after == "seq4_f":
            dbg_out["o"] = dt("dbg_o", [128, 8], F32, kind="ExternalOutput").ap()
            P.dma("sp", dbg_out["o"], gmixT[:, :], reads=[gmixT, ytk], is_output=True)
            P.emit()
            return nc
        P.arena_reset(SCR0)
        for T2 in range(N // 256):
            ht2 = P.aring("ht2", [128, 8, 258], BF16, bufs=2)
            rk = [hk0] + [htk[j] for j in range(2 * T2 - 1, 2 * T2 + 3) if 0 <= j < NTL]
            P.dma("sp", ht2[:, :, :], HT[:, :, T2 * 256:T2 * 256 + 258], reads=rk, writes=[ht2])
            q2 = P.aring("q2", [128, 12, 256], F32, bufs=2)
            pend = None
            for cc in range(12):
                ps = P.ps()
                for kc in range(8):
                    P.mm(ps, ps[:, 0:258], winb[:, kc, cc * 128:(cc + 1) * 128], ht2[:, kc, :], reads=[winb, ht2], start=(kc == 0), stop=(kc == 7))
                c1 = P.aring("fc1", [128, 256], bufs=3)
                c2 = P.aring("fc2", [128, 256], bufs=3)
                P.act(c1[:, :], ps[:, 0:256], AF.Copy, reads=[ps, wconvT], writes=[c1], scale=wconvT[:, 0, cc:cc + 1])
                P.stt("dve", c2[:, :], ps[:, 1:257], wconvT[:, 1, cc:cc + 1], c1[:, :], ALU.mult, ALU.add, reads=[ps, c1, wconvT], writes=[c2])
                P.stt("dve", c1[:, :], ps[:, 2:258], wconvT[:, 2, cc:cc + 1], c2[:, :], ALU.mult, ALU.add, reads=[ps, c2, wconvT], writes=[c1])
                if pend is not None:
                    P.act(q2[:, pend[1], :], pend[0][:, :], AF.Silu, reads=[pend[0]], writes=[q2])
                pend = (c1, cc)
            P.act(q2[:, pend[1], :], pend[0][:, :], AF.Silu, reads=[pend[0]], writes=[q2])
            sq = P.aring("fsq", [128, 8, 256], F32, bufs=2)
            P.act(sq[:, :, :], q2[:, 0:8, :], AF.Square, reads=[q2], writes=[sq])
            rn = P.aring("frn", [128, 8, 256], F32, bufs=2)
            for pair in range(4):
                ps = P.ps()
                for j in range(2):
                    P.mm(ps, ps[:, j * 256:(j + 1) * 256], ones, sq[:, pair * 2 + j, :], reads=[cst, sq])
                P.act(rn[:, pair * 2:pair * 2 + 2, :], ps[:, :].rearrange("p (a b) -> p a b", a=2), AF.Ln, reads=[ps, eps128, eps_c], writes=[rn],
                      scale=(128.0 if pair < 2 else 1.0), bias=(eps128 if pair < 2 else eps_c)[:, :])
            P.act(rn[:, :, :], rn[:, :, :], AF.Exp, reads=[rn], writes=[rn], scale=-0.5)
            P.tt("dve", q2[:, 0:8, :], q2[:, 0:8, :], rn[:, :, :], ALU.mult, reads=[q2, rn], writes=[q2])
            for blk in range(2):
                gtp = tokbase // 128 + 2 * T2 + blk
                bs = slice(blk * 128, (blk + 1) * 128)
                knv = P.aring("fknv", [128, 2, 4, 128], F32, bufs=2)
                for which, base in ((0, 4), (1, 8)):
                    ps = P.ps()
                    for j in range(4):
                        P.tr(ps, ps[:, j * 128:(j + 1) * 128], q2[:, base + j, bs], ident, reads=[q2, cst])
                    P.cp("act", knv[:, which, :, :], v3(ps), reads=[ps], writes=[knv])
                P.dma("act", SK[gtp].rearrange("p (a h j) -> p a h j", a=2, h=4), knv[:, :, :, :], reads=[knv], writes=[Key(f"SHk{gtp}")])
                ps = P.ps()
                for kc in range(8):
                    P.mm(ps, ps[:, 0:16], ht2[:, kc, 1 + blk * 128:129 + blk * 128], winb[:, kc, 2048:2064], reads=[ht2, winb], start=(kc == 0), stop=(kc == 7))
                lab = P.aring("flab", [128, 16], F32, bufs=2)
                P.tt("dve", lab[:, 0:8], ps[:, 0:8], dtb_bc[:, :], ALU.add, reads=[ps, dtb_bc], writes=[lab])
                P.act(lab[:, 8:16], ps[:, 8:16], AF.Exp, reads=[ps], writes=[lab], scale=-1.0)
                P.act(lab[:, 0:8], lab[:, 0:8], AF.Exp, reads=[lab], writes=[lab])
                P.act(lab[:, 0:16], lab[:, 0:16], AF.Ln, reads=[lab], writes=[lab], bias=1.0)
                P.act(lab[:, 8:16], lab[:, 8:16], AF.Exp, reads=[lab], writes=[lab], scale=-1.0)
                P.tt("dve", lab[:, 0:8], lab[:, 0:8], nexpA[:, :], ALU.mult, reads=[lab, nexpA], writes=[lab])
                P.dma("act", SL[gtp], lab[:, :], reads=[lab], writes=[Key(f"SHl{gtp}")])
                P.dma("pool", SQ[gtp][:, 0:1024].rearrange("p (c t) -> p c t", c=8), q2[:, 0:8, bs], reads=[q2], writes=[Key(f"SHq{gtp}")])
        P.arena_reset(SCR0)
        streams = [Stream(P, 0), Stream(P, 1)]
        for d in (0, 1):
            S = Sst[d]
            if grp == 0:
                P.memset("pool", S, S[:, :, :], 0.0)
            else:
                P.dma("sp", S[:, :, :], (s0f if d == 0 else s0b).rearrange("h k v -> k h v"), writes=[S])
            streams[d].u = streams[d].alloc("u", [128, 4, 128])
            P.memset("pool", streams[d].u, streams[d].u[:, :, :], 0.0)
            streams[d].oS = streams[d].alloc("oS", [128, 4, 128])

        def tile_gen(st, t, d, second):
            A = st.alloc
            R = st.release
            PS = lambda: P.ps(d)
            S = Sst[d]
            u, oS = st.u, st.oS
            gt_ = tokbase // 128 + t
            ht = None
            if second:
                ht = A("ht", [128, 8, 130], BF16)
            kT = A("kT", [128, 4, 128])
            qT = A("qT", [128, 4, 128])
            knv = A("knv", [128, 2, 4, 128])
            lab = A("lab", [128, 16])
            P.dma("sp", lab[:, :], SL[gt_], reads=[Key(f"SHl{gt_}")], writes=[lab])
            P.dma("sp", kT[:, :, :], SQ[gt_][:, 512:1024].rearrange("p (c t) -> p c t", c=4), reads=[Key(f"SHq{gt_}")], writes=[kT])
            P.dma("sp", qT[:, :, :], SQ[gt_][:, 0:512].rearrange("p (c t) -> p c t", c=4), reads=[Key(f"SHq{gt_}")], writes=[qT])
            P.dma("sp", knv[:, :, :, :], SK[gt_].rearrange("p (a h j) -> p a h j", a=2, h=4), reads=[Key(f"SHk{gt_}")], writes=[knv])
            if second:
                rk = [hk0] + [htk[j] for j in (t - 1, t, t + 1) if 0 <= j < NTL]
                P.dma("sp", ht[:, :, :], HT[:, :, t * 128:t * 128 + 130], reads=rk, writes=[ht])
            yield
            la = lab[:, d * 4:(d + 1) * 4]
            be = lab[:, 8 + d * 4:8 + (d + 1) * 4]
            yield
            labc = A("labc", [128, 4, 128])
            P.cp("pool", labc[:, :, :], bc4(la), reads=[lab], writes=[labc])
            psg = PS()
            for h in range(4):
                P.mm(psg, psg[:, h * 128:(h + 1) * 128], labc[:, h, :], C(f"U{d}"), reads=[labc, cst])
            R(labc)
            psc = PS()
            P.mm(psc, psc[:, 0:4], C(f"U{d}"), la, reads=[cst, lab])
            sm = A("sm", [128, 9, 4])
            P.cp("dve", sm[:, 0, :], psc[:, 0:4], reads=[psc], writes=[sm])
            P.ts("dve", sm[:, 8, :], sm[:, 0, :], -1.0, ALU.mult, reads=[sm], writes=[sm])
            li = (63, 127) if d == 0 else (0, 64)
            P.cp("dve", sm[:, 6, :], v3(psg)[:, :, li[0]], reads=[psg], writes=[sm])
            P.cp("dve", sm[:, 7, :], v3(psg)[:, :, li[1]], reads=[psg], writes=[sm])
            Dm = A("Dm", [128, 4, 128])
            DT = A("DT", [128, 4, 128])
            for h in range(4):
                P.act(Dm[:, h, :], psg[:, h * 128:(h + 1) * 128], AF.Relu, reads=[psg, sm], writes=[Dm], scale=1.0, bias=sm[:, 8, h:h + 1])
                P.act(DT[:, h, :], psg[:, h * 128:(h + 1) * 128], AF.Relu, reads=[psg, sm], writes=[DT], scale=-1.0, bias=sm[:, 0, h:h + 1])
            P.act(Dm[:, :, :], Dm[:, :, :], AF.Exp, reads=[Dm], writes=[Dm], scale=-1.0)
            P.act(DT[:, :, :], DT[:, :, :], AF.Exp, reads=[DT], writes=[DT], scale=-1.0)
            m0 = C("mab")[:, 0:1]
            m1 = C("mab")[:, 1:2]
            P.ts("dve", sm[:, 2, :], sm[:, 6, :], m0, ALU.mult, reads=[sm, cst], writes=[sm])
            P.stt("dve", sm[:, 2, :], sm[:, 7, :], m1, sm[:, 2, :], ALU.mult, ALU.add, reads=[sm, cst], writes=[sm])
            P.tt("dve", sm[:, 2, :], sm[:, 2, :], sm[:, 0, :], ALU.subtract, reads=[sm], writes=[sm])
            P.act(sm[:, 2, :], sm[:, 2, :], AF.Exp, reads=[sm], writes=[sm])
            P.ts("dve", sm[:, 3, :], sm[:, 2, :], m0, ALU.mult, reads=[sm, cst], writes=[sm])
            P.ts("dve", sm[:, 4, :], sm[:, 2, :], m1, ALU.mult, reads=[sm, cst], writes=[sm])
            P.act(sm[:, 1, :], sm[:, 0, :], AF.Exp, reads=[sm], writes=[sm])
            P.tt("dve", sm[:, 5, :], sm[:, 1, :], be, ALU.mult, reads=[sm, lab], writes=[sm])
            P.act(sm[:, 6:8, :], sm[:, 6:8, :], AF.Exp, reads=[sm], writes=[sm])
            yield
            psk = PS()
            for h in range(4):
                P.mm(psk, psk[:, h * 128:(h + 1) * 128], kT[:, h, :], kT[:, h, :], reads=[kT])
            t2 = A("t2", [128, 4, 128])
            P.tt("dve", t2[:, :, :], Dm[:, :, :], v3(psk), ALU.mult, reads=[Dm, psk], writes=[t2])
            R(Dm)
            P.tt("pool", t2[:, :, :], t2[:, :, :], bc4(be), ALU.mult, reads=[t2, lab], writes=[t2])
            Nn = A("Nn", [128, 4, 128])
            Aoff = A("Aoff", [128, 4, 128])
            P.tt("pool", Nn[:, :, :], t2[:, :, :], mk(f"NMD{d}"), ALU.mult, reads=[t2, cst], writes=[Nn])
            P.tt("dve", Aoff[:, :, :], t2[:, :, :], mk(f"MO{d}"), ALU.mult, reads=[t2, cst], writes=[Aoff])
            R(t2)
            psq = PS()
            for h in range(4):
                P.mm(psq, psq[:, h * 128:(h + 1) * 128], kT[:, h, :], qT[:, h, :], reads=[kT, qT])
            aqkT = A("aqkT", [128, 4, 128])
            P.tt("dve", aqkT[:, :, :], DT[:, :, :], v3(psq), ALU.mult, reads=[DT, psq], writes=[aqkT])
            R(DT)
            P.tt("pool", aqkT[:, :, :], aqkT[:, :, :], mk(f"MIT{d}"), ALU.mult, reads=[aqkT, cst], writes=[aqkT])
            yield
            RHS = A("RHS", [128, 4, 256])
            P.tt("pool", RHS[:, :, 0:128], knv[:, 1, :, :], bc4(be), ALU.mult, reads=[knv, lab], writes=[RHS])
            P.tt("pool", RHS[:, :, 128:256], knv[:, 0, :, :], bc4(sm[:, 5, :]), ALU.mult, reads=[knv, sm], writes=[RHS])
            kdA = A("kdA", [128, 4, 128])
            kdB = A("kdB", [128, 4, 128])
            P.tt("pool", kdA[:, :, :], knv[:, 0, :, :], bc4(sm[:, 3, :]), ALU.mult, reads=[knv, sm], writes=[kdA])
            P.tt("pool", kdB[:, :, :], knv[:, 0, :, :], bc4(sm[:, 4, :]), ALU.mult, reads=[knv, sm], writes=[kdB])
            R(knv)
            def mm4(lh, rh):
                ps_ = PS()
                for h in range(4):
                    P.mm(ps_, ps_[:, h * 128:(h + 1) * 128], lh[:, h, :], rh[:, h, :], reads=[lh, rh])
                return ps_

            def step2(lh, rh, evac, tr=False):
                for hp in range(2):
                    ps_ = PS()
                    for j in range(2):
                        h = hp * 2 + j
                        if tr:
                            P.tr(ps_, ps_[:, j * 128:(j + 1) * 128], lh(h), ident, reads=[rh, cst])
                        else:
                            P.mm(ps_, ps_[:, j * 128:(j + 1) * 128], lh[:, h, :], rh[:, h, :], reads=[lh, rh])
                    evac(hp, slice(hp * 2, hp * 2 + 2), ps_[:, 0:256].rearrange("p (j n) -> p j n", j=2), ps_)

            def ev_copy(dst, eng_pair=("act", "dve")):
                return lambda hp, hs, pv, ps_: P.cp(eng_pair[hp], dst[:, hs, :], pv, reads=[ps_], writes=[dst])

            def ev_add(dst, other):
                return lambda hp, hs, pv, ps_: P.tt("dve", dst[:, hs, :], pv, other[:, hs, :], ALU.add, reads=[ps_, other], writes=[dst])

            NTt = A("NTt", [128, 4, 128])
            TTa = A("TTa", [128, 4, 128])
            step2(lambda h: Nn[:, h, :], Nn, ev_copy(NTt, ("act", "act")), tr=True)
            P.tt("pool", TTa[:, :, :], NTt[:, :, :], mk("ident"), ALU.add, reads=[NTt, cst], writes=[TTa])
            yield
            Pa = A("Pa", [128, 4, 128])
            PaT = A("PaT", [128, 4, 128])
            step2(NTt, Nn, ev_copy(Pa))
            step2(Nn, NTt, ev_copy(PaT, ("dve", "act")))
            R(Nn); R(NTt)
            yield
            TTb = A("TTb", [128, 4, 128])
            step2(Pa, TTa, ev_add(TTb, TTa))
            Pb = A("Pb", [128, 4, 128])
            PbT = A("PbT", [128, 4, 128])
            step2(PaT, Pa, ev_copy(Pb))
            yield
            step2(Pa, PaT, ev_copy(PbT, ("dve", "act")))
            R(PaT)
            step2(Pb, TTb, ev_add(TTa, TTb))
            yield
            step2(PbT, Pb, ev_copy(Pa))
            R(PbT)
            step2(Pa, TTa, ev_add(TTb, TTa))
            R(TTa); R(Pa)
            yield
            step2(Aoff, TTb, lambda hp, hs, pv, ps_: P.act(Pb[:, hs, :], pv, AF.Copy, reads=[ps_], writes=[Pb], scale=-1.0))
            R(Aoff)
            nGT = Pb
            Xs = [A("Xa", [128, 4, 256]), A("Xb", [128, 4, 256])]

            XS1 = A("XS1", [128, 4, 256])

            def xsweep(dst, src):
                for hp in range(2):
                    ps_ = PS()
                    for j in range(2):
                        h = hp * 2 + j
                        if src is None:
                            P.mm(ps_, ps_[:, j * 256:(j + 1) * 256], TTb[:, h, :], RHS[:, h, :], reads=[TTb, RHS])
                        else:
                            P.mm(ps_, ps_[:, j * 256:(j + 1) * 256], nGT[:, h, :], src[:, h, :], reads=[nGT, src])
                    pv = ps_[:, :].rearrange("p (j n) -> p j n", j=2)
                    hsl = slice(hp * 2, hp * 2 + 2)
                    if src is None:
                        P.cp("act" if hp == 0 else "dve", dst[:, hsl, :], pv, reads=[ps_], writes=[dst])
                    else:
                        P.tt("dve", dst[:, hsl, :], pv, XS1[:, hsl, :], ALU.add, reads=[ps_, XS1], writes=[dst])
            xsweep(XS1, None)
            yield
            xsweep(Xs[0], XS1)
            yield
            xsweep(Xs[1], Xs[0])
            yield
            xsweep(Xs[0], Xs[1])
            Xs = [Xs[1], Xs[0]]
            Xc = Xs[1]
            R(TTb); R(RHS); R(Xs[0]); R(nGT); R(XS1)
            wT = A("wT", [128, 4, 128])
            step2(lambda h: Xc[:, h, 128:256], Xc, ev_copy(wT, ("act", "act")), tr=True)
            yield
            for c in ((0, 1) if d == 0 else (1, 0)):
                r0, r1 = c * 64, (c + 1) * 64
                ps = mm4(wT, S)
                P.tt("dve", u[r0:r1, :, :], Xc[r0:r1, :, 0:128], v3(ps)[r0:r1], ALU.subtract, reads=[Xc, ps], writes=[u])
                ps2 = mm4(qT, S)
                P.tt("dve", oS[r0:r1, :, :], v3(ps2)[r0:r1], sm[r0:r1, 1, :].unsqueeze(2).to_broadcast([64, 4, 128]), ALU.mult,
                     reads=[ps2, sm], writes=[oS])
                yield
                ps3 = mm4(kdA if c == 0 else kdB, u)
                for h in range(4):
                    P.stt("dve", S[:, h, :], S[:, h, :], sm[:, 6 + c, h:h + 1], ps3[:, h * 128:(h + 1) * 128], ALU.mult, ALU.add,
                          reads=[S, sm, ps3], writes=[S])
                yield
            ps = mm4(aqkT, u)
            ot = A("ot", [128, 4, 128])
            P.tt("dve", ot[:, :, :], oS[:, :, :], v3(ps), ALU.add, reads=[oS, ps], writes=[ot])
            for b_ in (qT, kT, aqkT, kdA, kdB, Xc, wT, sm):
                R(b_)
            ofk = Key(f"OF{gt_}")
            if not second:
                P.dma("act", OF[gt_ * 128:(gt_ + 1) * 128, :], ot[:, :, :].rearrange("p h v -> p (h v)"), reads=[ot], writes=[ofk])
                R(ot); R(lab)
                return
            yield
            of = A("of", [128, 4, 128])
            P.dma("sp", of[:, :, :], OF[gt_ * 128:(gt_ + 1) * 128, :].rearrange("p (h v) -> p h v", h=4), reads=[ofk], writes=[of])
            P.tt("pool", ot[:, :, :], ot[:, :, :], of[:, :, :], ALU.add, reads=[ot, of], writes=[ot])
            R(of)
            osq = A("osq", [128, 4, 128])
            P.act(osq[:, :, :], ot[:, :, :], AF.Square, reads=[ot], writes=[osq])
            rs = A("rs", [128, 4])
            P.op("dve", (lambda o_, i_: lambda e: e.tensor_reduce(out=o_, in_=i_, axis=AX.X, op=ALU.add))(rs[:, :], osq[:, :, :]),
                 reads=[osq], writes=[rs])
            R(osq)
            P.act(rs[:, :], rs[:, :], AF.Ln, reads=[rs, eps_c], writes=[rs], scale=1.0 / 128, bias=eps_c[:, :])
            P.act(rs[:, :], rs[:, :], AF.Exp, reads=[rs], writes=[rs], scale=-0.5)
            P.tt("dve", ot[:, :, :], ot[:, :, :], bc4(rs[:, :]), ALU.mult, reads=[ot, rs], writes=[ot])
            P.tt("pool", ot[:, :, :], ot[:, :, :], go_bc[:, :].unsqueeze(1).to_broadcast([128, 4, 128]), ALU.mult, reads=[ot, go_bc], writes=[ot])
            R(rs)
            psz = PS()
            for kc in range(8):
                P.mm(psz, psz[:, :], ht[:, kc, 1:129], winb[:, kc, 1536:2048], reads=[ht, winb], start=(kc == 0), stop=(kc == 7))
            sz = A("sz", [128, 512])
            P.act(sz[:, :], psz[:, :], AF.Silu, reads=[psz], writes=[sz])
            mixb = A("mixb", [128, 512], BF16)
            P.tt("dve", mixb[:, :], ot[:, :, :].rearrange("p h v -> p (h v)"), sz[:, :], ALU.mult, reads=[ot, sz], writes=[mixb])
            R(sz); R(ot); R(ht); R(lab)
            yield
            pst = PS()
            pstb = pst.t[:, :].bitcast(BF16)
            for j in range(4):
                P.tr(pst, pstb[:, j * 128:(j + 1) * 128], mixb[:, j * 128:(j + 1) * 128], CB("ident"), reads=[mixb, cbf])
            mixT = A("mixT", [128, 8, 128], BF16)
            P.cp("act", mixT[:, 0:4, :], pstb[:, 0:512].rearrange("p (k t) -> p k t", k=4), reads=[pst], writes=[mixT])
            R(mixb)
            P.dma("sp", mixT[:, 4:8, :], YTD[:, :, t * 128:(t + 1) * 128], reads=[ytk], writes=[mixT])
            xt = A("xt", [128, D])
            P.dma("sp", xt[:, :], xr[t * 128:(t + 1) * 128, :], writes=[xt])
            x1 = A("x1", [128, D])
            yield
            for half in range(2):
                pso = PS()
                for kc in range(8):
                    P.mm(pso, pso[:, :], mixT[:, kc, :], woutb[:, kc, half * 512:(half + 1) * 512], reads=[mixT, woutb], start=(kc == 0), stop=(kc == 7))
                P.tt("dve", x1[:, half * 512:(half + 1) * 512], pso[:, :], gates[:, cond, 0, half * 512:(half + 1) * 512], ALU.mult,
                     reads=[pso, gates], writes=[x1])
            P.tt("pool", x1[:, :], x1[:, :], xt[:, :], ALU.add, reads=[x1, xt], writes=[x1])
            P.dma("pool", X1[gt_ * 128:(gt_ + 1) * 128, :], x1[:, :], reads=[x1], writes=[Key(f"X1{gt_}")])
            R(mixT); R(xt); R(x1)

        order = {0: list(range(NTL)), 1: list(range(NTL - 1, -1, -1))}
        pos = {0: 0, 1: 0}
        done = {0: 0, 1: 0}
        cur = {0: None, 1: None}
        ycnt = {0: 0, 1: 0}

        def try_start(d):
            i = pos[d]
            if cur[d] is not None or i >= NTL:
                return
            t = order[d][i]
            second = (t >= NTL // 2) if d == 0 else (t < NTL // 2)
            if second and done[1 - d] < NTL - i:
                return
            cur[d] = tile_gen(streams[d], t, d, second)
            pos[d] += 1

        def step(d):
            try_start(d)
            if cur[d] is None:
                return False
            try:
                next(cur[d])
                ycnt[d] += 1
            except StopIteration:
                cur[d] = None
                done[d] += 1
            return True

        while done[0] < NTL or done[1] < NTL:
            p0 = step(0)
            lead = ycnt[0] - ycnt[1]
            if (not p0) or lead >= SKEW or done[0] >= NTL:
                p1 = step(1)
            else:
                p1 = False
            assert p0 or p1 or lead < SKEW, "scheduler stuck"
            if not p0 and not p1:
                assert step(1), "scheduler deadlock"
        if grp == 0:
            for d in (0, 1):
                P.dma("sp", (nsf if d == 0 else nsb)[si].rearrange("h k v -> k h v"), Sst[d][:, :, :], reads=[Sst[d]], is_output=True)
        tokbase += N

    if dbg:
        dbg_out["x1"] = dt("dbg_x1", [256, D], F32, kind="ExternalOutput").ap()
        P.arena_reset(SCR0)
        dtile = P.aring("xtd", [128, D], F32, bufs=1)
        for b in range(2):
            P.dma("sp", dtile[:, :], X1[b * 128:(b + 1) * 128, :], reads=[Key(f"X1{b}")], writes=[dtile])
            P.dma("sp", dbg_out["x1"][b * 128:(b + 1) * 128, :], dtile[:, :], reads=[dtile], is_output=True)
    if stop_after in ("seq0", "seq4", "ab"):
        P.emit()
        return nc

    P.arena_reset(0)
    wgub = P.abuf("wgub", [128, 8, 2 * DFF], BF16)
    wdnb = P.abuf("wdnb", [128, 22, D], BF16)
    gfin_bc = P.abuf("gfin", [128, D])
    P.dma("sp", gfin_bc[:, :], g_final.partition_broadcast(128), writes=[gfin_bc])
    stg = [P.abuf(f"stg{i}", [128, 1408]) for i in range(2)]
    n = 0
    for kc in range(8):
        for q in range(4):
            st = stg[n % 2]
            P.dma(dq[n % 2], st[:, :], w_gu[kc * 128:(kc + 1) * 128, q * 1408:(q + 1) * 1408], writes=[st])
            P.cp(("dve", "pool", "act")[n % 3], wgub[:, kc, q * 1408:(q + 1) * 1408], st[:, :], reads=[st], writes=[wgub])
            n += 1
    for fc in range(22):
        st = stg[n % 2]
        P.dma(dq[n % 2], st[:, 0:D], w_down[fc * 128:(fc + 1) * 128, :], writes=[st])
        P.cp(("dve", "pool", "act")[n % 3], wdnb[:, fc, :], st[:, 0:D], reads=[st], writes=[wdnb])
        n += 1
    PC0 = P.aoff
    P.arena_reset(PC0 - 2 * 2816)
    for T in range(20):
        cond = 0 if T < 4 else 1
        x1 = P.aring("x1c", [128, 2, D], F32, bufs=1)
        h2 = P.aring("h2", [128, 8, 256], BF16, bufs=1)
        for blk in range(2):
            gt_ = 2 * T + blk
            P.dma("sp", x1[:, blk, :], X1[gt_ * 128:(gt_ + 1) * 128, :], reads=[Key(f"X1{gt_}")], writes=[x1])
            junk = P.aring("junkc", [128, D], F32, bufs=1)
            ss = P.ring("ss", [128, 1], F32, bufs=2)
            P.act(junk[:, :], x1[:, blk, :], AF.Square, reads=[x1], writes=[junk, ss], accum_out=ss[:, :])
            P.act(ss[:, :], ss[:, :], AF.Ln, reads=[ss, eps_c], writes=[ss], scale=1.0 / D, bias=eps_c[:, :])
            P.act(ss[:, :], ss[:, :], AF.Exp, reads=[ss], writes=[ss], scale=-0.5)
            xnb = P.aring("xnbc", [128, D], BF16, bufs=2)
            P.act(xnb[:, :], x1[:, blk, :], AF.Copy, reads=[x1, ss], writes=[xnb], scale=ss[:, :])
            ps = P.ps()
            psb = ps.t[:, :].bitcast(BF16)
            for kc in range(8):
                P.tr(ps, psb[:, kc * 128:(kc + 1) * 128], xnb[:, kc * 128:(kc + 1) * 128], CB("ident"), reads=[xnb, cbf])
            tmp = P.aring("httmpc", [128, 8, 128], F32, bufs=1)
            P.tt("dve", tmp[:, :, :], psb.rearrange("p (k t) -> p k t", k=8),
                 s2T[:, :, cond].unsqueeze(2).to_broadcast([128, 8, 128]), ALU.mult, reads=[ps, s2T], writes=[tmp])
            P.tt("pool", h2[:, :, blk * 128:(blk + 1) * 128], tmp[:, :, :], sh2T[:, :, cond].unsqueeze(2).to_broadcast([128, 8, 128]), ALU.add,
                 reads=[tmp, sh2T], writes=[h2])
        actT = P.aring("actT", [128, 22, 256], BF16, bufs=1)
        for fc in range(22):
            ps = P.ps()
            for kc in range(8):
                P.mm(ps, ps[:, 0:256], wgub[:, kc, fc * 128:(fc + 1) * 128], h2[:, kc, :], reads=[wgub, h2], start=(kc == 0), stop=(kc == 7))
            for kc in range(8):
                P.mm(ps, ps[:, 256:512], wgub[:, kc, DFF + fc * 128:DFF + (fc + 1) * 128], h2[:, kc, :], reads=[wgub, h2], start=(kc == 0), stop=(kc == 7))
            sg = P.aring("sg", [128, 256], F32, bufs=2)
            P.act(sg[:, :], ps[:, 0:256], AF.Silu, reads=[ps], writes=[sg])
            P.tt("dve", actT[:, fc, :], sg[:, :], ps[:, 256:512], ALU.mult, reads=[sg, ps], writes=[actT])
        for blk in range(2):
            gt_ = 2 * T + blk
            yt = P.aring("yt", [128, D], F32, bufs=2)
            for half in range(2):
                ps = P.ps()
                for fc in range(22):
                    P.mm(ps, ps[:, :], actT[:, fc, blk * 128:(blk + 1) * 128], wdnb[:, fc, half * 512:(half + 1) * 512], reads=[actT, wdnb], start=(fc == 0), stop=(fc == 21))
                P.tt("dve", yt[:, half * 512:(half + 1) * 512], ps[:, :], gates[:, cond, 1, half * 512:(half + 1) * 512], ALU.mult,
                     reads=[ps, gates], writes=[yt])
            P.tt("pool", yt[:, :], yt[:, :], x1[:, blk, :], ALU.add, reads=[yt, x1], writes=[yt])
            junk = P.aring("junkc", [128, D], F32, bufs=1)
            ss2 = P.ring("ss", [128, 1], F32, bufs=2)
            P.act(junk[:, :], yt[:, :], AF.Square, reads=[yt], writes=[junk, ss2], accum_out=ss2[:, :])
            P.act(ss2[:, :], ss2[:, :], AF.Ln, reads=[ss2, eps_c], writes=[ss2], scale=1.0 / D, bias=eps_c[:, :])
            P.act(ss2[:, :], ss2[:, :], AF.Exp, reads=[ss2], writes=[ss2], scale=-0.5)
            P.stt("dve", yt[:, :], yt[:, :], ss2[:, :], gfin_bc[:, :], ALU.mult, ALU.mult, reads=[yt, ss2, gfin_bc], writes=[yt])
            dst = yout[0][gt_ * 128:(gt_ + 1) * 128, :] if gt_ < 8 else yout[1][(gt_ - 8) * 128:(gt_ - 7) * 128, :]
            P.dma("pool", dst, yt[:, :], reads=[yt], is_output=True)
    P.emit()
    return nc


def _in_maps(inputs):
    g = lambda k: np.ascontiguousarray(np.asarray(inputs[k], dtype=np.float32))
    maps = []
    for c in range(8):
        m = {
            "xp": g("x_prompt")[4 * c:4 * c + 4].reshape(1024, D),
            "xs": g("x_sample")[c],
            "cvec": np.stack([g("c_ctx"), g("c")[c]], 0),
            "s0f": g("state_dn_fwd")[c, 0], "s0b": g("state_dn_bwd")[c, 0],
            "w_ada": g("w_ada")[0], "b_ada": g("b_ada")[0], "g_mix": g("g_mix")[0], "w_in": g("w_in")[0],
            "w_conv": g("w_conv")[0], "a_log": g("a_log")[0].reshape(8), "dt_bias": g("dt_bias")[0].reshape(8),
            "g_o": g("g_o")[0], "w_fno": g("w_fno")[0], "w_out": g("w_out")[0], "g_ffn": g("g_ffn")[0],
            "w_gu": g("w_gu")[0], "w_down": g("w_down")[0], "g_final": g("g_final"), "cst": _CST,
        }
        maps.append({k: np.ascontiguousarray(v) for k, v in m.items()})
    return maps


def kernel(**inputs):
    nc = build_nc()
    res = run_bass_kernel_spmd(nc, _in_maps(inputs), core_ids=list(range(8)))
    r = res.results
    yp = np.concatenate([r[c]["yp"].reshape(4, 256, D) for c in range(8)], 0)
    ys = np.stack([r[c]["ys"] for c in range(8)], 0)
    f = np.concatenate([r[c]["nsf"] for c in range(8)], 0)[:, None]
    b = np.concatenate([r[c]["nsb"] for c in range(8)], 0)[:, None]
    return (yp.astype(np.float32), ys.astype(np.float32), f.astype(np.float32), b.astype(np.float32))
```
